# Optimizing a Trainium2 kernel written in Bass

```python
import jax, jax.numpy as jnp
from jax import lax
import numpy as np

D_MODEL = 1024
BATCH = 2
SEQ = 16384
DEPTH = 2
DEC_BATCH = 8
DEC_SEQ = 4096
PAST_LEN = 128

FNET_GROUPS = 4
FNET_GROUP_DIM = 64
FNET_WIDTH = FNET_GROUPS * FNET_GROUP_DIM
GLA_HEADS = 4
GLA_DK = 96
GLA_DV = 192
GLA_QK_WIDTH = GLA_HEADS * GLA_DK
GLA_V_WIDTH = GLA_HEADS * GLA_DV
GLA_GATE_RANK = 16
GLA_TAU = 16.0
GLA_CHUNK = 64
CONV_WIDTH = 512
MLSTM_HEADS = 4
MLSTM_DH = 128
MLSTM_WIDTH = MLSTM_HEADS * MLSTM_DH
MLSTM_CHUNK = 64
D_FF = 2816
PLE_DIM = 256
N_EVEN = (DEPTH + 1) // 2
N_ODD = DEPTH // 2
EVEN_IN = FNET_WIDTH + 2 * GLA_QK_WIDTH + 2 * GLA_V_WIDTH + 2 * GLA_GATE_RANK
EVEN_MIX = FNET_WIDTH + GLA_V_WIDTH
ODD_IN = 3 * CONV_WIDTH + 4 * MLSTM_WIDTH + 4 * MLSTM_HEADS
ODD_MIX = CONV_WIDTH + MLSTM_WIDTH
EPS = 1e-6

kernel_name = 'hybrid_bidir_fnet_gla_shortconv_mlstm'


def rmsnorm(x, g):
    xf = x.astype(jnp.float32)
    y = xf * lax.rsqrt(jnp.mean(xf * xf, axis=-1, keepdims=True) + EPS)
    return (y * g.astype(jnp.float32)).astype(x.dtype)


def dwconv3(x, w, b):
    y = lax.conv_general_dilated(x, w[:, None, :].astype(x.dtype), window_strides=(1,),
                                 padding=((1, 1),), dimension_numbers=('NWC', 'WIO', 'NWC'),
                                 feature_group_count=x.shape[-1])
    return y + b.astype(x.dtype)


def to_heads(t, h):
    b, s, _ = t.shape
    return t.reshape(b, s, h, -1).transpose(0, 2, 1, 3)


def head_rmsnorm(o, g, dtype):
    o = o * lax.rsqrt(jnp.mean(o * o, axis=-1, keepdims=True) + EPS)
    b, h, s, d = o.shape
    o = o.transpose(0, 2, 1, 3).reshape(b, s, h * d)
    return (o * g.astype(jnp.float32)).astype(dtype)


def flip_seq(t):
    return jnp.flip(t, axis=2)


def gla_causal(q, k, v, log_a):
    b, h, s, dk = q.shape
    dv = v.shape[-1]
    n = s // GLA_CHUNK
    q = q.reshape(b, h, n, GLA_CHUNK, dk)
    k = k.reshape(b, h, n, GLA_CHUNK, dk)
    log_a = log_a.reshape(b, h, n, GLA_CHUNK, dk)
    v = v.reshape(b, h, n, GLA_CHUNK, dv)
    cum = jnp.cumsum(log_a, axis=3)
    tot = cum[:, :, :, -1:, :]
    q_in = q * jnp.exp(cum)
    k_in = k * jnp.exp(-cum)
    k_out = k * jnp.exp(tot - cum)
    mask = jnp.tril(jnp.ones((GLA_CHUNK, GLA_CHUNK), dtype=bool))
    att = jnp.where(mask, jnp.einsum('bhnld,bhnmd->bhnlm', q_in, k_in), 0.0)
    o_intra = jnp.einsum('bhnlm,bhnme->bhnle', att, v)
    contrib = jnp.einsum('bhnld,bhnle->bhnde', k_out, v)
    decay = jnp.exp(tot[:, :, :, 0, :])

    def step(state, xs):
        dec, con = xs
        return dec[..., None] * state + con, state

    _, prev = lax.scan(step, jnp.zeros((b, h, dk, dv), jnp.float32),
                       (jnp.moveaxis(decay, 2, 0), jnp.moveaxis(contrib, 2, 0)))
    prev = jnp.moveaxis(prev, 0, 2)
    o_inter = jnp.einsum('bhnld,bhnde->bhnle', q_in, prev)
    return (o_intra + o_inter).reshape(b, h, s, dv)


def mlstm_causal(q, k, v, i_pre, f_pre):
    b, h, s, dk = q.shape
    dv = v.shape[-1]
    n = s // MLSTM_CHUNK
    L = MLSTM_CHUNK
    q = q.reshape(b, h, n, L, dk)
    k = k.reshape(b, h, n, L, dk)
    v = v.reshape(b, h, n, L, dv)
    ig = i_pre.reshape(b, h, n, L)
    cum = jnp.cumsum(jax.nn.log_sigmoid(f_pre.reshape(b, h, n, L)), axis=-1)
    tot = cum[..., -1]
    w_end = tot[..., None] - cum + ig
    m_loc = jnp.max(w_end, axis=-1)
    e = jnp.exp(w_end - m_loc[..., None])
    c_loc = jnp.einsum('bhnl,bhnld,bhnle->bhnde', e, k, v)
    n_loc = jnp.einsum('bhnl,bhnld->bhnd', e, k)

    def step(carry, xs):
        c, nv, m = carry
        g, ml, cl, nl = xs
        m_new = jnp.maximum(g + m, ml)
        a = jnp.exp(g + m - m_new)
        bb = jnp.exp(ml - m_new)
        c_new = a[..., None, None] * c + bb[..., None, None] * cl
        n_new = a[..., None] * nv + bb[..., None] * nl
        return (c_new, n_new, m_new), (c, nv, m)

    init = (jnp.zeros((b, h, dk, dv), jnp.float32), jnp.zeros((b, h, dk), jnp.float32),
            jnp.zeros((b, h), jnp.float32))
    xs = (jnp.moveaxis(tot, 2, 0), jnp.moveaxis(m_loc, 2, 0),
          jnp.moveaxis(c_loc, 2, 0), jnp.moveaxis(n_loc, 2, 0))
    _, (c_prev, n_prev, m_prev) = lax.scan(step, init, xs)
    c_prev = jnp.moveaxis(c_prev, 0, 2)
    n_prev = jnp.moveaxis(n_prev, 0, 2)
    m_prev = jnp.moveaxis(m_prev, 0, 2)
    mask = jnp.tril(jnp.ones((L, L), dtype=bool))
    d = jnp.where(mask, cum[..., :, None] - cum[..., None, :] + ig[..., None, :], -jnp.inf)
    lw = cum + m_prev[..., None]
    m_t = jnp.maximum(lw, jnp.max(d, axis=-1))
    p = jnp.exp(d - m_t[..., None])
    a_t = jnp.exp(lw - m_t)
    s_qk = jnp.einsum('bhnld,bhnmd->bhnlm', q, k) * p
    num = (jnp.einsum('bhnlm,bhnme->bhnle', s_qk, v)
           + a_t[..., None] * jnp.einsum('bhnld,bhnde->bhnle', q, c_prev))
    den = jnp.sum(s_qk, axis=-1) + a_t * jnp.einsum('bhnld,bhnd->bhnl', q, n_prev)
    hout = num / jnp.maximum(jnp.abs(den), jnp.exp(-m_t))[..., None]
    return hout.reshape(b, h, s, dv)


def even_mixer(xn, w_in, gla_w2_f, gla_b_f, gla_w2_b, gla_b_b, gla_norm, w_out):
    proj = xn @ w_in
    cuts = np.cumsum([FNET_WIDTH, GLA_QK_WIDTH, GLA_QK_WIDTH, GLA_V_WIDTH, GLA_V_WIDTH]).tolist()
    u, q, k, v, r, g = jnp.split(proj, cuts, axis=-1)
    g_f = g[..., :GLA_GATE_RANK]
    g_b = g[..., GLA_GATE_RANK:]
    b, s, _ = u.shape
    uf = u.astype(jnp.float32).reshape(b, s, FNET_GROUPS, FNET_GROUP_DIM)
    y_f = jnp.real(jnp.fft.fft2(uf, axes=(1, 3), norm='ortho')).reshape(b, s, FNET_WIDTH).astype(xn.dtype)
    qh = to_heads(q, GLA_HEADS).astype(jnp.float32) * (GLA_DK ** -0.5)
    kh = to_heads(k, GLA_HEADS).astype(jnp.float32)
    vh = to_heads(v, GLA_HEADS).astype(jnp.float32)
    la_f = to_heads(jax.nn.log_sigmoid((g_f @ gla_w2_f + gla_b_f).astype(jnp.float32)) / GLA_TAU, GLA_HEADS)
    la_b = to_heads(jax.nn.log_sigmoid((g_b @ gla_w2_b + gla_b_b).astype(jnp.float32)) / GLA_TAU, GLA_HEADS)
    o = (gla_causal(qh, kh, vh, la_f)
         + flip_seq(gla_causal(flip_seq(qh), flip_seq(kh), flip_seq(vh), flip_seq(la_b))))
    y_g = head_rmsnorm(o, gla_norm, xn.dtype) * jax.nn.silu(r)
    return jnp.concatenate([y_f, y_g], axis=-1) @ w_out


def odd_mixer(xn, w_in, conv_w, conv_b, gate_bias, mlstm_norm, w_out):
    proj = xn @ w_in
    cuts = np.cumsum([CONV_WIDTH] * 3 + [MLSTM_WIDTH] * 4).tolist()
    sb, sc, sh, q, k, v, og, gates = jnp.split(proj, cuts, axis=-1)
    y_c = sb * dwconv3(sc * sh, conv_w, conv_b)
    qh = to_heads(q, MLSTM_HEADS).astype(jnp.float32) * (MLSTM_DH ** -0.5)
    kh = to_heads(k, MLSTM_HEADS).astype(jnp.float32)
    vh = to_heads(v, MLSTM_HEADS).astype(jnp.float32)
    gates = (gates.astype(jnp.float32) + gate_bias.astype(jnp.float32)).transpose(0, 2, 1)
    i_f, f_f, i_b, f_b = jnp.split(gates, 4, axis=1)
    hm = (mlstm_causal(qh, kh, vh, i_f, f_f)
          + flip_seq(mlstm_causal(flip_seq(qh), flip_seq(kh), flip_seq(vh), flip_seq(i_b), flip_seq(f_b))))
    y_m = head_rmsnorm(hm, mlstm_norm, xn.dtype) * jax.nn.sigmoid(og)
    return jnp.concatenate([y_c, y_m], axis=-1) @ w_out


def conv_ffn(xn, w_up, conv_w, conv_b, w_down):
    gate, val = jnp.split(xn @ w_up, 2, axis=-1)
    return (jax.nn.silu(dwconv3(gate, conv_w, conv_b)) * val) @ w_down


def trunk(x, p, e_norm, e_w_in, e_gla_w2_f, e_gla_b_f, e_gla_w2_b, e_gla_b_b, e_gla_norm, e_w_out,
          o_norm, o_w_in, o_conv_w, o_conv_b, o_gate_bias, o_mlstm_norm, o_w_out,
          ffn_norm, ffn_w_up, ffn_conv_w, ffn_conv_b, ffn_w_down,
          ple_w, ple_gate_norm, ple_gate_w, final_norm):
    h = x
    for i in range(DEPTH):
        j = i // 2
        if i % 2 == 0:
            h = h + even_mixer(rmsnorm(h, e_norm[j]), e_w_in[j], e_gla_w2_f[j], e_gla_b_f[j],
                               e_gla_w2_b[j], e_gla_b_b[j], e_gla_norm[j], e_w_out[j])
        else:
            h = h + odd_mixer(rmsnorm(h, o_norm[j]), o_w_in[j], o_conv_w[j], o_conv_b[j],
                              o_gate_bias[j], o_mlstm_norm[j], o_w_out[j])
        h = h + conv_ffn(rmsnorm(h, ffn_norm[i]), ffn_w_up[i], ffn_conv_w[i], ffn_conv_b[i], ffn_w_down[i])
        gate = jax.nn.sigmoid((rmsnorm(h, ple_gate_norm[i]) @ ple_gate_w[i]).astype(jnp.float32)).astype(h.dtype)
        h = h + (p[i] @ ple_w[i]) * gate
    return rmsnorm(h, final_norm)


def setup_inputs(seed: int = 0) -> dict:
    key = jax.random.key(seed)
    ks = iter(jax.random.split(key, 40))

    def nrm(shape, scale):
        return jax.random.normal(next(ks), shape, jnp.float32) * scale

    def gain(shape):
        return 1.0 + nrm(shape, 0.05)

    fbias = jnp.linspace(3.0, 6.0, MLSTM_HEADS, dtype=jnp.float32)
    return {
        'x_prompt': nrm((BATCH, SEQ, D_MODEL), 1.0),
        'x_sample': nrm((DEC_BATCH, DEC_SEQ, D_MODEL), 1.0),
        'p_prompt': nrm((DEPTH, BATCH, SEQ, PLE_DIM), 1.0),
        'p_sample': nrm((DEPTH, DEC_BATCH, DEC_SEQ, PLE_DIM), 1.0),
        'e_norm': gain((N_EVEN, D_MODEL)),
        'e_w_in': nrm((N_EVEN, D_MODEL, EVEN_IN), D_MODEL ** -0.5),
        'e_gla_w2_f': nrm((N_EVEN, GLA_GATE_RANK, GLA_QK_WIDTH), GLA_GATE_RANK ** -0.5),
        'e_gla_b_f': nrm((N_EVEN, GLA_QK_WIDTH), 0.1),
        'e_gla_w2_b': nrm((N_EVEN, GLA_GATE_RANK, GLA_QK_WIDTH), GLA_GATE_RANK ** -0.5),
        'e_gla_b_b': nrm((N_EVEN, GLA_QK_WIDTH), 0.1),
        'e_gla_norm': gain((N_EVEN, GLA_V_WIDTH)),
        'e_w_out': nrm((N_EVEN, EVEN_MIX, D_MODEL), EVEN_MIX ** -0.5),
        'o_norm': gain((N_ODD, D_MODEL)),
        'o_w_in': nrm((N_ODD, D_MODEL, ODD_IN), D_MODEL ** -0.5),
        'o_conv_w': nrm((N_ODD, 3, CONV_WIDTH), 3 ** -0.5),
        'o_conv_b': nrm((N_ODD, CONV_WIDTH), 0.02),
        'o_gate_bias': jnp.concatenate([nrm((N_ODD, MLSTM_HEADS), 0.1),
                                        fbias + nrm((N_ODD, MLSTM_HEADS), 0.1),
                                        nrm((N_ODD, MLSTM_HEADS), 0.1),
                                        fbias + nrm((N_ODD, MLSTM_HEADS), 0.1)], axis=-1),
        'o_mlstm_norm': gain((N_ODD, MLSTM_WIDTH)),
        'o_w_out': nrm((N_ODD, ODD_MIX, D_MODEL), ODD_MIX ** -0.5),
        'ffn_norm': gain((DEPTH, D_MODEL)),
        'ffn_w_up': nrm((DEPTH, D_MODEL, 2 * D_FF), D_MODEL ** -0.5),
        'ffn_conv_w': nrm((DEPTH, 3, D_FF), 3 ** -0.5),
        'ffn_conv_b': nrm((DEPTH, D_FF), 0.02),
        'ffn_w_down': nrm((DEPTH, D_FF, D_MODEL), D_FF ** -0.5),
        'ple_w': nrm((DEPTH, PLE_DIM, D_MODEL), PLE_DIM ** -0.5),
        'ple_gate_norm': gain((DEPTH, D_MODEL)),
        'ple_gate_w': nrm((DEPTH, D_MODEL, D_MODEL), D_MODEL ** -0.5),
        'final_norm': gain((D_MODEL,)),
    }


def reference(x_prompt, x_sample, p_prompt, p_sample, e_norm, e_w_in, e_gla_w2_f, e_gla_b_f,
              e_gla_w2_b, e_gla_b_b, e_gla_norm, e_w_out, o_norm, o_w_in, o_conv_w, o_conv_b,
              o_gate_bias, o_mlstm_norm, o_w_out, ffn_norm, ffn_w_up, ffn_conv_w, ffn_conv_b,
              ffn_w_down, ple_w, ple_gate_norm, ple_gate_w, final_norm):
    y_prompt = trunk(x_prompt, p_prompt, e_norm, e_w_in, e_gla_w2_f, e_gla_b_f, e_gla_w2_b, e_gla_b_b,
                     e_gla_norm, e_w_out, o_norm, o_w_in, o_conv_w, o_conv_b, o_gate_bias, o_mlstm_norm,
                     o_w_out, ffn_norm, ffn_w_up, ffn_conv_w, ffn_conv_b, ffn_w_down,
                     ple_w, ple_gate_norm, ple_gate_w, final_norm)
    y_sample = trunk(x_sample, p_sample, e_norm, e_w_in, e_gla_w2_f, e_gla_b_f, e_gla_w2_b, e_gla_b_b,
                     e_gla_norm, e_w_out, o_norm, o_w_in, o_conv_w, o_conv_b, o_gate_bias, o_mlstm_norm,
                     o_w_out, ffn_norm, ffn_w_up, ffn_conv_w, ffn_conv_b, ffn_w_down,
                     ple_w, ple_gate_norm, ple_gate_w, final_norm)
    return (y_prompt, y_sample)
```

```python
import numpy as np
import ml_dtypes
from contextlib import ExitStack
import concourse.bass as bass
import concourse.mybir as mybir
from concourse.bass_utils import run_bass_kernel_spmd

F32 = mybir.dt.float32
BF16 = mybir.dt.bfloat16
ALU = mybir.AluOpType
AF = mybir.ActivationFunctionType
AX = mybir.AxisListType

D = 1024
EPS = 1e-6
ENGS = ('pe', 'act', 'dve', 'pool', 'sp')


class Op:
    __slots__ = ('eng', 'fn', 'waits', 'needed', 'sig', 'dkey', 'dval', 'idx')


class Res:
    __slots__ = ('w', 'r')

    def __init__(self):
        self.w = []
        self.r = []


class Builder:
    def __init__(self, nc, stack):
        self.nc = nc
        self.stack = stack
        self.ops = {e: [] for e in ENGS}
        self.nops = {e: 0 for e in ENGS}
        self.sigcnt = {e: 0 for e in ENGS}
        self.known = {e: {} for e in ENGS}
        self.last = {e: None for e in ENGS}
        self.dtot = {}
        self.sems = {}
        self.res = {}
        for e in ENGS:
            self.sems['e_' + e] = stack.enter_context(nc.semaphore('e_' + e))

    def R(self, key):
        r = self.res.get(key)
        if r is None:
            r = self.res[key] = Res()
        return r

    def _dsem(self, key):
        n = 'd_' + key
        if n not in self.sems:
            self.sems[n] = self.stack.enter_context(self.nc.semaphore(n))
        return self.sems[n]

    def op(self, eng, fn, reads=(), writes=(), dma=None, acc=()):
        o = Op()
        o.eng = eng
        o.fn = fn
        o.needed = False
        o.sig = None
        o.dkey = dma
        o.dval = None
        self.nops[eng] += 1
        o.idx = self.nops[eng]
        evs = []
        for k in reads:
            evs += [(ev, 0) for ev in self.R(k).w]
        for k in writes:
            r = self.R(k)
            evs += [(ev, 1) for ev in r.w]
            evs += [(ev, 2) for ev in r.r]
        for k in acc:
            r = self.R(k)
            evs += [(ev, 3) for ev in r.w]
            evs += [(ev, 2) for ev in r.r]
        waits = {}
        kn = self.known[eng]
        for ev, kind in evs:
            if ev[0] == 'e':
                p = ev[1]
                if p.eng == eng:
                    if eng == 'pe' or kind >= 2:
                        continue
                k = ('e', p.eng)
                v = p.idx
            else:
                p = None
                k = ('d', ev[1])
                v = ev[2]
            if kn.get(k, 0) >= v:
                continue
            if k not in waits or waits[k][0] < v:
                waits[k] = (v, p)
        for k, (v, p) in waits.items():
            kn[k] = v
            if p is not None:
                p.needed = True
        o.waits = waits
        if dma is not None:
            self._dsem(dma)
            self.dtot[dma] = self.dtot.get(dma, 0) + 16
            o.dval = self.dtot[dma]
            ev = ('d', dma, o.dval)
        else:
            ev = ('e', o)
        for k in reads:
            self.R(k).r.append(ev)
        for k in writes:
            r = self.R(k)
            r.w = [ev]
            r.r = []
        for k in acc:
            self.R(k).w.append(ev)
        self.ops[eng].append(o)
        if dma is None:
            self.last[eng] = o
        return o

    def barrier(self):
        lasts = {e: self.last[e] for e in ENGS if self.last[e] is not None}
        for e in ENGS:
            o = Op()
            o.eng = e
            o.fn = None
            o.needed = False
            o.sig = None
            o.dkey = None
            o.dval = None
            self.nops[e] += 1
            o.idx = self.nops[e]
            waits = {}
            for e2, p in lasts.items():
                if e2 == e:
                    continue
                if self.known[e].get(('e', e2), 0) >= p.idx:
                    continue
                waits[('e', e2)] = (p.idx, p)
                p.needed = True
                self.known[e][('e', e2)] = p.idx
            for key, tot in self.dtot.items():
                if self.known[e].get(('d', key), 0) >= tot:
                    continue
                waits[('d', key)] = (tot, None)
                self.known[e][('d', key)] = tot
            o.waits = waits
            self.ops[e].append(o)
        self.res = {}

    def emit(self):
        for e in ENGS:
            for o in self.ops[e]:
                if o.needed and o.dkey is None and o.fn is not None:
                    self.sigcnt[e] += 1
                    o.sig = self.sigcnt[e]
        sems = self.sems
        with self.nc.Block() as blk:
            decos = {'pe': blk.tensor, 'act': blk.scalar, 'dve': blk.vector, 'pool': blk.gpsimd, 'sp': blk.sync}
            for e in ENGS:
                ops = self.ops[e]

                def body(eng, ops=ops, e=e):
                    for o in ops:
                        for k, (v, p) in o.waits.items():
                            if k[0] == 'e':
                                eng.wait_ge(sems['e_' + k[1]], p.sig)
                            else:
                                eng.wait_ge(sems['d_' + k[1]], v)
                        if o.fn is None:
                            continue
                        ins = o.fn(eng)
                        if o.dkey is not None:
                            ins.then_inc(sems['d_' + o.dkey], 16)
                        elif o.sig is not None:
                            ins.then_inc(sems['e_' + e], 1)

                decos[e](body)
        self.ops = {e: [] for e in ENGS}


class Gen:
    def __init__(self, NT, debug=False):
        self.NT = NT
        self.T = 512 * NT
        self.NA = 4 * NT
        self.debug = debug
        self.nc = bass.Bass("TRN2", target_bir_lowering=False)
        self.din = {}
        self.dout = {}

    def inp(self, name, shape, dt=F32):
        t = self.nc.dram_tensor(name, list(shape), dt, kind="ExternalInput")
        self.din[name] = t
        return t.ap()

    def scratch(self, name, shape, dt):
        kind = "ExternalOutput" if self.debug else "Internal"
        t = self.nc.dram_tensor(name, list(shape), dt, kind=kind)
        return t.ap()

    def build(self):
        nc = self.nc
        NT, T, NA = self.NT, self.T, self.NA
        g = self
        I = {}
        I['x'] = self.inp('x', [T, D])
        I['p'] = self.inp('p', [2, T, 256])
        I['mfwd'] = self.inp('mfwd', [128, NT])
        I['mbwd'] = self.inp('mbwd', [128, NT])
        I['blend'] = self.inp('blend', [128, 2])
        I['e_w_in'] = self.inp('e_w_in', [D, 2592])
        I['e_w_out'] = self.inp('e_w_out', [D, D])
        I['o_w_in'] = self.inp('o_w_in', [D, 3600])
        I['o_w_out'] = self.inp('o_w_out', [D, D])
        I['ffn_w_up'] = self.inp('ffn_w_up', [2, D, 5632])
        I['ffn_w_down'] = self.inp('ffn_w_down', [2, 2816, D])
        I['ple_w'] = self.inp('ple_w', [2, 256, D])
        I['ple_gate_w'] = self.inp('ple_gate_w', [2, D, D])
        I['norms'] = self.inp('norms', [128, 7, D])
        I['gla_norm'] = self.inp('gla_norm', [128, 768])
        I['mlstm_norm'] = self.inp('mlstm_norm', [128, 512])
        I['gate_bias'] = self.inp('gate_bias', [128, 16])
        I['w2aug'] = self.inp('w2aug', [2, 17, 384])
        I['o_conv'] = self.inp('o_conv', [128, 4, 4])
        I['ffn_conv'] = self.inp('ffn_conv', [128, 2, 22, 4])
        I['ident'] = self.inp('ident', [128, 128], BF16)
        I['tri'] = self.inp('tri', [128, 8, 128])
        I['amask'] = self.inp('amask', [128, 2, 512], BF16)
        I['negcol'] = self.inp('negcol', [128, 2])
        I['negones'] = self.inp('negones', [128, 128])
        I['cs64'] = self.inp('cs64', [128, 2, 128], BF16)
        I['fR'] = self.inp('fR', [2, NA, 2, 2 * NA], BF16)
        I['fGA'] = self.inp('fGA', [128, NA, 2, 128], BF16)
        I['fGB'] = self.inp('fGB', [128, NA // 4, 2, 128], BF16)
        self.I = I
        yout = nc.dram_tensor('y', [T, D], F32, kind="ExternalOutput").ap()
        S = {}
        S['Z'] = self.scratch('Z', [512, T], BF16)
        S['yfA'] = self.scratch('yfA', [T, 256], BF16)
        S['yfB'] = self.scratch('yfB', [T, 256], BF16)
        S['ofw0'] = self.scratch('ofw0', [T, 768], F32)
        S['ofw1'] = self.scratch('ofw1', [T, 512], F32)
        S['hA'] = self.scratch('hA', [T, D], F32)
        S['hB'] = self.scratch('hB', [T, D], F32)
        S['hC'] = self.scratch('hC', [T, D], F32)
        self.S = S

        with ExitStack() as stack:
            B = Builder(nc, stack)
            self.B = B
            self.banks = [stack.enter_context(nc.psum_tensor('bank%d' % i, [128, 512], F32)) for i in range(8)]
            self.bank_i = 0
            C = {}
            C['ident'] = stack.enter_context(nc.sbuf_tensor('c_ident', [128, 128], BF16))
            C['mfwd'] = stack.enter_context(nc.sbuf_tensor('c_mfwd', [128, NT], F32))
            C['mbwd'] = stack.enter_context(nc.sbuf_tensor('c_mbwd', [128, NT], F32))
            C['blend'] = stack.enter_context(nc.sbuf_tensor('c_blend', [128, 2], F32))
            self.C = C
            for nm in ('ident', 'mfwd', 'mbwd', 'blend'):
                self.dma('sp', C[nm][:], I[nm], 'c_' + nm, writes=['c_' + nm])

            self.mixer_pass(0, 'f', I['x'], None)
            self.fnet_pass()
            self.mixer_pass(0, 'b', I['x'], S['hA'])
            self.ffn_pass(0, S['hA'], S['hB'])
            self.mixer_pass(1, 'f', S['hB'], None, ple_out=S['hC'])
            self.mixer_pass(1, 'b', S['hC'], S['hA'])
            self.ffn_pass(1, S['hA'], S['hB'])
            self.final_pass(S['hB'], yout)
        return nc

    def nb(self):
        i = self.bank_i
        self.bank_i = (i + 1) % 8
        return i

    def dma(self, eng, dst, src, key, reads=(), writes=(), acc=()):
        return self.B.op(eng, lambda e: e.dma_start(out=dst, in_=src), reads=list(reads), writes=list(writes),
                         acc=list(acc), dma=key)

    def alloc(self, st, name, shape, dt):
        self.uid = getattr(self, 'uid', 0) + 1
        return st.enter_context(self.nc.sbuf_tensor('sb%d_%s' % (self.uid, name), list(shape), dt))

    def load_weight(self, st, name, src, K, N):
        kc = K // 128
        w = self.alloc(st, name, [128, kc, N], BF16)
        srcv = src.rearrange("(k p) n -> p k n", p=128)
        step = max(1, 4096 // N)
        for k0 in range(0, kc, step):
            k1 = min(kc, k0 + step)
            self.dma('pool', w[:, k0:k1, :], srcv[:, k0:k1, :], 'w_' + name, acc=[name])
        return w

    def cload(self, st, name, shape, dt, src, eng='sp'):
        t = self.alloc(st, name, shape, dt)
        self.dma(eng, t[:], src, 'c_' + name, writes=[name])
        return t

    def norm_tile(self, xt, xtk, nrows, nblk, gain, gaink, xn, xnk, sq, rs):
        B = self.B
        B.op('pool', lambda e: e.memset(rs[:, 0:4], 0.0), writes=[(xnk, 'ssq')])
        for j in range(nblk):
            B.op('act', lambda e, j=j: e.activation(out=xn[0:nrows, j, :], in_=xt[0:nrows, j, :], func=AF.Square,
                                                    accum_out=rs[0:nrows, j:j + 1]),
                 reads=[xtk], acc=[(xnk, 'ssq'), (xnk, j)])
        B.op('act', lambda e: e.activation(out=rs[0:nrows, 4:4 + nblk], in_=rs[0:nrows, 0:nblk], func=AF.Ln,
                                           scale=1.0 / D, bias=self.epsc[0:nrows, 0:1]),
             reads=[(xnk, 'ssq'), 'epsc'], writes=[(xnk, 'rs0')])
        B.op('act', lambda e: e.activation(out=rs[0:nrows, 4:4 + nblk], in_=rs[0:nrows, 4:4 + nblk], func=AF.Exp,
                                           scale=-0.5),
             reads=[(xnk, 'rs0')], writes=[(xnk, 'rs')])
        for j in range(nblk):
            eng = 'dve'
            B.op(eng, lambda e, j=j: e.scalar_tensor_tensor(out=xn[0:nrows, j, :], in0=xt[0:nrows, j, :],
                                                            scalar=rs[0:nrows, 4 + j:5 + j], in1=gain[0:nrows, :],
                                                            op0=ALU.mult, op1=ALU.mult),
                 reads=[xtk, (xnk, 'rs'), gaink], writes=[(xnk, j)])

    def transpose_tile(self, src, src_keys, nrows, nblk, nk, dstT, dstk, kofs=0, cw=128, col0=0):
        B = self.B
        ident = self.C['ident']
        for k in range(nk):
            bi = self.nb()
            pb = self.banks[bi].bitcast(BF16)

            def f(e, k=k, pb=pb):
                ins = None
                for j in range(nblk):
                    ins = e.transpose(out=pb[0:cw, j * nrows:(j + 1) * nrows],
                                      in_=src[0:nrows, j, col0 + k * cw:col0 + (k + 1) * cw],
                                      identity=ident[0:nrows, 0:nrows])
                return ins
            B.op('pe', f, reads=list(src_keys) + ['c_ident'], writes=[('bank', bi)])
            if k % 2 == 0:
                B.op('act', lambda e, k=k, pb=pb: e.activation(out=dstT[0:cw, kofs + k, 0:nblk * nrows],
                                                               in_=pb[0:cw, 0:nblk * nrows], func=AF.Copy),
                     reads=[('bank', bi)], writes=[(dstk, kofs + k)])
            else:
                B.op('dve', lambda e, k=k, pb=pb: e.tensor_copy(out=dstT[0:cw, kofs + k, 0:nblk * nrows],
                                                                in_=pb[0:cw, 0:nblk * nrows]),
                     reads=[('bank', bi)], writes=[(dstk, kofs + k)])

    def mm_group(self, bi, out_ap, pairs, reads, more_banks=()):
        n = len(pairs)

        def f(e):
            ins = None
            for i, (l, r) in enumerate(pairs):
                ins = e.matmul(out_ap, lhsT=l, rhs=r, start=(i == 0), stop=(i == n - 1))
            return ins
        return self.B.op('pe', f, reads=list(reads), writes=[('bank', bi)] + [('bank', b) for b in more_banks])

    def mm_multi(self, banks_used, groups, reads):
        def f(e):
            ins = None
            for out_ap, pairs in groups:
                n = len(pairs)
                for i, (l, r) in enumerate(pairs):
                    ins = e.matmul(out_ap, lhsT=l, rhs=r, start=(i == 0), stop=(i == n - 1))
            return ins
        return self.B.op('pe', f, reads=list(reads), writes=[('bank', b) for b in banks_used])

    def xk(self, name, n=8):
        return [(name, k) for k in range(n)]

    def ple_alloc(self, st, li):
        I = self.I
        P = {'li': li}
        P['gw'] = self.load_weight(st, 'ple_gw', I['ple_gate_w'][li], D, D)
        P['pw'] = self.load_weight(st, 'ple_pw', I['ple_w'][li], 256, D)
        P['gain'] = self.cload(st, 'ple_gain', [128, D], F32, I['norms'][:, 4 + li, :])
        P['hn'] = self.alloc(st, 'ple_hn', [128, 4, D], BF16)
        P['hnT'] = self.alloc(st, 'ple_hnT', [128, 8, 512], BF16)
        P['pb'] = self.alloc(st, 'ple_pb', [128, 4, 256], BF16)
        P['pT'] = self.alloc(st, 'ple_pT', [128, 2, 512], BF16)
        P['sg'] = self.alloc(st, 'ple_sg', [128, 2, 512], F32)
        return P

    def ple_step(self, P, i, ht, htk, sq, rs):
        B = self.B
        I = self.I
        li = P['li']
        psrc = I['p'][li, 512 * i:512 * (i + 1), :].rearrange("(j p) c -> p j c", p=128)
        self.dma('pool', P['pb'][:], psrc, 'ple_p', writes=['ple_pb'])
        self.norm_tile(ht, htk, 128, 4, P['gain'], 'ple_gain', P['hn'], 'ple_hn', sq, rs)
        self.transpose_tile(P['hn'], self.xk('ple_hn', 4), 128, 4, 8, P['hnT'], 'ple_hnT')
        self.transpose_tile(P['pb'], ['ple_pb'], 128, 4, 2, P['pT'], 'ple_pT')
        banks = self.banks
        for j in range(4):
            for n in range(2):
                s = n
                bg = self.nb()
                self.mm_group(bg, banks[bg][:, :],
                              [(P['hnT'][:, k, j * 128:(j + 1) * 128], P['gw'][:, k, n * 512:(n + 1) * 512]) for k in range(8)],
                              reads=self.xk('ple_hnT') + ['ple_gw'])
                B.op('act', lambda e, bg=bg, s=s: e.activation(out=P['sg'][:, s, :], in_=banks[bg][:, :], func=AF.Sigmoid),
                     reads=[('bank', bg)], writes=[('ple_sg', s)])
                bp = self.nb()
                self.mm_group(bp, banks[bp][:, :],
                              [(P['pT'][:, k, j * 128:(j + 1) * 128], P['pw'][:, k, n * 512:(n + 1) * 512]) for k in range(2)],
                              reads=self.xk('ple_pT', 2) + ['ple_pw'])
                B.op('dve', lambda e, bp=bp, s=s: e.tensor_tensor(out=P['sg'][:, s, :], in0=P['sg'][:, s, :],
                                                                  in1=banks[bp][:, :], op=ALU.mult),
                     reads=[('bank', bp), ('ple_sg', s)], writes=[('ple_sg', s)])
                B.op('pool', lambda e, j=j, n=n, s=s: e.tensor_tensor(out=ht[:, j, n * 512:(n + 1) * 512],
                                                                      in0=ht[:, j, n * 512:(n + 1) * 512],
                                                                      in1=P['sg'][:, s, :], op=ALU.add),
                     reads=[('ple_sg', s), htk], acc=[htk])

    def mixer_pass(self, L, dirn, src, dst, ple_out=None):
        B, I, S, C, NT, banks = self.B, self.I, self.S, self.C, self.NT, self.banks
        fwd = dirn == 'f'
        even = (L == 0)
        H = 4
        if even:
            DK, DVP, DV, QW, VW = 96, 192, 192, 384, 768
            qo, ko, vo, ro, YO, WN = 256, 640, 1024, 1792, 256, 2592
        else:
            DK, DVP, DV, QW, VW = 128, 129, 128, 512, 512
            qo, ko, vo, ro, YO, WN = 1536, 2048, 2560, 3072, 512, 3600
        LW = QW if even else 4
        qscale = float(DK) ** -0.5
        ofw = S['ofw0'] if even else S['ofw1']
        tI, tS = ((0, 1) if fwd else (2, 3))
        if not even:
            tI, tS = tI + 4, tS + 4
        seg = max(1, NT // 4)
        with ExitStack() as st:
            A = lambda n, sh, dt: self.alloc(st, n, sh, dt)
            Win = self.load_weight(st, 'Win', I['e_w_in'] if even else I['o_w_in'], D, WN)
            gain = self.cload(st, 'gain', [128, D], F32, I['norms'][:, L, :])
            self.epsc = A('epsc', [128, 1], F32)
            B.op('pool', lambda e: e.memset(self.epsc[:, :], EPS), writes=['epsc'])
            xt = A('xt', [128, 4, D], F32)
            sq = None
            rs = A('rs', [128, 8], F32)
            xn = A('xn', [128, 4, D], BF16)
            xnT = A('xnT', [128, 8, 512], BF16)
            tri = self.cload(st, 'tri', [128, 8, 128], F32, I['tri'])
            amask = self.cload(st, 'amask', [128, 512], BF16, I['amask'][:, 0 if fwd else 1, :])
            negcol = self.cload(st, 'negcol', [128, 2], F32, I['negcol'])
            negones = self.cload(st, 'negones', [128, 128], F32, I['negones'])
            qtm = A('qtm', [128, QW], F32)
            ktm = A('ktm', [128, QW], F32)
            vb = A('vb', [128, H, DVP], BF16)
            lap = A('lap', [128, LW], F32)
            E = A('E', [128, 3, LW], F32)
            qin = A('qin', [128, QW], BF16)
            kin = A('kin', [128, QW], BF16)
            kout = A('kout', [128, QW], BF16)
            qinT = A('qinT', [128, 1, 512], BF16)
            kinT = A('kinT', [128, 1, 512], BF16)
            attm = A('attm', [128, 512], BF16)
            dec = A('dec', [128, H], F32)
            S32 = A('S32', [128, H, DVP], F32)
            Sbf = A('Sbf', [128, H, DVP], BF16)
            ost = A('ost', [128, 4, VW], F32)
            if even:
                gTa = A('gTa', [17, 512], BF16)
                w2 = A('w2', [17, 384], BF16)
                self.dma('pool', w2[:, :], I['w2aug'][0 if fwd else 1], 'c_w2', writes=['w2'])
                B.op('pool', lambda e: e.memset(gTa[:, :], 1.0), writes=['gTa'])
                if fwd:
                    uT = A('uT', [128, 2, 512], BF16)
                    zst = A('zst', [128, 4, 512], BF16)
                    cs64 = self.cload(st, 'cs64', [128, 2, 128], BF16, I['cs64'])
            else:
                gts = A('gts', [128, 16], F32)
                gbias = self.cload(st, 'gbias', [128, 16], F32, I['gate_bias'])
                smal = A('smal', [128, 16], F32)
                B.op('pool', lambda e: e.memset(vb[:, :, :], 1.0), writes=['vb'])
            if not even:
                rec = A('rec', [128, 8], F32)
                otmp = A('otmp', [128, VW], F32)
            if not fwd:
                Wout = self.load_weight(st, 'Wout', I['e_w_out'] if even else I['o_w_out'], D, D)
                rg = A('rg', [128, VW], F32)
                ybf = A('ybf', [128, 4, D], BF16)
                yT = A('yT', [128, 8, 512], BF16)
                gmix = self.cload(st, 'gmix', [128, VW], F32, I['gla_norm'] if even else I['mlstm_norm'])
                osum = A('osum', [128, VW], F32)
                ytmp = A('ytmp', [128, VW], F32)
                hrs = A('hrs', [128, 8], F32)
                if even:
                    yfa = A('yfa', [128, 4, 256], BF16)
                    yfb = A('yfb', [128, 4, 256], BF16)
                    yft = A('yft', [128, 4, 256], F32)
                else:
                    PB = A('PB', [128, 4, 514], F32)
                    scs = A('scs', [128, 512], F32)
                    cacc = A('cacc', [128, 512], F32)
                    oconv = self.cload(st, 'oconv', [128, 4, 4], F32, I['o_conv'])
                    HALO = A('HALO', [128, 4, NT], F32)
                    carry = A('carry', [128, 4, 1], F32)
            if ple_out is not None:
                P = self.ple_alloc(st, 0)

            B.op('dve', lambda e: e.memset(S32[:, :, :], 0.0), writes=['S32'])
            B.op('pool', lambda e: e.memset(Sbf[:, :, :], 0.0), writes=['Sbf'])

            if (not even) and (not fwd):
                srcv = src.rearrange("(i t) d -> i t d", t=512)
                self.dma('sp', xt[0:NT, 0, :], srcv[:, 511, :], 'ld_x', writes=['xt'])
                self.norm_tile(xt, 'xt', NT, 1, gain, 'gain', xn, 'xn', sq, rs)
                self.transpose_tile(xn, [('xn', 0)], NT, 1, 8, xnT, 'xnT')
                for c in range(4):
                    b1 = self.nb()
                    self.mm_group(b1, banks[b1][:, 0:NT],
                                  [(Win[:, k, 512 + c * 128:512 + (c + 1) * 128], xnT[:, k, 0:NT]) for k in range(8)],
                                  reads=self.xk('xnT') + ['Win'])
                    B.op('act', lambda e, b1=b1: e.activation(out=scs[:, 0:NT], in_=banks[b1][:, 0:NT], func=AF.Copy),
                         reads=[('bank', b1)], writes=['scs'])
                    b2 = self.nb()
                    self.mm_group(b2, banks[b2][:, 0:NT],
                                  [(Win[:, k, 1024 + c * 128:1024 + (c + 1) * 128], xnT[:, k, 0:NT]) for k in range(8)],
                                  reads=self.xk('xnT') + ['Win'])
                    B.op('dve', lambda e, b2=b2, c=c: e.tensor_tensor(out=HALO[:, c, :], in0=scs[:, 0:NT],
                                                                      in1=banks[b2][:, 0:NT], op=ALU.mult),
                         reads=[('bank', b2), 'scs'], acc=['HALO'])

            order = list(range(NT)) if fwd else list(range(NT - 1, -1, -1))
            for i in order:
                tsl = slice(512 * i, 512 * (i + 1))
                self.dma('sp', xt[:, :, :], src[tsl, :].rearrange("(j p) d -> p j d", p=128), 'ld_x', writes=['xt'])
                if ple_out is not None:
                    self.ple_step(P, i, xt, 'xt', sq, rs)
                    self.dma('pool', ple_out[tsl, :].rearrange("(j p) d -> p j d", p=128), xt[:, :, :], 'st_ple',
                             reads=['xt'])
                self.norm_tile(xt, 'xt', 128, 4, gain, 'gain', xn, 'xn', sq, rs)
                self.transpose_tile(xn, self.xk('xn', 4), 128, 4, 8, xnT, 'xnT')
                xr = self.xk('xnT') + ['Win']
                if not fwd:
                    self.dma('sp', ost[:, :, :], ofw[tsl, :].rearrange("(j p) d -> p j d", p=128), 'ld_o', writes=['ost'])
                    if even:
                        self.dma('sp', yfa[:, :, :], S['yfA'][tsl, :].rearrange("(j p) d -> p j d", p=128), 'ld_yfa',
                                 writes=['yfa'])
                        self.dma('sp', yfb[:, :, :], S['yfB'][tsl, :].rearrange("(j p) d -> p j d", p=128), 'ld_yfb',
                                 writes=['yfb'])
                if even:
                    go = 2560 if fwd else 2576
                    bi = self.nb()
                    self.mm_group(bi, banks[bi][0:16, :], [(Win[:, k, go:go + 16], xnT[:, k, :]) for k in range(8)], reads=xr)
                    B.op('act', lambda e, bi=bi: e.activation(out=gTa[0:16, :], in_=banks[bi][0:16, :], func=AF.Copy),
                         reads=[('bank', bi)], acc=['gTa'])
                    if fwd:
                        for m in range(2):
                            bi = self.nb()
                            self.mm_group(bi, banks[bi][:, :],
                                          [(Win[:, k, m * 128:(m + 1) * 128], xnT[:, k, :]) for k in range(8)], reads=xr)
                            B.op('dve', lambda e, bi=bi, m=m: e.tensor_copy(out=uT[:, m, :], in_=banks[bi][:, :]),
                                 reads=[('bank', bi)], writes=[('uT', m)])
                        for m in range(2):
                            for t in range(2):
                                bi = self.nb()
                                self.mm_group(bi, banks[bi][:, :], [(cs64[:, t, :], uT[:, m, :])], reads=[('uT', m), 'cs64'])
                                B.op('act', lambda e, bi=bi, m=m, t=t: e.activation(out=zst[:, t * 2 + m, :],
                                                                                    in_=banks[bi][:, :], func=AF.Copy),
                                     reads=[('bank', bi)], acc=['zst'])
                        self.dma('pool', S['Z'].rearrange("(r p) t -> p r t", p=128)[:, :, tsl], zst[:, :, :], 'st_z',
                                 reads=['zst'])
                if fwd and i > 0 and i % seg == 0:
                    mcol = C['mfwd'][:, i:i + 1]
                elif (not fwd) and i < NT - 1 and (i + 1) % seg == 0:
                    mcol = C['mbwd'][:, i:i + 1]
                else:
                    mcol = None
                if mcol is not None:
                    B.op('dve', lambda e, mcol=mcol: e.tensor_scalar(out=S32[:, :, :], in0=S32[:, :, :], scalar1=mcol,
                                                                     scalar2=None, op0=ALU.mult),
                         reads=['S32', 'c_mfwd', 'c_mbwd'], writes=['S32'])
                    B.op('act', lambda e: e.activation(out=Sbf[:, :, :], in_=S32[:, :, :], func=AF.Copy),
                         reads=['S32'], writes=['Sbf'])
                if (not even) and (not fwd):
                    if i > 0:
                        B.op('pool', lambda e, i=i: e.tensor_scalar(out=PB[:, :, 0:1], in0=HALO[:, :, i - 1:i],
                                                                    scalar1=C['mfwd'][:, i:i + 1], scalar2=None, op0=ALU.mult),
                             reads=['HALO', 'c_mfwd'], writes=[('PB', 'l')])
                    else:
                        B.op('pool', lambda e: e.memset(PB[:, :, 0:1], 0.0), writes=[('PB', 'l')])
                    if i < NT - 1:
                        B.op('pool', lambda e, i=i: e.tensor_scalar(out=PB[:, :, 513:514], in0=carry[:, :, 0:1],
                                                                    scalar1=C['mbwd'][:, i:i + 1], scalar2=None, op0=ALU.mult),
                             reads=['carry', 'c_mbwd'], writes=[('PB', 'r')])
                    else:
                        B.op('pool', lambda e: e.memset(PB[:, :, 513:514], 0.0), writes=[('PB', 'r')])
                    for c in range(4):
                        b1 = self.nb()
                        self.mm_group(b1, banks[b1][:, :],
                                      [(Win[:, k, 512 + c * 128:512 + (c + 1) * 128], xnT[:, k, :]) for k in range(8)], reads=xr)
                        B.op('act', lambda e, b1=b1: e.activation(out=scs[:, :], in_=banks[b1][:, :], func=AF.Copy),
                             reads=[('bank', b1)], writes=['scs'])
                        b2 = self.nb()
                        self.mm_group(b2, banks[b2][:, :],
                                      [(Win[:, k, 1024 + c * 128:1024 + (c + 1) * 128], xnT[:, k, :]) for k in range(8)], reads=xr)
                        B.op('dve', lambda e, b2=b2, c=c: e.tensor_tensor(out=PB[:, c, 1:513], in0=scs[:, :],
                                                                          in1=banks[b2][:, :], op=ALU.mult),
                             reads=[('bank', b2), 'scs'], writes=[('PB', c)])
                        B.op('dve', lambda e, c=c: e.tensor_scalar(out=cacc[:, :], in0=PB[:, c, 1:513],
                                                                   scalar1=oconv[:, c, 1:2], scalar2=oconv[:, c, 3:4],
                                                                   op0=ALU.mult, op1=ALU.add),
                             reads=[('PB', c), 'oconv'], writes=['cacc'])
                        B.op('dve', lambda e, c=c: e.scalar_tensor_tensor(out=cacc[:, :], in0=PB[:, c, 0:512],
                                                                           scalar=oconv[:, c, 0:1], in1=cacc[:, :],
                                                                           op0=ALU.mult, op1=ALU.add),
                             reads=[('PB', c), ('PB', 'l'), 'cacc', 'oconv'], writes=['cacc'])
                        B.op('dve', lambda e, c=c: e.scalar_tensor_tensor(out=cacc[:, :], in0=PB[:, c, 2:514],
                                                                          scalar=oconv[:, c, 2:3], in1=cacc[:, :],
                                                                          op0=ALU.mult, op1=ALU.add),
                             reads=[('PB', c), ('PB', 'r'), 'cacc', 'oconv'], writes=['cacc'])
                        b3 = self.nb()
                        self.mm_group(b3, banks[b3][:, :],
                                      [(Win[:, k, c * 128:(c + 1) * 128], xnT[:, k, :]) for k in range(8)], reads=xr)
                        B.op('dve', lambda e, b3=b3, c=c: e.tensor_tensor(out=yT[:, c, :], in0=cacc[:, :],
                                                                          in1=banks[b3][:, :], op=ALU.mult),
                             reads=[('bank', b3), 'cacc'], writes=[('yT', c)])
                    B.op('pool', lambda e: e.tensor_copy(out=carry[:, :, 0:1], in_=PB[:, :, 1:2]),
                         reads=[('PB', c) for c in range(4)] + [('PB', 'r')], writes=['carry'])

                for j in (range(4) if fwd else range(3, -1, -1)):
                    cs = slice(j * 128, (j + 1) * 128)
                    bq = self.nb()
                    self.mm_group(bq, banks[bq][:, 0:QW], [(xnT[:, k, cs], Win[:, k, qo:qo + QW]) for k in range(8)], reads=xr)
                    B.op('act', lambda e, bq=bq: e.activation(out=qtm[:, :], in_=banks[bq][:, 0:QW], func=AF.Copy),
                         reads=[('bank', bq)], writes=['qtm'])
                    bk = self.nb()
                    self.mm_group(bk, banks[bk][:, 0:QW], [(xnT[:, k, cs], Win[:, k, ko:ko + QW]) for k in range(8)], reads=xr)
                    B.op('dve', lambda e, bk=bk: e.tensor_copy(out=ktm[:, :], in_=banks[bk][:, 0:QW]),
                         reads=[('bank', bk)], writes=['ktm'])
                    if even:
                        for g2 in range(2):
                            bv = self.nb()
                            self.mm_group(bv, banks[bv][:, 0:384],
                                          [(xnT[:, k, cs], Win[:, k, vo + g2 * 384:vo + (g2 + 1) * 384]) for k in range(8)], reads=xr)
                            B.op('act' if g2 == 0 else 'dve',
                                 (lambda e, bv=bv, g2=g2: e.activation(out=vb[:, 2 * g2:2 * g2 + 2, :],
                                                                       in_=banks[bv][:, 0:384].rearrange("p (h d) -> p h d", h=2),
                                                                       func=AF.Copy)) if g2 == 0 else
                                 (lambda e, bv=bv, g2=g2: e.tensor_copy(out=vb[:, 2 * g2:2 * g2 + 2, :],
                                                                        in_=banks[bv][:, 0:384].rearrange("p (h d) -> p h d", h=2))),
                                 reads=[('bank', bv)], acc=['vb'])
                        if not fwd:
                            for g2 in range(2):
                                br = self.nb()
                                self.mm_group(br, banks[br][:, 0:384],
                                              [(xnT[:, k, cs], Win[:, k, ro + g2 * 384:ro + (g2 + 1) * 384]) for k in range(8)], reads=xr)
                                B.op('act', lambda e, br=br, g2=g2: e.activation(out=rg[:, g2 * 384:(g2 + 1) * 384],
                                                                                 in_=banks[br][:, 0:384], func=AF.Silu),
                                     reads=[('bank', br)], acc=['rg'])
                        bz = self.nb()
                        self.mm_group(bz, banks[bz][:, 0:384], [(gTa[0:17, cs], w2[0:17, :])], reads=['gTa', 'w2'])
                        B.op('act', lambda e, bz=bz: e.activation(out=E[:, 0, :], in_=banks[bz][:, 0:384], func=AF.Exp, scale=-1.0),
                             reads=[('bank', bz)], writes=[('E', 0)])
                        B.op('act', lambda e: e.activation(out=lap[:, :], in_=E[:, 0, :], func=AF.Ln, bias=1.0),
                             reads=[('E', 0)], writes=['lap'])
                    else:
                        bv = self.nb()
                        self.mm_group(bv, banks[bv][:, :], [(xnT[:, k, cs], Win[:, k, vo:vo + 512]) for k in range(8)], reads=xr)
                        B.op('act', lambda e, bv=bv: e.activation(out=vb[:, :, 0:128],
                                                                  in_=banks[bv][:, :].rearrange("p (h d) -> p h d", h=4),
                                                                  func=AF.Copy),
                             reads=[('bank', bv)], acc=['vb'])
                        if not fwd:
                            br = self.nb()
                            self.mm_group(br, banks[br][:, :], [(xnT[:, k, cs], Win[:, k, ro:ro + 512]) for k in range(8)], reads=xr)
                            B.op('act', lambda e, br=br: e.activation(out=rg[:, :], in_=banks[br][:, :], func=AF.Sigmoid),
                                 reads=[('bank', br)], writes=['rg'])
                        bz = self.nb()
                        self.mm_group(bz, banks[bz][:, 0:16], [(xnT[:, k, cs], Win[:, k, 3584:3600]) for k in range(8)], reads=xr)
                        B.op('dve', lambda e, bz=bz: e.tensor_tensor(out=gts[:, :], in0=banks[bz][:, 0:16], in1=gbias[:, :], op=ALU.add),
                             reads=[('bank', bz), 'gbias'], writes=['gts'])
                        io, fo = (0, 4) if fwd else (8, 12)
                        B.op('act', lambda e, fo=fo: e.activation(out=smal[:, 0:4], in_=gts[:, fo:fo + 4], func=AF.Exp, scale=-1.0),
                             reads=['gts'], writes=['smal'])
                        B.op('act', lambda e: e.activation(out=lap[:, :], in_=smal[:, 0:4], func=AF.Ln, bias=1.0),
                             reads=['smal'], writes=['lap'])
                    bc = self.nb()
                    self.mm_group(bc, banks[bc][:, 0:LW], [(tri[:, tI, :], lap[:, :])], reads=['tri', 'lap'])
                    bt = self.nb()
                    self.mm_group(bt, banks[bt][:, 0:LW], [(tri[:, tS, :], lap[:, :])], reads=['tri', 'lap'])
                    bd = self.nb()
                    if even:
                        self.mm_multi([bd], [(banks[bd][0:DK, h:h + 1], [(lap[:, h * DK:(h + 1) * DK], negcol[:, 0:1])]) for h in range(H)],
                                      reads=['lap', 'negcol'])
                        B.op('act', lambda e, bc=bc: e.activation(out=E[:, 0, :], in_=banks[bc][:, 0:LW], func=AF.Exp),
                             reads=[('bank', bc)], writes=[('E', 0)])
                        B.op('act', lambda e, bc=bc: e.activation(out=E[:, 1, :], in_=banks[bc][:, 0:LW], func=AF.Exp, scale=-1.0),
                             reads=[('bank', bc)], writes=[('E', 1)])
                        B.op('act', lambda e, bt=bt: e.activation(out=E[:, 2, :], in_=banks[bt][:, 0:LW], func=AF.Exp),
                             reads=[('bank', bt)], writes=[('E', 2)])
                        B.op('act', lambda e, bd=bd: e.activation(out=dec[0:DK, :], in_=banks[bd][0:DK, 0:H], func=AF.Exp),
                             reads=[('bank', bd)], writes=['dec'])
                        B.op('dve', lambda e: e.scalar_tensor_tensor(out=qin[:, :], in0=qtm[:, :], scalar=qscale, in1=E[:, 0, :],
                                                                     op0=ALU.mult, op1=ALU.mult),
                             reads=['qtm', ('E', 0)], writes=['qin'])
                        B.op('pool', lambda e: e.tensor_tensor(out=kin[:, :], in0=ktm[:, :], in1=E[:, 1, :], op=ALU.mult),
                             reads=['ktm', ('E', 1)], writes=['kin'])
                        B.op('dve', lambda e: e.tensor_tensor(out=kout[:, :], in0=ktm[:, :], in1=E[:, 2, :], op=ALU.mult),
                             reads=['ktm', ('E', 2)], writes=['kout'])
                    else:
                        self.mm_group(bd, banks[bd][:, 0:4], [(negones[:, :], lap[:, :])], reads=['negones', 'lap'])
                        B.op('dve', lambda e, bc=bc, io=io: e.tensor_tensor(out=E[:, 1, :], in0=gts[:, io:io + 4], in1=banks[bc][:, 0:4],
                                                                            op=ALU.subtract),
                             reads=[('bank', bc), 'gts'], writes=[('E', 1)])
                        B.op('dve', lambda e, bt=bt, io=io: e.tensor_tensor(out=E[:, 2, :], in0=gts[:, io:io + 4], in1=banks[bt][:, 0:4],
                                                                            op=ALU.add),
                             reads=[('bank', bt), 'gts'], writes=[('E', 2)])
                        B.op('act', lambda e, bc=bc: e.activation(out=E[:, 0, :], in_=banks[bc][:, 0:4], func=AF.Exp),
                             reads=[('bank', bc)], writes=[('E', 0)])
                        B.op('act', lambda e: e.activation(out=E[:, 1, :], in_=E[:, 1, :], func=AF.Exp),
                             reads=[('E', 1)], writes=[('E', 1)])
                        B.op('act', lambda e: e.activation(out=E[:, 2, :], in_=E[:, 2, :], func=AF.Exp),
                             reads=[('E', 2)], writes=[('E', 2)])
                        B.op('act', lambda e, bd=bd: e.activation(out=dec[:, :], in_=banks[bd][:, 0:4], func=AF.Exp),
                             reads=[('bank', bd)], writes=['dec'])
                        v3 = lambda t: t[:, :].rearrange("p (h d) -> p h d", h=4)
                        bc3 = lambda s_: E[:, s_, :].unsqueeze(2).to_broadcast([128, 4, 128])
                        B.op('dve', lambda e: e.scalar_tensor_tensor(out=v3(qin), in0=v3(qtm), scalar=qscale, in1=bc3(0),
                                                                     op0=ALU.mult, op1=ALU.mult),
                             reads=['qtm', ('E', 0)], writes=['qin'])
                        B.op('pool', lambda e: e.tensor_tensor(out=v3(kin), in0=v3(ktm), in1=bc3(1), op=ALU.mult),
                             reads=['ktm', ('E', 1)], writes=['kin'])
                        B.op('dve', lambda e: e.tensor_tensor(out=v3(kout), in0=v3(ktm), in1=bc3(2), op=ALU.mult),
                             reads=['ktm', ('E', 2)], writes=['kout'])
                    self.transpose_tile(qin[:, :].rearrange("p (h d) -> p h d", h=H), ['qin'], 128, H, 1, qinT, 'qinT', cw=DK)
                    self.transpose_tile(kin[:, :].rearrange("p (h d) -> p h d", h=H), ['kin'], 128, H, 1, kinT, 'kinT', cw=DK)
                    ba = self.nb()
                    self.mm_multi([ba], [(banks[ba][:, h * 128:(h + 1) * 128],
                                          [(kinT[0:DK, 0, h * 128:(h + 1) * 128], qinT[0:DK, 0, h * 128:(h + 1) * 128])]) for h in range(H)],
                                  reads=[('qinT', 0), ('kinT', 0)])
                    B.op('dve', lambda e, ba=ba: e.tensor_tensor(out=attm[:, :], in0=banks[ba][:, :], in1=amask[:, :], op=ALU.mult),
                         reads=[('bank', ba), 'amask'], writes=['attm'])
                    b0 = self.nb()
                    b1 = self.nb()
                    ob = [b0, b1]
                    self.mm_multi(ob, [(banks[ob[h // 2]][:, (h % 2) * DVP:(h % 2 + 1) * DVP],
                                        [(attm[:, h * 128:(h + 1) * 128], vb[:, h, :]),
                                         (qinT[0:DK, 0, h * 128:(h + 1) * 128], Sbf[0:DK, h, :])]) for h in range(H)],
                                  reads=['attm', 'vb', ('qinT', 0), 'Sbf'])
                    c0 = self.nb()
                    c1 = self.nb()
                    cb = [c0, c1]
                    self.mm_multi(cb, [(banks[cb[h // 2]][0:DK, (h % 2) * DVP:(h % 2 + 1) * DVP],
                                        [(kout[:, h * DK:(h + 1) * DK], vb[:, h, :])]) for h in range(H)],
                                  reads=['kout', 'vb'])
                    for h in range(H):
                        B.op('dve', lambda e, h=h, cb=cb: e.scalar_tensor_tensor(
                            out=S32[0:DK, h, :], in0=S32[0:DK, h, :], scalar=dec[0:DK, h:h + 1],
                            in1=banks[cb[h // 2]][0:DK, (h % 2) * DVP:(h % 2 + 1) * DVP], op0=ALU.mult, op1=ALU.add),
                             reads=['S32', 'dec', ('bank', cb[h // 2])], acc=['S32'])
                    B.op('act', lambda e: e.activation(out=Sbf[0:DK, :, :], in_=S32[0:DK, :, :], func=AF.Copy),
                         reads=['S32'], writes=['Sbf'])
                    ov = lambda b_: banks[b_][:, 0:2 * DVP].rearrange("p (h d) -> p h d", h=2)
                    if even:
                        if fwd:
                            B.op('act', lambda e, b0=b0, j=j: e.activation(out=ost[:, j, 0:384], in_=banks[b0][:, 0:384], func=AF.Copy),
                                 reads=[('bank', b0)], acc=['ost'])
                            B.op('dve', lambda e, b1=b1, j=j: e.tensor_copy(out=ost[:, j, 384:768], in_=banks[b1][:, 0:384]),
                                 reads=[('bank', b1)], acc=['ost'])
                        else:
                            for g2, bb in enumerate(ob):
                                B.op('dve', lambda e, bb=bb, g2=g2, j=j: e.tensor_tensor(out=osum[:, g2 * 384:(g2 + 1) * 384],
                                                                                         in0=banks[bb][:, 0:384],
                                                                                         in1=ost[:, j, g2 * 384:(g2 + 1) * 384], op=ALU.add),
                                     reads=[('bank', bb), 'ost'], acc=['osum'])
                    else:
                        for g2, bb in enumerate(ob):
                            B.op('act', lambda e, bb=bb, g2=g2: e.activation(out=rec[:, 2 * g2:2 * g2 + 2], in_=ov(bb)[:, :, 128],
                                                                             func=AF.Square),
                                 reads=[('bank', bb)], acc=['rec0'])
                        B.op('dve', lambda e: e.tensor_scalar(out=rec[:, 0:4], in0=rec[:, 0:4], scalar1=1.0, scalar2=None, op0=ALU.max),
                             reads=['rec0'], writes=['rec0'])
                        B.op('act', lambda e: e.activation(out=rec[:, 4:8], in_=rec[:, 0:4], func=AF.Ln), reads=['rec0'], writes=['rec1'])
                        B.op('act', lambda e: e.activation(out=rec[:, 4:8], in_=rec[:, 4:8], func=AF.Exp, scale=-0.5),
                             reads=['rec1'], writes=['rec'])
                        for g2, bb in enumerate(ob):
                            dst_ap = (ost[:, j, g2 * 256:(g2 + 1) * 256] if fwd else otmp[:, g2 * 256:(g2 + 1) * 256])
                            B.op('dve', lambda e, bb=bb, g2=g2, dst_ap=dst_ap: e.tensor_tensor(
                                out=dst_ap.rearrange("p (h d) -> p h d", h=2), in0=ov(bb)[:, :, 0:128],
                                in1=rec[:, 4 + 2 * g2:6 + 2 * g2].unsqueeze(2).to_broadcast([128, 2, 128]), op=ALU.mult),
                                 reads=[('bank', bb), 'rec'], acc=['ost' if fwd else 'otmp'])
                        if not fwd:
                            B.op('pool', lambda e, j=j: e.tensor_tensor(out=osum[:, :], in0=otmp[:, :], in1=ost[:, j, :], op=ALU.add),
                                 reads=['otmp', 'ost'], writes=['osum'])
                    if not fwd:
                        B.op('pool', lambda e: e.memset(hrs[:, 0:4], 0.0), writes=['hrs0'])
                        for h in range(H):
                            B.op('act', lambda e, h=h: e.activation(out=ytmp[:, h * DV:(h + 1) * DV], in_=osum[:, h * DV:(h + 1) * DV],
                                                                    func=AF.Square, accum_out=hrs[:, h:h + 1]),
                                 reads=['osum'], acc=['hrs0', 'ytmp'])
                        B.op('act', lambda e: e.activation(out=hrs[:, 4:8], in_=hrs[:, 0:4], func=AF.Ln, scale=1.0 / DV,
                                                           bias=self.epsc[:, 0:1]),
                             reads=['hrs0', 'epsc'], writes=['hrs1'])
                        B.op('act', lambda e: e.activation(out=hrs[:, 4:8], in_=hrs[:, 4:8], func=AF.Exp, scale=-0.5),
                             reads=['hrs1'], writes=['hrs'])
                        B.op('dve', lambda e: e.tensor_tensor(out=ytmp[:, :].rearrange("p (h d) -> p h d", h=H),
                                                              in0=osum[:, :].rearrange("p (h d) -> p h d", h=H),
                                                              in1=hrs[:, 4:8].unsqueeze(2).to_broadcast([128, H, DV]), op=ALU.mult),
                             reads=['osum', 'hrs'], writes=['ytmp'])
                        B.op('pool', lambda e: e.tensor_tensor(out=ytmp[:, :], in0=ytmp[:, :], in1=gmix[:, :], op=ALU.mult),
                             reads=['ytmp', 'gmix'], writes=['ytmp'])
                        B.op('dve', lambda e, j=j: e.tensor_tensor(out=ybf[:, j, YO:YO + VW], in0=ytmp[:, :], in1=rg[:, :], op=ALU.mult),
                             reads=['ytmp', 'rg'], writes=[('ybf', j)])
                if fwd:
                    self.dma('pool', ofw[tsl, :].rearrange("(j p) d -> p j d", p=128), ost[:, :, :], 'st_o', reads=['ost'])
                else:
                    if even:
                        B.op('dve', lambda e: e.tensor_scalar(out=yft[:, :, :], in0=yfa[:, :, :], scalar1=C['blend'][:, 0:1],
                                                              scalar2=None, op0=ALU.mult),
                             reads=['yfa', 'c_blend'], writes=['yft'])
                        B.op('dve', lambda e: e.scalar_tensor_tensor(out=ybf[:, :, 0:256], in0=yfb[:, :, :], scalar=C['blend'][:, 1:2],
                                                                     in1=yft[:, :, :], op0=ALU.mult, op1=ALU.add),
                             reads=['yfb', 'yft', 'c_blend'], writes=[('ybf', 'f')])
                        self.transpose_tile(ybf, self.xk('ybf', 4) + [('ybf', 'f')], 128, 4, 8, yT, 'yT')
                    else:
                        self.transpose_tile(ybf, self.xk('ybf', 4), 128, 4, 4, yT, 'yT', kofs=4, col0=512)
                    for j in range(4):
                        for n in range(2):
                            bi = self.nb()
                            self.mm_group(bi, banks[bi][:, :],
                                          [(yT[:, k, j * 128:(j + 1) * 128], Wout[:, k, n * 512:(n + 1) * 512]) for k in range(8)],
                                          reads=self.xk('yT') + ['Wout'])
                            B.op('dve', lambda e, bi=bi, j=j, n=n: e.tensor_tensor(out=xt[:, j, n * 512:(n + 1) * 512],
                                                                                   in0=xt[:, j, n * 512:(n + 1) * 512],
                                                                                   in1=banks[bi][:, :], op=ALU.add),
                                 reads=[('bank', bi), 'xt'], acc=['xt'])
                    self.dma('pool', dst[tsl, :].rearrange("(j p) d -> p j d", p=128), xt[:, :, :], 'st_h', reads=['xt'])
            B.barrier()
            B.emit()

    def fnet_pass(self):
        B, I, S, NT, NA, banks = self.B, self.I, self.S, self.NT, self.NA, self.banks
        GC = min(16, NA // 4)
        with ExitStack() as st:
            A = lambda n, sh, dt: self.alloc(st, n, sh, dt)
            zt = A('zt', [128, 2, 128, 128], BF16)
            Y = A('Y', [128, 2, NA, 128], BF16)
            Ost = A('Ost', [128, NA, 128], BF16)
            Rm = A('Rm', [128, 2, 2 * NA], BF16)
            Gt = A('Gt', [128, GC, 2, 128], BF16)
            Zv = S['Z'].rearrange("(r c) (a b) -> a r c b", r=2, b=128)
            for path in (0, 1):
                Aa = NA if path == 0 else NA // 4
                nseq = 1 if path == 0 else 4
                yf = S['yfA'] if path == 0 else S['yfB']
                G = I['fGA'] if path == 0 else I['fGB']
                self.dma('sp', Rm[0:NA, :, :], I['fR'][path], 'c_Rm', writes=['Rm'])
                yfv = yf.rearrange("(j d c) h -> d j c h", j=nseq, d=128, c=Aa)
                for half in range(2):
                    for ri in range(2):
                        for cq in range(4):
                            self.dma('sp', zt[0:NA, ri, cq * 32:(cq + 1) * 32, :],
                                     Zv[:, ri, half * 128 + cq * 32:half * 128 + (cq + 1) * 32, :], 'ld_z%d' % ri,
                                     writes=[('zt', ri)] if cq == 0 else [], acc=[] if cq == 0 else [('zt', ri)])
                    for c in range(128):
                        bi = self.nb()
                        self.mm_group(bi, banks[bi][:, 0:2 * NA],
                                      [(zt[0:NA, 0, c, :], Rm[0:NA, 0, :]), (zt[0:NA, 1, c, :], Rm[0:NA, 1, :])],
                                      reads=[('zt', 0), ('zt', 1), 'Rm'])
                        srcv = banks[bi][:, 0:2 * NA].rearrange("p (r s) -> p r s", r=2)
                        if c % 2 == 0:
                            B.op('act', lambda e, c=c, srcv=srcv: e.activation(out=Y[:, :, :, c], in_=srcv, func=AF.Copy),
                                 reads=[('bank', bi)], acc=['Y'])
                        else:
                            B.op('dve', lambda e, c=c, srcv=srcv: e.tensor_copy(out=Y[:, :, :, c], in_=srcv),
                                 reads=[('bank', bi)], acc=['Y'])
                    Y5 = Y[:, :, :, :].rearrange("p r (j c) h -> p r j c h", j=nseq)
                    O4 = Ost[:, :, :].rearrange("p (j c) h -> p j c h", j=nseq)
                    for c0 in range(0, Aa, GC):
                        gc = min(GC, Aa - c0)
                        self.dma('sp', Gt[:, 0:gc, :, :], G[:, c0:c0 + gc, :, :], 'ld_g', writes=['Gt'])
                        if path == 0:
                            for c4 in range(c0, c0 + gc, 4):
                                bi = self.nb()
                                n4 = min(4, c0 + gc - c4)
                                self.mm_multi([bi], [(banks[bi][:, q * 128:(q + 1) * 128],
                                                      [(Gt[:, c4 + q - c0, 0, :], Y[:, 0, c4 + q, :]),
                                                       (Gt[:, c4 + q - c0, 1, :], Y[:, 1, c4 + q, :])]) for q in range(n4)],
                                              reads=['Gt', 'Y'])
                                B.op('act' if (c4 // 4) % 2 == 0 else 'dve',
                                     (lambda e, bi=bi, c4=c4, n4=n4: e.activation(out=Ost[:, c4:c4 + n4, :],
                                                                                   in_=banks[bi][:, 0:n4 * 128].rearrange("p (q h) -> p q h", q=n4),
                                                                                   func=AF.Copy)) if (c4 // 4) % 2 == 0 else
                                     (lambda e, bi=bi, c4=c4, n4=n4: e.tensor_copy(out=Ost[:, c4:c4 + n4, :],
                                                                                    in_=banks[bi][:, 0:n4 * 128].rearrange("p (q h) -> p q h", q=n4))),
                                     reads=[('bank', bi)], acc=['Ost'])
                        else:
                            for c in range(c0, c0 + gc):
                                bi = self.nb()
                                self.mm_group(bi, banks[bi][:, :].rearrange("p (j h) -> p j h", j=4),
                                              [(Gt[:, c - c0, 0, :], Y5[:, 0, :, c, :]), (Gt[:, c - c0, 1, :], Y5[:, 1, :, c, :])],
                                              reads=['Gt', 'Y'])
                                B.op('act' if c % 2 == 0 else 'dve',
                                     (lambda e, bi=bi, c=c: e.activation(out=O4[:, :, c, :],
                                                                         in_=banks[bi][:, :].rearrange("p (j h) -> p j h", j=4),
                                                                         func=AF.Copy)) if c % 2 == 0 else
                                     (lambda e, bi=bi, c=c: e.tensor_copy(out=O4[:, :, c, :],
                                                                          in_=banks[bi][:, :].rearrange("p (j h) -> p j h", j=4))),
                                     reads=[('bank', bi)], acc=['Ost'])
                    for jq in range(nseq):
                        for cq in range(0, Aa, 32):
                            ce = min(Aa, cq + 32)
                            self.dma('pool', yfv[:, jq, cq:ce, half * 128:(half + 1) * 128], O4[:, jq, cq:ce, :], 'st_yf',
                                     reads=['Ost'])
                    B.op('pool', lambda e: e.memset(Ost[:, 0:1, 0:1], 0.0), reads=[], writes=['Ost'])
                    B.op('pool', lambda e: e.memset(Y[:, 0:1, 0:1, 0:1], 0.0), reads=[], writes=['Y'])
            B.barrier()
            B.emit()

    def ffn_pass(self, L, src, dst):
        B, I, S, C, NT, banks = self.B, self.I, self.S, self.C, self.NT, self.banks
        NF = 22
        with ExitStack() as st:
            A = lambda n, sh, dt: self.alloc(st, n, sh, dt)
            Wup = self.load_weight(st, 'Wup', I['ffn_w_up'][L], D, 5632)
            Wdn = self.load_weight(st, 'Wdn', I['ffn_w_down'][L], 2816, D)
            gain = self.cload(st, 'gain', [128, D], F32, I['norms'][:, 2 + L, :])
            fconv = self.cload(st, 'fconv', [128, NF, 4], F32, I['ffn_conv'][:, L, :, :])
            self.epsc = A('epsc', [128, 1], F32)
            B.op('pool', lambda e: e.memset(self.epsc[:, :], EPS), writes=['epsc'])
            xt = A('xt', [128, 4, D], F32)
            sq = None
            rs = A('rs', [128, 8], F32)
            xn = A('xn', [128, 4, D], BF16)
            xnT = A('xnT', [128, 8, 512], BF16)
            GH = A('GH', [128, NF, 2 * NT], F32)
            Gb = A('Gb', [128, 2, 514], F32)
            acc = A('acc', [128, 2, 512], F32)
            sl = A('sl', [128, 2, 512], BF16)
            actT = A('actT', [128, NF, 512], BF16)
            srcv = src.rearrange("(i t) d -> i t d", t=512)
            self.dma('sp', xt[0:NT, 0, :], srcv[:, 0, :], 'ld_x', writes=['xt'])
            self.dma('sp', xt[NT:2 * NT, 0, :], srcv[:, 511, :], 'ld_x', acc=['xt'])
            self.norm_tile(xt, 'xt', 2 * NT, 1, gain, 'gain', xn, 'xn', sq, rs)
            self.transpose_tile(xn, [('xn', 0)], 2 * NT, 1, 8, xnT, 'xnT')
            for f in range(NF):
                bi = self.nb()
                self.mm_group(bi, banks[bi][:, 0:2 * NT],
                              [(Wup[:, k, f * 128:(f + 1) * 128], xnT[:, k, 0:2 * NT]) for k in range(8)],
                              reads=self.xk('xnT') + ['Wup'])
                B.op('act', lambda e, bi=bi, f=f: e.activation(out=GH[:, f, :], in_=banks[bi][:, 0:2 * NT], func=AF.Copy),
                     reads=[('bank', bi)], acc=['GH'])
            for i in range(NT):
                tsl = slice(512 * i, 512 * (i + 1))
                self.dma('sp', xt[:, :, :], src[tsl, :].rearrange("(j p) d -> p j d", p=128), 'ld_x', writes=['xt'])
                self.norm_tile(xt, 'xt', 128, 4, gain, 'gain', xn, 'xn', sq, rs)
                self.transpose_tile(xn, self.xk('xn', 4), 128, 4, 8, xnT, 'xnT')
                xr = self.xk('xnT') + ['Wup']
                for f in range(NF):
                    s = f % 2
                    bg = self.nb()
                    self.mm_group(bg, banks[bg][:, :], [(Wup[:, k, f * 128:(f + 1) * 128], xnT[:, k, :]) for k in range(8)], reads=xr)
                    bv = self.nb()
                    self.mm_group(bv, banks[bv][:, :],
                                  [(Wup[:, k, 2816 + f * 128:2816 + (f + 1) * 128], xnT[:, k, :]) for k in range(8)], reads=xr)
                    B.op('act', lambda e, bg=bg, s=s: e.activation(out=Gb[:, s, 1:513], in_=banks[bg][:, :], func=AF.Copy),
                         reads=[('bank', bg)], writes=[('Gb', s)])
                    if i > 0:
                        B.op('pool', lambda e, s=s, f=f, i=i: e.tensor_scalar(out=Gb[:, s, 0:1], in0=GH[:, f, NT + i - 1:NT + i],
                                                                              scalar1=C['mfwd'][:, i:i + 1], scalar2=None, op0=ALU.mult),
                             reads=['GH', 'c_mfwd'], writes=[('Gb', s, 'l')])
                    else:
                        B.op('pool', lambda e, s=s: e.memset(Gb[:, s, 0:1], 0.0), writes=[('Gb', s, 'l')])
                    if i < NT - 1:
                        B.op('pool', lambda e, s=s, f=f, i=i: e.tensor_scalar(out=Gb[:, s, 513:514], in0=GH[:, f, i + 1:i + 2],
                                                                              scalar1=C['mbwd'][:, i:i + 1], scalar2=None, op0=ALU.mult),
                             reads=['GH', 'c_mbwd'], writes=[('Gb', s, 'r')])
                    else:
                        B.op('pool', lambda e, s=s: e.memset(Gb[:, s, 513:514], 0.0), writes=[('Gb', s, 'r')])
                    B.op('dve', lambda e, s=s, f=f: e.tensor_scalar(out=acc[:, s, :], in0=Gb[:, s, 1:513], scalar1=fconv[:, f, 1:2],
                                                                    scalar2=fconv[:, f, 3:4], op0=ALU.mult, op1=ALU.add),
                         reads=[('Gb', s), 'fconv'], writes=[('acc', s)])
                    B.op('dve', lambda e, s=s, f=f: e.scalar_tensor_tensor(out=acc[:, s, :], in0=Gb[:, s, 0:512], scalar=fconv[:, f, 0:1],
                                                                            in1=acc[:, s, :], op0=ALU.mult, op1=ALU.add),
                         reads=[('Gb', s), ('Gb', s, 'l'), ('acc', s), 'fconv'], writes=[('acc', s)])
                    B.op('dve', lambda e, s=s, f=f: e.scalar_tensor_tensor(out=acc[:, s, :], in0=Gb[:, s, 2:514], scalar=fconv[:, f, 2:3],
                                                                           in1=acc[:, s, :], op0=ALU.mult, op1=ALU.add),
                         reads=[('Gb', s), ('Gb', s, 'r'), ('acc', s), 'fconv'], writes=[('acc', s)])
                    B.op('act', lambda e, s=s: e.activation(out=sl[:, s, :], in_=acc[:, s, :], func=AF.Silu),
                         reads=[('acc', s)], writes=[('sl', s)])
                    B.op('dve', lambda e, s=s, f=f, bv=bv: e.tensor_tensor(out=actT[:, f, :], in0=sl[:, s, :], in1=banks[bv][:, :],
                                                                           op=ALU.mult),
                         reads=[('sl', s), ('bank', bv)], writes=[('actT', f)])
                for j in range(4):
                    for n in range(2):
                        bi = self.nb()
                        self.mm_group(bi, banks[bi][:, :],
                                      [(actT[:, f, j * 128:(j + 1) * 128], Wdn[:, f, n * 512:(n + 1) * 512]) for f in range(NF)],
                                      reads=self.xk('actT', NF) + ['Wdn'])
                        B.op('dve', lambda e, bi=bi, j=j, n=n: e.tensor_tensor(out=xt[:, j, n * 512:(n + 1) * 512],
                                                                               in0=xt[:, j, n * 512:(n + 1) * 512],
                                                                               in1=banks[bi][:, :], op=ALU.add),
                             reads=[('bank', bi), 'xt'], acc=['xt'])
                self.dma('pool', dst[tsl, :].rearrange("(j p) d -> p j d", p=128), xt[:, :, :], 'st_h', reads=['xt'])
            B.barrier()
            B.emit()

    def final_pass(self, src, yout):
        B, I, NT = self.B, self.I, self.NT
        with ExitStack() as st:
            A = lambda n, sh, dt: self.alloc(st, n, sh, dt)
            gain = self.cload(st, 'gain', [128, D], F32, I['norms'][:, 6, :])
            self.epsc = A('epsc', [128, 1], F32)
            B.op('pool', lambda e: e.memset(self.epsc[:, :], EPS), writes=['epsc'])
            xt = A('xt', [128, 4, D], F32)
            yo = A('yo', [128, 4, D], F32)
            sq = None
            rs = A('rs', [128, 8], F32)
            P = self.ple_alloc(st, 1)
            for i in range(NT):
                tsl = slice(512 * i, 512 * (i + 1))
                self.dma('sp', xt[:, :, :], src[tsl, :].rearrange("(j p) d -> p j d", p=128), 'ld_x', writes=['xt'])
                self.ple_step(P, i, xt, 'xt', sq, rs)
                self.norm_tile(xt, 'xt', 128, 4, gain, 'gain', yo, 'yo', sq, rs)
                self.dma('pool', yout[tsl, :].rearrange("(j p) d -> p j d", p=128), yo[:, :, :], 'st_y', reads=self.xk('yo', 4))
            B.barrier()
            B.emit()


def _bf(a):
    return np.ascontiguousarray(a).astype(ml_dtypes.bfloat16)


def _consts(NT):
    NA = 4 * NT
    T = 512 * NT
    c = {}
    c['ident'] = _bf(np.eye(128, dtype=np.float32))
    lp = np.arange(128)[:, None]
    l = np.arange(128)[None, :]
    mats = [(lp <= l), (lp > l), (lp >= l), (lp < l)]
    tri = np.zeros((128, 8, 128), np.float32)
    for q, m in enumerate(mats):
        tri[:, q, :] = m.astype(np.float32) * (-1.0 / 16.0)
        tri[:, 4 + q, :] = m.astype(np.float32) * (-1.0)
    c['tri'] = tri
    am = np.zeros((128, 2, 512), np.float32)
    am[:, 0, :] = np.tile((lp <= l).astype(np.float32), (1, 4))
    am[:, 1, :] = np.tile((lp >= l).astype(np.float32), (1, 4))
    c['amask'] = _bf(am)
    nc_ = np.zeros((128, 2), np.float32)
    nc_[:, 0] = -1.0 / 16.0
    nc_[:, 1] = -1.0
    c['negcol'] = nc_
    c['negones'] = -np.ones((128, 128), np.float32)
    ch = np.arange(64)
    ang = 2 * np.pi * np.outer(ch, ch) / 64.0
    cs = np.zeros((128, 2, 128), np.float64)
    for b in range(2):
        cs[b * 64:(b + 1) * 64, 0, b * 64:(b + 1) * 64] = np.cos(ang)
        cs[b * 64:(b + 1) * 64, 1, b * 64:(b + 1) * 64] = np.sin(ang)
    c['cs64'] = _bf(cs.astype(np.float32))
    fR = np.zeros((2, NA, 2, 2 * NA), np.float64)
    for path in range(2):
        Aa = NA if path == 0 else NA // 4
        nseq = 1 if path == 0 else 4
        a = np.arange(Aa)
        angA = 2 * np.pi * np.outer(a, a) / Aa
        Cm = np.zeros((NA, NA))
        Sm = np.zeros((NA, NA))
        for j in range(nseq):
            Cm[j * Aa:(j + 1) * Aa, j * Aa:(j + 1) * Aa] = np.cos(angA)
            Sm[j * Aa:(j + 1) * Aa, j * Aa:(j + 1) * Aa] = np.sin(angA)
        fR[path, :, 0, :NA] = Cm
        fR[path, :, 0, NA:] = -Sm
        fR[path, :, 1, :NA] = -Sm
        fR[path, :, 1, NA:] = -Cm
    c['fR'] = _bf(fR.astype(np.float32))
    for path, name in ((0, 'fGA'), (1, 'fGB')):
        Aa = NA if path == 0 else NA // 4
        Sx = 128 * Aa
        b = np.arange(128)[:, None, None]
        cc = np.arange(Aa)[None, :, None]
        d = np.arange(128)[None, None, :]
        ph = 2 * np.pi * ((b * (cc + Aa * d)) % Sx) / Sx
        nrm = 1.0 / np.sqrt(Sx * 64.0)
        G = np.zeros((128, Aa, 2, 128), np.float64)
        G[:, :, 0, :] = np.cos(ph) * nrm
        G[:, :, 1, :] = np.sin(ph) * nrm
        c[name] = _bf(G.astype(np.float32))
    return c


_NC_CACHE = {}


def _get_nc(NT, debug=False):
    key = (NT, debug)
    if key not in _NC_CACHE:
        g = Gen(NT, debug=debug)
        g.build()
        _NC_CACHE[key] = g
    return _NC_CACHE[key]


def _rep(v):
    v = np.asarray(v, np.float32).reshape(1, -1)
    return np.ascontiguousarray(np.broadcast_to(v, (128, v.shape[1])))


def _shared_inputs(w, NT):
    f32 = lambda a: np.ascontiguousarray(np.asarray(a, np.float32))
    d = dict(_consts(NT))
    d['e_w_in'] = f32(w['e_w_in'][0])
    d['e_w_out'] = f32(w['e_w_out'][0])
    d['o_w_in'] = f32(w['o_w_in'][0])
    d['o_w_out'] = f32(w['o_w_out'][0])
    d['ffn_w_up'] = f32(w['ffn_w_up'])
    d['ffn_w_down'] = f32(w['ffn_w_down'])
    d['ple_w'] = f32(w['ple_w'])
    d['ple_gate_w'] = f32(w['ple_gate_w'])
    norms = np.stack([w['e_norm'][0], w['o_norm'][0], w['ffn_norm'][0], w['ffn_norm'][1],
                      w['ple_gate_norm'][0], w['ple_gate_norm'][1], w['final_norm']], 0).astype(np.float32)
    d['norms'] = np.ascontiguousarray(np.broadcast_to(norms[None], (128, 7, D)))
    d['gla_norm'] = _rep(w['e_gla_norm'][0])
    d['mlstm_norm'] = _rep(w['o_mlstm_norm'][0])
    d['gate_bias'] = _rep(w['o_gate_bias'][0])
    w2 = np.zeros((2, 17, 384), np.float32)
    w2[0, :16] = w['e_gla_w2_f'][0]
    w2[0, 16] = w['e_gla_b_f'][0]
    w2[1, :16] = w['e_gla_w2_b'][0]
    w2[1, 16] = w['e_gla_b_b'][0]
    d['w2aug'] = w2
    oc = np.zeros((128, 4, 4), np.float32)
    cw = np.asarray(w['o_conv_w'][0], np.float32)
    cb = np.asarray(w['o_conv_b'][0], np.float32)
    for t in range(3):
        oc[:, :, t] = cw[t].reshape(4, 128).T
    oc[:, :, 3] = cb.reshape(4, 128).T
    d['o_conv'] = oc
    fc = np.zeros((128, 2, 22, 4), np.float32)
    for L in range(2):
        fw = np.asarray(w['ffn_conv_w'][L], np.float32)
        fb = np.asarray(w['ffn_conv_b'][L], np.float32)
        for t in range(3):
            fc[:, L, :, t] = fw[t].reshape(22, 128).T
        fc[:, L, :, 3] = fb.reshape(22, 128).T
    d['ffn_conv'] = fc
    return d


def _core_inputs(shared, x, p, is_prompt, NT):
    d = dict(shared)
    d['x'] = np.ascontiguousarray(x, np.float32)
    d['p'] = np.ascontiguousarray(p, np.float32)
    seg = max(1, NT // 4)
    mf = np.ones((NT,), np.float32)
    mb = np.ones((NT,), np.float32)
    mf[0] = 0.0
    mb[NT - 1] = 0.0
    if not is_prompt:
        for i in range(NT):
            if i % seg == 0:
                mf[i] = 0.0
            if (i + 1) % seg == 0:
                mb[i] = 0.0
    d['mfwd'] = _rep(mf)
    d['mbwd'] = _rep(mb)
    d['blend'] = _rep(np.array([1.0, 0.0] if is_prompt else [0.0, 1.0], np.float32))
    return d


def run_streams(streams, weights, NT, debug=False, n_cores=8):
    g = _get_nc(NT, debug)
    shared = _shared_inputs(weights, NT)
    in_maps = []
    for c in range(n_cores):
        x, p, ip = streams[c % len(streams)]
        in_maps.append(_core_inputs(shared, x, p, ip, NT))
    res = run_bass_kernel_spmd(g.nc, in_maps, core_ids=list(range(n_cores)))
    return [res.results[c]['y'] for c in range(len(streams))], res


def kernel(x_prompt, x_sample, p_prompt, p_sample, **w):
    NT = 32
    T = 512 * NT
    xp = np.asarray(x_prompt, np.float32)
    xs = np.asarray(x_sample, np.float32)
    pp = np.asarray(p_prompt, np.float32)
    ps = np.asarray(p_sample, np.float32)
    streams = []
    for b in range(2):
        streams.append((xp[b], pp[:, b], True))
    for c in range(2):
        streams.append((xs[4 * c:4 * c + 4].reshape(T, D), ps[:, 4 * c:4 * c + 4].reshape(2, T, 256), False))
    ys, _ = run_streams(streams, w, NT)
    y_prompt = np.stack([ys[0], ys[1]], 0).astype(np.float32)
    y_sample = np.concatenate([ys[2].reshape(4, 4096, D), ys[3].reshape(4, 4096, D)], 0).astype(np.float32)
    return (y_prompt, y_sample)
```

```python
import numpy as np
import ml_dtypes
from contextlib import ExitStack
import concourse.bass as bass
import concourse.mybir as mybir
from concourse.bass_utils import run_bass_kernel_spmd

F32 = mybir.dt.float32
BF16 = mybir.dt.bfloat16
ALU = mybir.AluOpType
AF = mybir.ActivationFunctionType
AX = mybir.AxisListType

D = 1024
EPS = 1e-6
ENGS = ('pe', 'act', 'dve', 'pool', 'sp')


class Op:
    __slots__ = ('eng', 'fn', 'waits', 'needed', 'sig', 'dkey', 'dval', 'idx', 'dinc')


class Res:
    __slots__ = ('w', 'r')

    def __init__(self):
        self.w = []
        self.r = []


class Builder:
    def __init__(self, nc, stack):
        self.nc = nc
        self.stack = stack
        self.ops = {e: [] for e in ENGS}
        self.nops = {e: 0 for e in ENGS}
        self.sigcnt = {e: 0 for e in ENGS}
        self.known = {e: {} for e in ENGS}
        self.last = {e: None for e in ENGS}
        self.dtot = {}
        self.sems = {}
        self.res = {}
        for e in ENGS:
            self.sems['e_' + e] = stack.enter_context(nc.semaphore('e_' + e))

    def R(self, key):
        r = self.res.get(key)
        if r is None:
            r = self.res[key] = Res()
        return r

    def _dsem(self, key):
        n = 'd_' + key
        if n not in self.sems:
            self.sems[n] = self.stack.enter_context(self.nc.semaphore(n))
        return self.sems[n]

    def op(self, eng, fn, reads=(), writes=(), dma=None, acc=(), dinc=16):
        o = Op()
        o.dinc = dinc
        o.eng = eng
        o.fn = fn
        o.needed = False
        o.sig = None
        o.dkey = dma
        o.dval = None
        self.nops[eng] += 1
        o.idx = self.nops[eng]
        evs = []
        for k in reads:
            evs += [(ev, 0) for ev in self.R(k).w]
        for k in writes:
            r = self.R(k)
            evs += [(ev, 1) for ev in r.w]
            evs += [(ev, 2) for ev in r.r]
        for k in acc:
            r = self.R(k)
            evs += [(ev, 3) for ev in r.w]
            evs += [(ev, 2) for ev in r.r]
        waits = {}
        kn = self.known[eng]
        for ev, kind in evs:
            if ev[0] == 'e':
                p = ev[1]
                if p.eng == eng:
                    if eng == 'pe' or kind >= 2:
                        continue
                k = ('e', p.eng)
                v = p.idx
            else:
                p = None
                k = ('d', ev[1])
                v = ev[2]
            if kn.get(k, 0) >= v:
                continue
            if k not in waits or waits[k][0] < v:
                waits[k] = (v, p)
        for k, (v, p) in waits.items():
            kn[k] = v
            if p is not None:
                p.needed = True
        o.waits = waits
        if dma is not None:
            self._dsem(dma)
            self.dtot[dma] = self.dtot.get(dma, 0) + dinc
            o.dval = self.dtot[dma]
            ev = ('d', dma, o.dval)
        else:
            ev = ('e', o)
        for k in reads:
            self.R(k).r.append(ev)
        for k in writes:
            r = self.R(k)
            r.w = [ev]
            r.r = []
        for k in acc:
            self.R(k).w.append(ev)
        self.ops[eng].append(o)
        if dma is None:
            self.last[eng] = o
        return o

    def barrier(self):
        lasts = {e: self.last[e] for e in ENGS if self.last[e] is not None}
        for e in ENGS:
            o = Op()
            o.eng = e
            o.fn = None
            o.needed = False
            o.sig = None
            o.dkey = None
            o.dval = None
            o.dinc = 16
            self.nops[e] += 1
            o.idx = self.nops[e]
            waits = {}
            for e2, p in lasts.items():
                if e2 == e:
                    continue
                if self.known[e].get(('e', e2), 0) >= p.idx:
                    continue
                waits[('e', e2)] = (p.idx, p)
                p.needed = True
                self.known[e][('e', e2)] = p.idx
            for key, tot in self.dtot.items():
                if self.known[e].get(('d', key), 0) >= tot:
                    continue
                waits[('d', key)] = (tot, None)
                self.known[e][('d', key)] = tot
            o.waits = waits
            self.ops[e].append(o)
        self.res = {}

    def emit(self):
        for e in ENGS:
            for o in self.ops[e]:
                if o.needed and o.dkey is None and o.fn is not None:
                    self.sigcnt[e] += 1
                    o.sig = self.sigcnt[e]
        sems = self.sems
        with self.nc.Block() as blk:
            decos = {'pe': blk.tensor, 'act': blk.scalar, 'dve': blk.vector, 'pool': blk.gpsimd, 'sp': blk.sync}
            for e in ENGS:
                ops = self.ops[e]

                def body(eng, ops=ops, e=e):
                    for o in ops:
                        for k, (v, p) in o.waits.items():
                            if k[0] == 'e':
                                eng.wait_ge(sems['e_' + k[1]], p.sig)
                            else:
                                eng.wait_ge(sems['d_' + k[1]], v)
                        if o.fn is None:
                            continue
                        ins = o.fn(eng)
                        if o.dkey is not None:
                            ins.then_inc(sems['d_' + o.dkey], o.dinc)
                        elif o.sig is not None:
                            ins.then_inc(sems['e_' + e], 1)

                decos[e](body)
        self.ops = {e: [] for e in ENGS}


class Gen:
    def __init__(self, NT, debug=False):
        self.NT = NT
        self.SEG = NT // 2
        self.T = 512 * NT
        self.TB = 256 * NT
        self.NAs = 2 * NT
        self.debug = debug
        self.nc = bass.Bass("TRN2", target_bir_lowering=False)

    def inp(self, name, shape, dt=F32):
        return self.nc.dram_tensor(name, list(shape), dt, kind="ExternalInput").ap()

    def scratch(self, name, shape, dt):
        kind = "ExternalOutput" if self.debug else "Internal"
        return self.nc.dram_tensor(name, list(shape), dt, kind=kind).ap()

    def build(self):
        nc = self.nc
        NT, T, TB, NAs = self.NT, self.T, self.TB, self.NAs
        I = {}
        I['x'] = self.inp('x', [T, D])
        I['xfull'] = self.inp('xfull', [4 * TB, D])
        I['p'] = self.inp('p', [2, T, 256])
        I['coef'] = self.inp('coef', [128, 48])
        I['e_w_in'] = self.inp('e_w_in', [D, 2592])
        I['e_w_out'] = self.inp('e_w_out', [D, D])
        I['o_w_in'] = self.inp('o_w_in', [D, 3600])
        I['o_w_out'] = self.inp('o_w_out', [D, D])
        I['ffn_w_up'] = self.inp('ffn_w_up', [2, D, 5632])
        I['ffn_w_down'] = self.inp('ffn_w_down', [2, 2816, D])
        I['ple_w'] = self.inp('ple_w', [2, 256, D])
        I['ple_gate_w'] = self.inp('ple_gate_w', [2, D, D])
        I['norms'] = self.inp('norms', [128, 7, D])
        I['gla_norm'] = self.inp('gla_norm', [128, 768])
        I['mlstm_norm'] = self.inp('mlstm_norm', [128, 512])
        I['gate_bias'] = self.inp('gate_bias', [128, 16])
        I['w2aug'] = self.inp('w2aug', [2, 17, 384])
        I['o_conv'] = self.inp('o_conv', [128, 4, 4])
        I['ffn_conv'] = self.inp('ffn_conv', [128, 2, 22, 4])
        I['ident'] = self.inp('ident', [128, 128], BF16)
        I['tri'] = self.inp('tri', [128, 8, 128])
        I['amask'] = self.inp('amask', [128, 2, 512], BF16)
        I['negcol'] = self.inp('negcol', [128, 2])
        I['negones'] = self.inp('negones', [128, 128])
        I['cs64'] = self.inp('cs64', [128, 2, 128], BF16)
        I['fRs'] = self.inp('fRs', [NAs, 2, 2 * NAs], BF16)
        I['fRp'] = self.inp('fRp', [4 * NAs, 2, 8 * NAs], BF16)
        I['fGs'] = self.inp('fGs', [128, NAs, 2, 128], BF16)
        I['fGp'] = self.inp('fGp', [128, 4 * NAs, 2, 32], BF16)
        self.I = I
        yout = nc.dram_tensor('y', [T, D], F32, kind="ExternalOutput").ap()
        S = {}
        S['Zs'] = self.scratch('Zs', [512, TB], BF16)
        S['Zp'] = self.scratch('Zp', [512, 4 * TB], BF16)
        S['yf'] = self.scratch('yf', [T, 256], BF16)
        S['ofw0'] = self.scratch('ofw0', [T, 768], F32)
        S['ofw1'] = self.scratch('ofw1', [T, 512], F32)
        S['hA'] = self.scratch('hA', [T, D], F32)
        S['hB'] = self.scratch('hB', [T, D], F32)
        S['hC'] = self.scratch('hC', [T, D], F32)
        self.SUMW = [2 * (4 * 192 + 4), 2 * (4 * 129 + 4)]
        for L in range(2):
            S['sumin%d' % L] = nc.dram_tensor('sumin%d' % L, [128, self.SUMW[L]], F32).ap()
            S['sumout%d' % L] = nc.dram_tensor('sumout%d' % L, [8 * 128, self.SUMW[L]], F32).ap()
            S['hinf%d' % L] = nc.dram_tensor('hinf%d' % L, [128, 44], F32).ap()
            S['houtf%d' % L] = nc.dram_tensor('houtf%d' % L, [8 * 128, 44], F32).ap()
        S['hins'] = nc.dram_tensor('hins', [128, 8], F32).ap()
        S['houts'] = nc.dram_tensor('houts', [8 * 128, 8], F32).ap()
        self.S = S

        with ExitStack() as stack:
            B = Builder(nc, stack)
            self.B = B
            self.banks = [stack.enter_context(nc.psum_tensor('bank%d' % i, [128, 512], F32)) for i in range(8)]
            self.bank_i = 0
            C = {}
            C['ident'] = stack.enter_context(nc.sbuf_tensor('c_ident', [128, 128], BF16))
            C['coef'] = stack.enter_context(nc.sbuf_tensor('c_coef', [128, 48], F32))
            C['Sin'] = stack.enter_context(nc.sbuf_tensor('c_Sin', [128, 2, 4, 192], F32))
            self.C = C
            for nm in ('ident', 'coef'):
                self.dma('sp', C[nm][:], I[nm], 'c_' + nm, writes=['c_' + nm])

            self.mixer_pass(0, 'f', I['x'], None, pre=True)
            self.mixer_pass(0, 'b', I['x'], None, pre=True)
            self.zpre_pass()
            self.mixer_pass(0, 'f', I['x'], None)
            self.fnet_pass()
            self.mixer_pass(0, 'b', I['x'], S['hA'])
            self.ffn_pass(0, S['hA'], S['hB'])
            self.ple_pass(0, S['hB'], S['hC'], final=False)
            self.mixer_pass(1, 'f', S['hC'], None, pre=True)
            self.mixer_pass(1, 'b', S['hC'], None, pre=True)
            self.mixer_pass(1, 'f', S['hC'], None)
            self.mixer_pass(1, 'b', S['hC'], S['hA'])
            self.ffn_pass(1, S['hA'], S['hB'])
            self.ple_pass(1, S['hB'], yout, final=True)
        return nc

    def nb(self):
        i = self.bank_i
        self.bank_i = (i + 1) % 8
        return i

    def dma(self, eng, dst, src, key, reads=(), writes=(), acc=()):
        return self.B.op(eng, lambda e: e.dma_start(out=dst, in_=src), reads=list(reads), writes=list(writes),
                         acc=list(acc), dma=key)

    def alloc(self, st, name, shape, dt):
        self.uid = getattr(self, 'uid', 0) + 1
        return st.enter_context(self.nc.sbuf_tensor('sb%d_%s' % (self.uid, name), list(shape), dt))

    def load_weight(self, st, name, src, K, N):
        kc = K // 128
        w = self.alloc(st, name, [128, kc, N], BF16)
        srcv = src.rearrange("(k p) n -> p k n", p=128)
        step = max(1, 4096 // N)
        for k0 in range(0, kc, step):
            k1 = min(kc, k0 + step)
            self.dma('pool', w[:, k0:k1, :], srcv[:, k0:k1, :], 'w_' + name, acc=[name])
        return w

    def cload(self, st, name, shape, dt, src, eng='sp'):
        t = self.alloc(st, name, shape, dt)
        self.dma(eng, t[:], src, 'c_' + name, writes=[name])
        return t

    def norm_tile(self, xt, xtk, nrows, nblk, gain, gaink, xn, xnk, sq, rs, jo=0, xkeys=None):
        B = self.B
        B.op('pool', lambda e: e.memset(rs[:, 0:4], 0.0), writes=[(xnk, 'ssq')])
        for j in range(nblk):
            B.op('act', lambda e, j=j: e.activation(out=xn[0:nrows, jo + j, :], in_=xt[0:nrows, j, :], func=AF.Square,
                                                    accum_out=rs[0:nrows, j:j + 1]),
                 reads=[xkeys[j] if xkeys else xtk], acc=[(xnk, 'ssq'), (xnk, jo + j)])
        B.op('act', lambda e: e.activation(out=rs[0:nrows, 4:4 + nblk], in_=rs[0:nrows, 0:nblk], func=AF.Ln,
                                           scale=1.0 / D, bias=self.epsc[0:nrows, 0:1]),
             reads=[(xnk, 'ssq'), 'epsc'], writes=[(xnk, 'rs0')])
        B.op('act', lambda e: e.activation(out=rs[0:nrows, 4:4 + nblk], in_=rs[0:nrows, 4:4 + nblk], func=AF.Exp,
                                           scale=-0.5),
             reads=[(xnk, 'rs0')], writes=[(xnk, 'rs')])
        for j in range(nblk):
            eng = 'dve'
            B.op(eng, lambda e, j=j: e.scalar_tensor_tensor(out=xn[0:nrows, jo + j, :], in0=xt[0:nrows, j, :],
                                                            scalar=rs[0:nrows, 4 + j:5 + j], in1=gain[0:nrows, :],
                                                            op0=ALU.mult, op1=ALU.mult),
                 reads=[xkeys[j] if xkeys else xtk, (xnk, 'rs'), gaink], writes=[(xnk, jo + j)])

    def transpose_tile(self, src, src_keys, nrows, nblk, nk, dstT, dstk, kofs=0, cw=128, col0=0):
        B = self.B
        ident = self.C['ident']
        for k in range(nk):
            bi = self.nb()
            pb = self.banks[bi].bitcast(BF16)

            def f(e, k=k, pb=pb):
                ins = None
                for j in range(nblk):
                    ins = e.transpose(out=pb[0:cw, j * nrows:(j + 1) * nrows],
                                      in_=src[0:nrows, j, col0 + k * cw:col0 + (k + 1) * cw],
                                      identity=ident[0:nrows, 0:nrows])
                return ins
            B.op('pe', f, reads=list(src_keys) + ['c_ident'], writes=[('bank', bi)])
            if k % 2 == 0:
                B.op('act', lambda e, k=k, pb=pb: e.activation(out=dstT[0:cw, kofs + k, 0:nblk * nrows],
                                                               in_=pb[0:cw, 0:nblk * nrows], func=AF.Copy),
                     reads=[('bank', bi)], writes=[(dstk, kofs + k)])
            else:
                B.op('dve', lambda e, k=k, pb=pb: e.tensor_copy(out=dstT[0:cw, kofs + k, 0:nblk * nrows],
                                                                in_=pb[0:cw, 0:nblk * nrows]),
                     reads=[('bank', bi)], writes=[(dstk, kofs + k)])

    def mm_group(self, bi, out_ap, pairs, reads, more_banks=()):
        n = len(pairs)

        def f(e):
            ins = None
            for i, (l, r) in enumerate(pairs):
                ins = e.matmul(out_ap, lhsT=l, rhs=r, start=(i == 0), stop=(i == n - 1))
            return ins
        return self.B.op('pe', f, reads=list(reads), writes=[('bank', bi)] + [('bank', b) for b in more_banks])

    def mm_multi(self, banks_used, groups, reads):
        def f(e):
            ins = None
            for out_ap, pairs in groups:
                n = len(pairs)
                for i, (l, r) in enumerate(pairs):
                    ins = e.matmul(out_ap, lhsT=l, rhs=r, start=(i == 0), stop=(i == n - 1))
            return ins
        return self.B.op('pe', f, reads=list(reads), writes=[('bank', b) for b in banks_used])

    def xk(self, name, n=8):
        return [(name, k) for k in range(n)]

    def ple_alloc(self, st, li):
        I = self.I
        P = {'li': li}
        P['gw'] = self.load_weight(st, 'ple_gw', I['ple_gate_w'][li], D, D)
        P['pw'] = self.load_weight(st, 'ple_pw', I['ple_w'][li], 256, D)
        P['gain'] = self.cload(st, 'ple_gain', [128, D], F32, I['norms'][:, 4 + li, :])
        P['hn'] = self.alloc(st, 'ple_hn', [128, 4, D], BF16)
        P['hnT'] = [self.alloc(st, 'ple_hnT%d' % s, [128, 8, 512], BF16) for s in range(2)]
        P['pb'] = self.alloc(st, 'ple_pb', [128, 4, 256], BF16)
        P['pT'] = [self.alloc(st, 'ple_pT%d' % s, [128, 2, 512], BF16) for s in range(2)]
        P['sg'] = self.alloc(st, 'ple_sg', [128, 2, 512], F32)
        P['rs'] = self.alloc(st, 'ple_rs', [128, 8], F32)
        return P

    def ple_pro(self, P, i, ht, htk, s, part=None):
        I = self.I
        li = P['li']
        if part in (None, 0):
            psrc = I['p'][li, 512 * i:512 * (i + 1), :].rearrange("(j p) c -> p j c", p=128)
            self.dma('pool', P['pb'][:], psrc, 'ple_p', writes=['ple_pb'])
            self.norm_tile(ht, htk, 128, 4, P['gain'], 'ple_gain', P['hn'], 'ple_hn', None, P['rs'])
        if part == 0:
            return
        self.transpose_tile(P['hn'], self.xk('ple_hn', 4), 128, 4, 8, P['hnT'][s], 'ple_hnT%d' % s)
        self.transpose_tile(P['pb'], ['ple_pb'], 128, 4, 2, P['pT'][s], 'ple_pT%d' % s)

    def ple_body(self, P, ht, htk, s, hook=None, hook2=None):
        B = self.B
        banks = self.banks
        hnT, pT = P['hnT'][s], P['pT'][s]
        for j in range(4):
            if hook is not None and j == 0:
                hook()
            if hook2 is not None and j == 3:
                hook2()
            for n in range(2):
                bg = self.nb()
                self.mm_group(bg, banks[bg][:, :],
                              [(hnT[:, k, j * 128:(j + 1) * 128], P['gw'][:, k, n * 512:(n + 1) * 512]) for k in range(8)],
                              reads=self.xk('ple_hnT%d' % s) + ['ple_gw'])
                B.op('act', lambda e, bg=bg, n=n: e.activation(out=P['sg'][:, n, :], in_=banks[bg][:, :], func=AF.Sigmoid),
                     reads=[('bank', bg)], writes=[('ple_sg', n)])
                bp = self.nb()
                self.mm_group(bp, banks[bp][:, :],
                              [(pT[:, k, j * 128:(j + 1) * 128], P['pw'][:, k, n * 512:(n + 1) * 512]) for k in range(2)],
                              reads=self.xk('ple_pT%d' % s, 2) + ['ple_pw'])
                B.op('dve', lambda e, bp=bp, n=n: e.tensor_tensor(out=P['sg'][:, n, :], in0=P['sg'][:, n, :],
                                                                  in1=banks[bp][:, :], op=ALU.mult),
                     reads=[('bank', bp), ('ple_sg', n)], writes=[('ple_sg', n)])
                B.op('dve', lambda e, j=j, n=n, ht=ht: e.tensor_tensor(out=ht[:, j, n * 512:(n + 1) * 512],
                                                                       in0=ht[:, j, n * 512:(n + 1) * 512],
                                                                       in1=P['sg'][:, n, :], op=ALU.add),
                     reads=[('ple_sg', n), htk], acc=[htk])

    def mixer_pass(self, L, dirn, src, dst, pre=False):
        B, I, S, C, NT, banks = self.B, self.I, self.S, self.C, self.NT, self.banks
        SEG = self.SEG
        ple_out = None
        fwd = dirn == 'f'
        even = (L == 0)
        H = 4
        if even:
            DK, DVP, DV, QW, VW = 96, 192, 192, 384, 768
            qo, ko, vo, ro, YO, WN = 256, 640, 1024, 1792, 256, 2592
        else:
            DK, DVP, DV, QW, VW = 128, 129, 128, 512, 512
            qo, ko, vo, ro, YO, WN = 1536, 2048, 2560, 3072, 512, 3600
        LW = QW if even else 4
        qscale = float(DK) ** -0.5
        ofw = S['ofw0'] if even else S['ofw1']
        tI, tS = ((0, 1) if fwd else (2, 3))
        if not even:
            tI, tS = tI + 4, tS + 4
        with ExitStack() as st:
            A = lambda n, sh, dt: self.alloc(st, n, sh, dt)
            Win = self.load_weight(st, 'Win', I['e_w_in'] if even else I['o_w_in'], D, WN)
            gain = self.cload(st, 'gain', [128, D], F32, I['norms'][:, L, :])
            self.epsc = A('epsc', [128, 1], F32)
            B.op('pool', lambda e: e.memset(self.epsc[:, :], EPS), writes=['epsc'])
            xts = [A('xt%d' % s_, [128, 4, D], F32) for s_ in range(2)]
            xt = xts[0]
            sq = None
            rs = A('rs', [128, 8], F32)
            xn = A('xn', [128, 4, D], BF16)
            xnTs = [A('xnT%d' % s_, [128, 8, 512], BF16) for s_ in range(2)]
            xnT = xnTs[0]
            tri = self.cload(st, 'tri', [128, 8, 128], F32, I['tri'])
            amask = self.cload(st, 'amask', [128, 512], BF16, I['amask'][:, 0 if fwd else 1, :])
            negcol = self.cload(st, 'negcol', [128, 2], F32, I['negcol'])
            negones = self.cload(st, 'negones', [128, 128], F32, I['negones'])
            qtm = A('qtm', [128, QW], F32)
            ktm = A('ktm', [128, QW], F32)
            vb = A('vb', [128, H, DVP], BF16)
            lap = A('lap', [128, LW], F32)
            E = A('E', [128, 3, LW], F32)
            qin = A('qin', [128, QW], BF16)
            kin = A('kin', [128, QW], BF16)
            kout = A('kout', [128, QW], BF16)
            qinT = A('qinT', [128, 1, 512], BF16)
            kinT = A('kinT', [128, 1, 512], BF16)
            attm = A('attm', [128, 512], BF16)
            dec = A('dec', [128, H], F32)
            S32 = A('S32', [128, H, DVP], F32)
            Sbf = A('Sbf', [128, H, DVP], BF16)
            ost = A('ost', [128, 4, VW], F32)
            if even:
                gTa = A('gTa', [17, 512], BF16)
                w2 = A('w2', [17, 384], BF16)
                self.dma('pool', w2[:, :], I['w2aug'][0 if fwd else 1], 'c_w2', writes=['w2'])
                B.op('pool', lambda e: e.memset(gTa[:, :], 1.0), writes=['gTa'])
                if fwd and not pre:
                    uT = A('uT', [128, 2, 512], BF16)
                    zst = A('zst', [128, 4, 512], BF16)
                    cs64 = self.cload(st, 'cs64', [128, 2, 128], BF16, I['cs64'])
            else:
                gts = A('gts', [128, 16], F32)
                gbias = self.cload(st, 'gbias', [128, 16], F32, I['gate_bias'])
                smal = A('smal', [128, 16], F32)
                B.op('pool', lambda e: e.memset(vb[:, :, :], 1.0), writes=['vb'])
            if not even:
                rec = A('rec', [128, 8], F32)
                otmp = A('otmp', [128, VW], F32)
            if (not fwd) and (not pre):
                Wout = self.load_weight(st, 'Wout', I['e_w_out'] if even else I['o_w_out'], D, D)
                rg = A('rg', [128, VW], F32)
                ybf = A('ybf', [128, 4, D], BF16)
                yT = A('yT', [128, 8, 512], BF16)
                gmix = self.cload(st, 'gmix', [128, VW], F32, I['gla_norm'] if even else I['mlstm_norm'])
                osum = A('osum', [128, VW], F32)
                ytmp = A('ytmp', [128, VW], F32)
                hrs = A('hrs', [128, 8], F32)
                if even:
                    yfa = A('yfa', [128, 4, 256], BF16)
                    yfb = A('yfb', [128, 4, 256], BF16)
                    yft = A('yft', [128, 4, 256], F32)
                else:
                    PB = A('PB', [128, 4, 514], F32)
                    sxl = A('sxl', [128, 2, 4], F32)
                    hx = A('hx', [128, 2, 4], F32)
                    hall = A('hall', [128, 8, 8], F32)
                    scs = A('scs', [128, 512], F32)
                    cacc = A('cacc', [128, 512], F32)
                    oconv = self.cload(st, 'oconv', [128, 4, 4], F32, I['o_conv'])
                    HALO = A('HALO', [128, 4, 2 * NT], F32)
            if pre:
                Dlog = A('Dlog', [128, H], F32)
                B.op('pool', lambda e: e.memset(Dlog[:, :], 0.0), writes=['Dlog'])

            B.op('dve', lambda e: e.memset(S32[:, :, :], 0.0), writes=['S32'])
            B.op('pool', lambda e: e.memset(Sbf[:, :, :], 0.0), writes=['Sbf'])
            Sin = C['Sin']
            if (not pre) and fwd:
                SW = self.SUMW[L]
                HWc = 4 * DVP + 4
                sumin, sumout = S['sumin%d' % L], S['sumout%d' % L]
                B.op('pool', lambda e: e.collective_compute("AllGather", ALU.bypass, replica_groups=[list(range(8))],
                                                            ins=[sumin.opt()], outs=[sumout.opt()]),
                     reads=[], writes=['sumout'], dma='cc', dinc=1)
                SUM = A('SUM', [128, 2, SW], F32)
                ctmp = A('ctmp', [128, 4, DVP], F32)

            def do_combine():
                B.op('dve', lambda e: e.memset(Sin[:, :, :, :], 0.0), writes=['Sin'])
                nld = [0]
                for di in range(2):
                    Sv = Sin[:, di, :, 0:DVP]
                    ranks = range(8) if di == 0 else range(7, -1, -1)
                    for r in ranks:
                        off = di * HWc
                        ss = nld[0] % 2
                        nld[0] += 1
                        self.dma('sp', SUM[:, ss, :], sumout[r * 128:(r + 1) * 128, :], 'ld_sum%d' % ss, reads=['sumout'], writes=[('SUM', ss)])
                        for h in range(H):
                            B.op('dve', lambda e, ss=ss, h=h, off=off, Sv=Sv: e.scalar_tensor_tensor(
                                out=ctmp[:, h, :], in0=Sv[:, h, :], scalar=SUM[:, ss, off + 4 * DVP + h:off + 4 * DVP + h + 1],
                                in1=SUM[:, ss, off + h * DVP:off + (h + 1) * DVP], op0=ALU.mult, op1=ALU.add),
                                 reads=['Sin', ('SUM', ss)], acc=['ctmp'] if h else [], writes=[] if h else ['ctmp'])
                        ucol = C['coef'][:, 16 * di + r:16 * di + r + 1]
                        ncol = C['coef'][:, 16 * di + 8 + r:16 * di + 8 + r + 1]
                        B.op('dve', lambda e, Sv=Sv, ncol=ncol: e.tensor_scalar(out=Sv, in0=Sv, scalar1=ncol, scalar2=None, op0=ALU.mult),
                             reads=['Sin', 'ctmp', 'c_coef'], writes=['Sin'])
                        B.op('dve', lambda e, Sv=Sv, ucol=ucol: e.scalar_tensor_tensor(out=Sv, in0=ctmp[:, :, :], scalar=ucol, in1=Sv,
                                                                                      op0=ALU.mult, op1=ALU.add),
                             reads=['Sin', 'ctmp', 'c_coef'], writes=['Sin'])

            if (not even) and (not fwd) and (not pre):
                srcv = src.rearrange("(i t) d -> i t d", t=512)
                self.dma('sp', xt[0:NT, 0, :], srcv[:, 0, :], 'ld_x', writes=['xt0'])
                self.dma('sp', xt[NT:2 * NT, 0, :], srcv[:, 511, :], 'ld_x', acc=['xt0'])
                self.norm_tile(xt, 'xt0', 2 * NT, 1, gain, 'gain', xn, 'xn', sq, rs)
                self.transpose_tile(xn, [('xn', 0)], 2 * NT, 1, 8, xnT, 'xnT0')
                for c in range(4):
                    b1 = self.nb()
                    self.mm_group(b1, banks[b1][:, 0:2 * NT],
                                  [(Win[:, k, 512 + c * 128:512 + (c + 1) * 128], xnT[:, k, 0:2 * NT]) for k in range(8)],
                                  reads=self.xk('xnT0') + ['Win'])
                    B.op('act', lambda e, b1=b1: e.activation(out=scs[:, 0:2 * NT], in_=banks[b1][:, 0:2 * NT], func=AF.Copy),
                         reads=[('bank', b1)], writes=['scs'])
                    b2 = self.nb()
                    self.mm_group(b2, banks[b2][:, 0:2 * NT],
                                  [(Win[:, k, 1024 + c * 128:1024 + (c + 1) * 128], xnT[:, k, 0:2 * NT]) for k in range(8)],
                                  reads=self.xk('xnT0') + ['Win'])
                    B.op('dve', lambda e, b2=b2, c=c: e.tensor_tensor(out=HALO[:, c, :], in0=scs[:, 0:2 * NT],
                                                                      in1=banks[b2][:, 0:2 * NT], op=ALU.mult),
                         reads=[('bank', b2), 'scs'], acc=['HALO'])
                B.op('pool', lambda e: e.tensor_copy(out=hx[:, 0, :], in_=HALO[:, :, SEG]), reads=['HALO'], writes=[('hx', 0)])
                B.op('pool', lambda e: e.tensor_copy(out=hx[:, 1, :], in_=HALO[:, :, 2 * NT - 1]), reads=['HALO'], writes=[('hx', 1)])
                self.dma('pool', S['hins'], hx[:, :, :].rearrange("p a c -> p (a c)"), 'st_hx', reads=[('hx', 0), ('hx', 1)], writes=['hins'])
                B.op('pool', lambda e: e.collective_compute("AllGather", ALU.bypass, replica_groups=[list(range(8))],
                                                            ins=[S['hins'].opt()], outs=[S['houts'].opt()]),
                     reads=['hins'], writes=['houts'], dma='cc', dinc=1)
                self.dma('sp', hall[:, :, :], S['houts'].rearrange("(r p) w -> p r w", p=128), 'ld_hall', reads=['houts'], writes=['hall'])
                for sd in range(2):
                    for r in range(8):
                        scol = C['coef'][:, 32 + 8 * sd + r:32 + 8 * sd + r + 1]
                        srcp = hall[:, r, 4:8] if sd == 0 else hall[:, r, 0:4]
                        if r == 0:
                            B.op('dve', lambda e, sd=sd, scol=scol, srcp=srcp: e.tensor_scalar(out=sxl[:, sd, :], in0=srcp, scalar1=scol,
                                                                                              scalar2=None, op0=ALU.mult),
                                 reads=['hall', 'c_coef'], writes=[('sxl', sd)])
                        else:
                            B.op('dve', lambda e, sd=sd, scol=scol, srcp=srcp: e.scalar_tensor_tensor(out=sxl[:, sd, :], in0=srcp, scalar=scol,
                                                                                                     in1=sxl[:, sd, :], op0=ALU.mult, op1=ALU.add),
                                 reads=['hall', 'c_coef', ('sxl', sd)], writes=[('sxl', sd)])

            tiles = list(range(SEG, NT)) if pre else list(range(NT))
            order = tiles if fwd else tiles[::-1]
            def pro_a(i, xt, xtk, s_):
                tsl = slice(512 * i, 512 * (i + 1))
                self.dma('sp', xt[:, :, :], src[tsl, :].rearrange("(j p) d -> p j d", p=128), 'ld_x%d' % s_, writes=[xtk])
                self.norm_tile(xt, xtk, 128, 4, gain, 'gain', xn, 'xn', sq, rs)

            def pro_b(i, xnT, xnTk):
                self.transpose_tile(xn, self.xk('xn', 4), 128, 4, 8, xnT, xnTk)

            def tile_body(i, xt, xtk, xnT, xnTk, hooks):
                tsl = slice(512 * i, 512 * (i + 1))
                xr = self.xk(xnTk) + ['Win']
                if (not fwd) and (not pre):
                    self.dma('sp', ost[:, :, :], ofw[tsl, :].rearrange("(j p) d -> p j d", p=128), 'ld_o', writes=['ost'])
                    if even:
                        self.dma('sp', yfa[:, :, :], S['yf'][tsl, :].rearrange("(j p) d -> p j d", p=128), 'ld_yfa',
                                 writes=['yfa'])
                if even:
                    go = 2560 if fwd else 2576
                    bi = self.nb()
                    self.mm_group(bi, banks[bi][0:16, :], [(Win[:, k, go:go + 16], xnT[:, k, :]) for k in range(8)], reads=xr)
                    B.op('act', lambda e, bi=bi: e.activation(out=gTa[0:16, :], in_=banks[bi][0:16, :], func=AF.Copy),
                         reads=[('bank', bi)], acc=['gTa'])
                    if fwd and (not pre) and i < SEG:
                        for m in range(2):
                            bi = self.nb()
                            self.mm_group(bi, banks[bi][:, :],
                                          [(Win[:, k, m * 128:(m + 1) * 128], xnT[:, k, :]) for k in range(8)], reads=xr)
                            B.op('dve', lambda e, bi=bi, m=m: e.tensor_copy(out=uT[:, m, :], in_=banks[bi][:, :]),
                                 reads=[('bank', bi)], writes=[('uT', m)])
                        for m in range(2):
                            for t in range(2):
                                bi = self.nb()
                                self.mm_group(bi, banks[bi][:, :], [(cs64[:, t, :], uT[:, m, :])], reads=[('uT', m), 'cs64'])
                                B.op('act', lambda e, bi=bi, m=m, t=t: e.activation(out=zst[:, t * 2 + m, :],
                                                                                    in_=banks[bi][:, :], func=AF.Copy),
                                     reads=[('bank', bi)], acc=['zst'])
                        self.dma('pool', S['Zs'].rearrange("(r p) t -> p r t", p=128)[:, :, tsl], zst[:, :, :], 'st_z',
                                 reads=['zst'])
                inj = None
                if not pre:
                    if fwd and i == SEG:
                        inj = 0
                    elif (not fwd) and i == NT - 1:
                        inj = 1
                    elif (not fwd) and i == SEG - 1:
                        inj = 'zero'
                if inj == 'zero':
                    B.op('dve', lambda e: e.memset(S32[:, :, :], 0.0), reads=[], writes=['S32'])
                    B.op('pool', lambda e: e.memset(Sbf[:, :, :], 0.0), reads=[], writes=['Sbf'])
                elif inj is not None:
                    B.op('dve', lambda e, inj=inj: e.tensor_copy(out=S32[:, :, :], in_=Sin[:, inj, :, 0:DVP]), reads=['Sin'], writes=['S32'])
                    B.op('act', lambda e: e.activation(out=Sbf[:, :, :], in_=S32[:, :, :], func=AF.Copy), reads=['S32'], writes=['Sbf'])
                if (not even) and (not fwd) and (not pre):
                    if i == 0:
                        B.op('pool', lambda e: e.memset(PB[:, :, 0:1], 0.0), writes=[('PB', 'l')])
                    elif i == SEG:
                        B.op('pool', lambda e: e.tensor_copy(out=PB[:, :, 0], in_=sxl[:, 0, :]), reads=[('sxl', 0)], writes=[('PB', 'l')])
                    else:
                        B.op('pool', lambda e, i=i: e.tensor_copy(out=PB[:, :, 0], in_=HALO[:, :, NT + i - 1]), reads=['HALO'], writes=[('PB', 'l')])
                    if i == SEG - 1:
                        B.op('pool', lambda e: e.memset(PB[:, :, 513:514], 0.0), writes=[('PB', 'r')])
                    elif i == NT - 1:
                        B.op('pool', lambda e: e.tensor_copy(out=PB[:, :, 513], in_=sxl[:, 1, :]), reads=[('sxl', 1)], writes=[('PB', 'r')])
                    else:
                        B.op('pool', lambda e, i=i: e.tensor_copy(out=PB[:, :, 513], in_=HALO[:, :, i + 1]), reads=['HALO'], writes=[('PB', 'r')])
                    for c in range(4):
                        b1 = self.nb()
                        self.mm_group(b1, banks[b1][:, :],
                                      [(Win[:, k, 512 + c * 128:512 + (c + 1) * 128], xnT[:, k, :]) for k in range(8)], reads=xr)
                        B.op('act', lambda e, b1=b1: e.activation(out=scs[:, :], in_=banks[b1][:, :], func=AF.Copy),
                             reads=[('bank', b1)], writes=['scs'])
                        b2 = self.nb()
                        self.mm_group(b2, banks[b2][:, :],
                                      [(Win[:, k, 1024 + c * 128:1024 + (c + 1) * 128], xnT[:, k, :]) for k in range(8)], reads=xr)
                        B.op('dve', lambda e, b2=b2, c=c: e.tensor_tensor(out=PB[:, c, 1:513], in0=scs[:, :],
                                                                          in1=banks[b2][:, :], op=ALU.mult),
                             reads=[('bank', b2), 'scs'], writes=[('PB', c)])
                        B.op('dve', lambda e, c=c: e.tensor_scalar(out=cacc[:, :], in0=PB[:, c, 1:513],
                                                                   scalar1=oconv[:, c, 1:2], scalar2=oconv[:, c, 3:4],
                                                                   op0=ALU.mult, op1=ALU.add),
                             reads=[('PB', c), 'oconv'], writes=['cacc'])
                        B.op('dve', lambda e, c=c: e.scalar_tensor_tensor(out=cacc[:, :], in0=PB[:, c, 0:512],
                                                                           scalar=oconv[:, c, 0:1], in1=cacc[:, :],
                                                                           op0=ALU.mult, op1=ALU.add),
                             reads=[('PB', c), ('PB', 'l'), 'cacc', 'oconv'], writes=['cacc'])
                        B.op('dve', lambda e, c=c: e.scalar_tensor_tensor(out=cacc[:, :], in0=PB[:, c, 2:514],
                                                                          scalar=oconv[:, c, 2:3], in1=cacc[:, :],
                                                                          op0=ALU.mult, op1=ALU.add),
                             reads=[('PB', c), ('PB', 'r'), 'cacc', 'oconv'], writes=['cacc'])
                        b3 = self.nb()
                        self.mm_group(b3, banks[b3][:, :],
                                      [(Win[:, k, c * 128:(c + 1) * 128], xnT[:, k, :]) for k in range(8)], reads=xr)
                        B.op('dve', lambda e, b3=b3, c=c: e.tensor_tensor(out=yT[:, c, :], in0=cacc[:, :],
                                                                          in1=banks[b3][:, :], op=ALU.mult),
                             reads=[('bank', b3), 'cacc'], writes=[('yT', c)])

                for jn, j in enumerate(range(4) if fwd else range(3, -1, -1)):
                    cs = slice(j * 128, (j + 1) * 128)
                    if hooks and jn == 1:
                        hooks[0]()
                    if hooks and jn == 3:
                        hooks[1]()
                    if not pre:
                        bq = self.nb()
                        self.mm_group(bq, banks[bq][:, 0:QW], [(xnT[:, k, cs], Win[:, k, qo:qo + QW]) for k in range(8)], reads=xr)
                        B.op('act', lambda e, bq=bq: e.activation(out=qtm[:, :], in_=banks[bq][:, 0:QW], func=AF.Copy),
                             reads=[('bank', bq)], writes=['qtm'])
                    bk = self.nb()
                    self.mm_group(bk, banks[bk][:, 0:QW], [(xnT[:, k, cs], Win[:, k, ko:ko + QW]) for k in range(8)], reads=xr)
                    B.op('dve', lambda e, bk=bk: e.tensor_copy(out=ktm[:, :], in_=banks[bk][:, 0:QW]),
                         reads=[('bank', bk)], writes=['ktm'])
                    if even:
                        for g2 in range(2):
                            bv = self.nb()
                            self.mm_group(bv, banks[bv][:, 0:384],
                                          [(xnT[:, k, cs], Win[:, k, vo + g2 * 384:vo + (g2 + 1) * 384]) for k in range(8)], reads=xr)
                            B.op('act' if g2 == 0 else 'dve',
                                 (lambda e, bv=bv, g2=g2: e.activation(out=vb[:, 2 * g2:2 * g2 + 2, :],
                                                                       in_=banks[bv][:, 0:384].rearrange("p (h d) -> p h d", h=2),
                                                                       func=AF.Copy)) if g2 == 0 else
                                 (lambda e, bv=bv, g2=g2: e.tensor_copy(out=vb[:, 2 * g2:2 * g2 + 2, :],
                                                                        in_=banks[bv][:, 0:384].rearrange("p (h d) -> p h d", h=2))),
                                 reads=[('bank', bv)], acc=['vb'])
                        if (not fwd) and (not pre):
                            for g2 in range(2):
                                br = self.nb()
                                self.mm_group(br, banks[br][:, 0:384],
                                              [(xnT[:, k, cs], Win[:, k, ro + g2 * 384:ro + (g2 + 1) * 384]) for k in range(8)], reads=xr)
                                B.op('act', lambda e, br=br, g2=g2: e.activation(out=rg[:, g2 * 384:(g2 + 1) * 384],
                                                                                 in_=banks[br][:, 0:384], func=AF.Silu),
                                     reads=[('bank', br)], acc=['rg'])
                        bz = self.nb()
                        self.mm_group(bz, banks[bz][:, 0:384], [(gTa[0:17, cs], w2[0:17, :])], reads=['gTa', 'w2'])
                        B.op('act', lambda e, bz=bz: e.activation(out=E[:, 0, :], in_=banks[bz][:, 0:384], func=AF.Exp, scale=-1.0),
                             reads=[('bank', bz)], writes=[('E', 0)])
                        B.op('act', lambda e: e.activation(out=lap[:, :], in_=E[:, 0, :], func=AF.Ln, bias=1.0),
                             reads=[('E', 0)], writes=['lap'])
                    else:
                        bv = self.nb()
                        self.mm_group(bv, banks[bv][:, :], [(xnT[:, k, cs], Win[:, k, vo:vo + 512]) for k in range(8)], reads=xr)
                        B.op('act', lambda e, bv=bv: e.activation(out=vb[:, :, 0:128],
                                                                  in_=banks[bv][:, :].rearrange("p (h d) -> p h d", h=4),
                                                                  func=AF.Copy),
                             reads=[('bank', bv)], acc=['vb'])
                        if (not fwd) and (not pre):
                            br = self.nb()
                            self.mm_group(br, banks[br][:, :], [(xnT[:, k, cs], Win[:, k, ro:ro + 512]) for k in range(8)], reads=xr)
                            B.op('act', lambda e, br=br: e.activation(out=rg[:, :], in_=banks[br][:, :], func=AF.Sigmoid),
                                 reads=[('bank', br)], writes=['rg'])
                        bz = self.nb()
                        self.mm_group(bz, banks[bz][:, 0:16], [(xnT[:, k, cs], Win[:, k, 3584:3600]) for k in range(8)], reads=xr)
                        B.op('dve', lambda e, bz=bz: e.tensor_tensor(out=gts[:, :], in0=banks[bz][:, 0:16], in1=gbias[:, :], op=ALU.add),
                             reads=[('bank', bz), 'gbias'], writes=['gts'])
                        io, fo = (0, 4) if fwd else (8, 12)
                        B.op('act', lambda e, fo=fo: e.activation(out=smal[:, 0:4], in_=gts[:, fo:fo + 4], func=AF.Exp, scale=-1.0),
                             reads=['gts'], writes=['smal'])
                        B.op('act', lambda e: e.activation(out=lap[:, :], in_=smal[:, 0:4], func=AF.Ln, bias=1.0),
                             reads=['smal'], writes=['lap'])
                    bc = self.nb()
                    if not pre:
                        self.mm_group(bc, banks[bc][:, 0:LW], [(tri[:, tI, :], lap[:, :])], reads=['tri', 'lap'])
                    bt = self.nb()
                    self.mm_group(bt, banks[bt][:, 0:LW], [(tri[:, tS, :], lap[:, :])], reads=['tri', 'lap'])
                    bd = self.nb()
                    if even:
                        self.mm_multi([bd], [(banks[bd][0:DK, h:h + 1], [(lap[:, h * DK:(h + 1) * DK], negcol[:, 0:1])]) for h in range(H)],
                                      reads=['lap', 'negcol'])
                        if not pre:
                            B.op('act', lambda e, bc=bc: e.activation(out=E[:, 0, :], in_=banks[bc][:, 0:LW], func=AF.Exp),
                                 reads=[('bank', bc)], writes=[('E', 0)])
                            B.op('act', lambda e, bc=bc: e.activation(out=E[:, 1, :], in_=banks[bc][:, 0:LW], func=AF.Exp, scale=-1.0),
                                 reads=[('bank', bc)], writes=[('E', 1)])
                        B.op('act', lambda e, bt=bt: e.activation(out=E[:, 2, :], in_=banks[bt][:, 0:LW], func=AF.Exp),
                             reads=[('bank', bt)], writes=[('E', 2)])
                        B.op('act', lambda e, bd=bd: e.activation(out=dec[0:DK, :], in_=banks[bd][0:DK, 0:H], func=AF.Exp),
                             reads=[('bank', bd)], writes=['dec'])
                        if not pre:
                            B.op('dve', lambda e: e.scalar_tensor_tensor(out=qin[:, :], in0=qtm[:, :], scalar=qscale, in1=E[:, 0, :],
                                                                         op0=ALU.mult, op1=ALU.mult),
                                 reads=['qtm', ('E', 0)], writes=['qin'])
                            B.op('dve', lambda e: e.tensor_tensor(out=kin[:, :], in0=ktm[:, :], in1=E[:, 1, :], op=ALU.mult),
                                 reads=['ktm', ('E', 1)], writes=['kin'])
                        B.op('dve', lambda e: e.tensor_tensor(out=kout[:, :], in0=ktm[:, :], in1=E[:, 2, :], op=ALU.mult),
                             reads=['ktm', ('E', 2)], writes=['kout'])
                    else:
                        self.mm_group(bd, banks[bd][:, 0:4], [(negones[:, :], lap[:, :])], reads=['negones', 'lap'])
                        if not pre:
                            B.op('dve', lambda e, bc=bc, io=io: e.tensor_tensor(out=E[:, 1, :], in0=gts[:, io:io + 4], in1=banks[bc][:, 0:4],
                                                                                op=ALU.subtract),
                                 reads=[('bank', bc), 'gts'], writes=[('E', 1)])
                        B.op('dve', lambda e, bt=bt, io=io: e.tensor_tensor(out=E[:, 2, :], in0=gts[:, io:io + 4], in1=banks[bt][:, 0:4],
                                                                            op=ALU.add),
                             reads=[('bank', bt), 'gts'], writes=[('E', 2)])
                        if not pre:
                            B.op('act', lambda e, bc=bc: e.activation(out=E[:, 0, :], in_=banks[bc][:, 0:4], func=AF.Exp),
                                 reads=[('bank', bc)], writes=[('E', 0)])
                            B.op('act', lambda e: e.activation(out=E[:, 1, :], in_=E[:, 1, :], func=AF.Exp),
                                 reads=[('E', 1)], writes=[('E', 1)])
                        B.op('act', lambda e: e.activation(out=E[:, 2, :], in_=E[:, 2, :], func=AF.Exp),
                             reads=[('E', 2)], writes=[('E', 2)])
                        B.op('act', lambda e, bd=bd: e.activation(out=dec[:, :], in_=banks[bd][:, 0:4], func=AF.Exp),
                             reads=[('bank', bd)], writes=['dec'])
                        v3 = lambda t: t[:, :].rearrange("p (h d) -> p h d", h=4)
                        bc3 = lambda s_: E[:, s_, :].unsqueeze(2).to_broadcast([128, 4, 128])
                        if not pre:
                            B.op('dve', lambda e: e.scalar_tensor_tensor(out=v3(qin), in0=v3(qtm), scalar=qscale, in1=bc3(0),
                                                                         op0=ALU.mult, op1=ALU.mult),
                                 reads=['qtm', ('E', 0)], writes=['qin'])
                            B.op('dve', lambda e: e.tensor_tensor(out=v3(kin), in0=v3(ktm), in1=bc3(1), op=ALU.mult),
                                 reads=['ktm', ('E', 1)], writes=['kin'])
                        B.op('dve', lambda e: e.tensor_tensor(out=v3(kout), in0=v3(ktm), in1=bc3(2), op=ALU.mult),
                             reads=['ktm', ('E', 2)], writes=['kout'])
                    if pre:
                        B.op('dve', lambda e, bd=bd: e.tensor_tensor(out=Dlog[0:DK, :], in0=Dlog[0:DK, :], in1=banks[bd][0:DK, 0:H], op=ALU.add),
                             reads=[('bank', bd), 'Dlog'], writes=['Dlog'])
                    if not pre:
                        self.transpose_tile(qin[:, :].rearrange("p (h d) -> p h d", h=H), ['qin'], 128, H, 1, qinT, 'qinT', cw=DK)
                        self.transpose_tile(kin[:, :].rearrange("p (h d) -> p h d", h=H), ['kin'], 128, H, 1, kinT, 'kinT', cw=DK)
                        ba = self.nb()
                        self.mm_multi([ba], [(banks[ba][:, h * 128:(h + 1) * 128],
                                              [(kinT[0:DK, 0, h * 128:(h + 1) * 128], qinT[0:DK, 0, h * 128:(h + 1) * 128])]) for h in range(H)],
                                      reads=[('qinT', 0), ('kinT', 0)])
                        B.op('dve', lambda e, ba=ba: e.tensor_tensor(out=attm[:, :], in0=banks[ba][:, :], in1=amask[:, :], op=ALU.mult),
                             reads=[('bank', ba), 'amask'], writes=['attm'])
                        b0 = self.nb()
                        b1 = self.nb()
                        ob = [b0, b1]
                        self.mm_multi(ob, [(banks[ob[h // 2]][:, (h % 2) * DVP:(h % 2 + 1) * DVP],
                                            [(attm[:, h * 128:(h + 1) * 128], vb[:, h, :]),
                                             (qinT[0:DK, 0, h * 128:(h + 1) * 128], Sbf[0:DK, h, :])]) for h in range(H)],
                                      reads=['attm', 'vb', ('qinT', 0), 'Sbf'])
                    c0 = self.nb()
                    c1 = self.nb()
                    cb = [c0, c1]
                    self.mm_multi(cb, [(banks[cb[h // 2]][0:DK, (h % 2) * DVP:(h % 2 + 1) * DVP],
                                        [(kout[:, h * DK:(h + 1) * DK], vb[:, h, :])]) for h in range(H)],
                                  reads=['kout', 'vb'])
                    for h in range(H):
                        B.op('dve', lambda e, h=h, cb=cb: e.scalar_tensor_tensor(
                            out=S32[0:DK, h, :], in0=S32[0:DK, h, :], scalar=dec[0:DK, h:h + 1],
                            in1=banks[cb[h // 2]][0:DK, (h % 2) * DVP:(h % 2 + 1) * DVP], op0=ALU.mult, op1=ALU.add),
                             reads=['S32', 'dec', ('bank', cb[h // 2])], acc=['S32'])
                    B.op('act', lambda e: e.activation(out=Sbf[0:DK, :, :], in_=S32[0:DK, :, :], func=AF.Copy),
                         reads=['S32'], writes=['Sbf'])
                    if not pre:
                        ov = lambda b_: banks[b_][:, 0:2 * DVP].rearrange("p (h d) -> p h d", h=2)
                        if even:
                            if fwd:
                                B.op('act', lambda e, b0=b0, j=j: e.activation(out=ost[:, j, 0:384], in_=banks[b0][:, 0:384], func=AF.Copy),
                                     reads=[('bank', b0)], acc=['ost'])
                                B.op('dve', lambda e, b1=b1, j=j: e.tensor_copy(out=ost[:, j, 384:768], in_=banks[b1][:, 0:384]),
                                     reads=[('bank', b1)], acc=['ost'])
                            else:
                                for g2, bb in enumerate(ob):
                                    B.op('dve', lambda e, bb=bb, g2=g2, j=j: e.tensor_tensor(out=osum[:, g2 * 384:(g2 + 1) * 384],
                                                                                             in0=banks[bb][:, 0:384],
                                                                                             in1=ost[:, j, g2 * 384:(g2 + 1) * 384], op=ALU.add),
                                         reads=[('bank', bb), 'ost'], acc=['osum'])
                        else:
                            for g2, bb in enumerate(ob):
                                B.op('act', lambda e, bb=bb, g2=g2: e.activation(out=rec[:, 2 * g2:2 * g2 + 2], in_=ov(bb)[:, :, 128],
                                                                                 func=AF.Square),
                                     reads=[('bank', bb)], acc=['rec0'])
                            B.op('dve', lambda e: e.tensor_scalar(out=rec[:, 0:4], in0=rec[:, 0:4], scalar1=1.0, scalar2=None, op0=ALU.max),
                                 reads=['rec0'], writes=['rec0'])
                            B.op('act', lambda e: e.activation(out=rec[:, 4:8], in_=rec[:, 0:4], func=AF.Ln), reads=['rec0'], writes=['rec1'])
                            B.op('act', lambda e: e.activation(out=rec[:, 4:8], in_=rec[:, 4:8], func=AF.Exp, scale=-0.5),
                                 reads=['rec1'], writes=['rec'])
                            for g2, bb in enumerate(ob):
                                dst_ap = (ost[:, j, g2 * 256:(g2 + 1) * 256] if fwd else otmp[:, g2 * 256:(g2 + 1) * 256])
                                B.op('dve', lambda e, bb=bb, g2=g2, dst_ap=dst_ap: e.tensor_tensor(
                                    out=dst_ap.rearrange("p (h d) -> p h d", h=2), in0=ov(bb)[:, :, 0:128],
                                    in1=rec[:, 4 + 2 * g2:6 + 2 * g2].unsqueeze(2).to_broadcast([128, 2, 128]), op=ALU.mult),
                                     reads=[('bank', bb), 'rec'], acc=['ost' if fwd else 'otmp'])
                            if not fwd:
                                B.op('dve', lambda e, j=j: e.tensor_tensor(out=osum[:, :], in0=otmp[:, :], in1=ost[:, j, :], op=ALU.add),
                                     reads=['otmp', 'ost'], writes=['osum'])
                        if not fwd:
                            B.op('pool', lambda e: e.memset(hrs[:, 0:4], 0.0), writes=['hrs0'])
                            for h in range(H):
                                B.op('act', lambda e, h=h: e.activation(out=ytmp[:, h * DV:(h + 1) * DV], in_=osum[:, h * DV:(h + 1) * DV],
                                                                        func=AF.Square, accum_out=hrs[:, h:h + 1]),
                                     reads=['osum'], acc=['hrs0', 'ytmp'])
                            B.op('act', lambda e: e.activation(out=hrs[:, 4:8], in_=hrs[:, 0:4], func=AF.Ln, scale=1.0 / DV,
                                                               bias=self.epsc[:, 0:1]),
                                 reads=['hrs0', 'epsc'], writes=['hrs1'])
                            B.op('act', lambda e: e.activation(out=hrs[:, 4:8], in_=hrs[:, 4:8], func=AF.Exp, scale=-0.5),
                                 reads=['hrs1'], writes=['hrs'])
                            B.op('dve', lambda e: e.tensor_tensor(out=ytmp[:, :].rearrange("p (h d) -> p h d", h=H),
                                                                  in0=osum[:, :].rearrange("p (h d) -> p h d", h=H),
                                                                  in1=hrs[:, 4:8].unsqueeze(2).to_broadcast([128, H, DV]), op=ALU.mult),
                                 reads=['osum', 'hrs'], writes=['ytmp'])
                            B.op('dve', lambda e: e.tensor_tensor(out=ytmp[:, :], in0=ytmp[:, :], in1=gmix[:, :], op=ALU.mult),
                                 reads=['ytmp', 'gmix'], writes=['ytmp'])
                            B.op('dve', lambda e, j=j: e.tensor_tensor(out=ybf[:, j, YO:YO + VW], in0=ytmp[:, :], in1=rg[:, :], op=ALU.mult),
                                 reads=['ytmp', 'rg'], writes=[('ybf', j)])
                if pre:
                    pass
                elif fwd:
                    self.dma('pool', ofw[tsl, :].rearrange("(j p) d -> p j d", p=128), ost[:, :, :], 'st_o', reads=['ost'])
                else:
                    if even:
                        B.op('pool', lambda e: e.tensor_copy(out=ybf[:, :, 0:256], in_=yfa[:, :, :]), reads=['yfa'], writes=[('ybf', 'f')])
                        self.transpose_tile(ybf, self.xk('ybf', 4) + [('ybf', 'f')], 128, 4, 8, yT, 'yT')
                    else:
                        self.transpose_tile(ybf, self.xk('ybf', 4), 128, 4, 4, yT, 'yT', kofs=4, col0=512)
                    for j in range(4):
                        for n in range(2):
                            bi = self.nb()
                            self.mm_group(bi, banks[bi][:, :],
                                          [(yT[:, k, j * 128:(j + 1) * 128], Wout[:, k, n * 512:(n + 1) * 512]) for k in range(8)],
                                          reads=self.xk('yT') + ['Wout'])
                            B.op('dve', lambda e, bi=bi, j=j, n=n: e.tensor_tensor(out=xt[:, j, n * 512:(n + 1) * 512],
                                                                                   in0=xt[:, j, n * 512:(n + 1) * 512],
                                                                                   in1=banks[bi][:, :], op=ALU.add),
                                 reads=[('bank', bi), xtk], acc=[xtk])
                    self.dma('pool', dst[tsl, :].rearrange("(j p) d -> p j d", p=128), xt[:, :, :], 'st_h', reads=[xtk])
            pro_a(order[0], xts[0], 'xt0', 0)
            pro_b(order[0], xnTs[0], 'xnT0')
            for n_, i_ in enumerate(order):
                s_ = n_ % 2
                hooks = []
                if (not pre) and fwd and n_ == 2:
                    do_combine()
                if n_ + 1 < len(order):
                    i2, s2 = order[n_ + 1], (n_ + 1) % 2
                    hooks = [(lambda i2=i2, s2=s2: pro_a(i2, xts[s2], 'xt%d' % s2, s2)),
                             (lambda i2=i2, s2=s2: pro_b(i2, xnTs[s2], 'xnT%d' % s2))]
                tile_body(i_, xts[s_], 'xt%d' % s_, xnTs[s_], 'xnT%d' % s_, hooks)
            if pre:
                di = 0 if fwd else 1
                off = di * (4 * DVP + 4)
                Dex = A('Dex', [128, H], F32)
                B.op('act', lambda e: e.activation(out=Dex[:, :], in_=Dlog[:, :], func=AF.Exp), reads=['Dlog'], writes=['Dex'])
                sumin = S['sumin%d' % L]
                self.dma('pool', sumin[:, off:off + 4 * DVP], S32[:, :, :].rearrange("p h d -> p (h d)"), 'st_sum', reads=['S32'])
                self.dma('pool', sumin[:, off + 4 * DVP:off + 4 * DVP + 4], Dex[:, :], 'st_sum', reads=['Dex'])
            B.barrier()
            B.emit()


    def zpre_pass(self):
        B, I, S, C, NT, banks = self.B, self.I, self.S, self.C, self.NT, self.banks
        NTF = 2 * NT
        with ExitStack() as st:
            A = lambda n, sh, dt: self.alloc(st, n, sh, dt)
            Wu = self.alloc(st, 'Wu', [128, 8, 256], BF16)
            self.dma('pool', Wu[:, :, :], I['e_w_in'].rearrange("(k p) n -> p k n", p=128)[:, :, 0:256], 'w_Wu', writes=['Wu'])
            gain = self.cload(st, 'gain', [128, D], F32, I['norms'][:, 0, :])
            cs64 = self.cload(st, 'cs64', [128, 2, 128], BF16, I['cs64'])
            self.epsc = A('epsc', [128, 1], F32)
            B.op('pool', lambda e: e.memset(self.epsc[:, :], EPS), writes=['epsc'])
            rss = [A('rs%d' % s, [128, 8], F32) for s in range(2)]
            xts = [A('xt%d' % s, [128, 4, D], F32) for s in range(2)]
            xns = [A('xn%d' % s, [128, 4, D], BF16) for s in range(2)]
            xnT = A('xnT', [128, 8, 512], BF16)
            uT = A('uT', [128, 2, 512], BF16)
            zsts = [A('zst%d' % s, [128, 4, 512], BF16) for s in range(2)]
            Zv = S['Zp'].rearrange("(r p) t -> p r t", p=128)
            for i in range(NTF):
                s = i % 2
                xt, xn, zst = xts[s], xns[s], zsts[s]
                tsl = slice(512 * i, 512 * (i + 1))
                self.dma('sp', xt[:, :, :], I['xfull'][tsl, :].rearrange("(j p) d -> p j d", p=128), 'ld_x%d' % s, writes=['xt%d' % s])
                self.norm_tile(xt, 'xt%d' % s, 128, 4, gain, 'gain', xn, 'xn%d' % s, None, rss[s])
                self.transpose_tile(xn, self.xk('xn%d' % s, 4), 128, 4, 8, xnT, 'xnT')
                for m in range(2):
                    bi = self.nb()
                    self.mm_group(bi, banks[bi][:, :], [(Wu[:, k, m * 128:(m + 1) * 128], xnT[:, k, :]) for k in range(8)],
                                  reads=self.xk('xnT') + ['Wu'])
                    B.op('dve', lambda e, bi=bi, m=m: e.tensor_copy(out=uT[:, m, :], in_=banks[bi][:, :]),
                         reads=[('bank', bi)], writes=[('uT', m)])
                for m in range(2):
                    for t in range(2):
                        bi = self.nb()
                        self.mm_group(bi, banks[bi][:, :], [(cs64[:, t, :], uT[:, m, :])], reads=[('uT', m), 'cs64'])
                        B.op('act', lambda e, bi=bi, m=m, t=t, zst=zst: e.activation(out=zst[:, t * 2 + m, :], in_=banks[bi][:, :],
                                                                                     func=AF.Copy),
                             reads=[('bank', bi)], acc=['zst%d' % s])
                self.dma('pool', Zv[:, :, tsl], zst[:, :, :], 'st_z%d' % s, reads=['zst%d' % s])
            B.barrier()
            B.emit()

    def fnet_pass(self):
        B, I, S, NT, NAs, banks, TB = self.B, self.I, self.S, self.NT, self.NAs, self.banks, self.TB
        with ExitStack() as st:
            A = lambda n, sh, dt: self.alloc(st, n, sh, dt)
            zt = A('zt', [128, 2, 128, 128], BF16)
            Y = A('Y', [128, 2, 4 * NAs, 128], BF16)
            Ost = A('Ost', [128, 4 * NAs, 128], BF16)
            Rm = A('Rm', [128, 2, 8 * NAs], BF16)
            GC = 8
            Gt = A('Gt', [128, GC, 2, 128], BF16)
            for path in (0, 1):
                if path == 0:
                    K, NS, MD = NAs, NAs, 128
                    Zv = S['Zs'].rearrange("(r c) (a b) -> a r c b", r=2, b=128)
                    self.dma('sp', Rm[0:K, :, 0:2 * NS], I['fRs'], 'c_Rm', writes=['Rm'])
                    G = I['fGs']
                    yfv = S['yf'][0:TB, :].rearrange("(d c) h -> d c h", c=NS)
                else:
                    K, NS, MD = 4 * NAs, 4 * NAs, 32
                    Zv = S['Zp'].rearrange("(r c) (a b) -> a r c b", r=2, b=128)
                    self.dma('sp', Rm[0:K, :, 0:2 * NS], I['fRp'], 'c_Rm', writes=['Rm'])
                    G = I['fGp']
                    yfv = S['yf'][TB:2 * TB, :].rearrange("(d c) h -> d c h", c=NS)
                for half in range(2):
                    for ri in range(2):
                        for cq in range(4):
                            self.dma('sp', zt[0:K, ri, cq * 32:(cq + 1) * 32, :],
                                     Zv[:, ri, half * 128 + cq * 32:half * 128 + (cq + 1) * 32, :], 'ld_z%d' % ri,
                                     writes=[('zt', ri)] if cq == 0 else [], acc=[] if cq == 0 else [('zt', ri)])
                    for c in range(128):
                        bi = self.nb()
                        self.mm_group(bi, banks[bi][:, 0:2 * NS],
                                      [(zt[0:K, 0, c, :], Rm[0:K, 0, 0:2 * NS]), (zt[0:K, 1, c, :], Rm[0:K, 1, 0:2 * NS])],
                                      reads=[('zt', 0), ('zt', 1), 'Rm'])
                        srcv = banks[bi][:, 0:2 * NS].rearrange("p (r s) -> p r s", r=2)
                        if c % 2 == 0:
                            B.op('act', lambda e, c=c, srcv=srcv, NS=NS: e.activation(out=Y[:, :, 0:NS, c], in_=srcv, func=AF.Copy),
                                 reads=[('bank', bi)], acc=['Y'])
                        else:
                            B.op('dve', lambda e, c=c, srcv=srcv, NS=NS: e.tensor_copy(out=Y[:, :, 0:NS, c], in_=srcv),
                                 reads=[('bank', bi)], acc=['Y'])
                    for c0 in range(0, NS, GC):
                        gw = 128 if path == 0 else 32
                        self.dma('sp', Gt[:, 0:GC, :, 0:gw], G[:, c0:c0 + GC, :, :], 'ld_g', writes=['Gt'])
                        for c4 in range(c0, c0 + GC, 4):
                            bi = self.nb()
                            self.mm_multi([bi], [(banks[bi][0:MD, q * 128:(q + 1) * 128],
                                                  [(Gt[:, c4 + q - c0, 0, 0:gw], Y[:, 0, c4 + q, :]),
                                                   (Gt[:, c4 + q - c0, 1, 0:gw], Y[:, 1, c4 + q, :])]) for q in range(4)],
                                          reads=['Gt', 'Y'])
                            if (c4 // 4) % 2 == 0:
                                B.op('act', lambda e, bi=bi, c4=c4, MD=MD: e.activation(
                                    out=Ost[0:MD, c4:c4 + 4, :], in_=banks[bi][0:MD, :].rearrange("p (q h) -> p q h", q=4), func=AF.Copy),
                                     reads=[('bank', bi)], acc=['Ost'])
                            else:
                                B.op('dve', lambda e, bi=bi, c4=c4, MD=MD: e.tensor_copy(
                                    out=Ost[0:MD, c4:c4 + 4, :], in_=banks[bi][0:MD, :].rearrange("p (q h) -> p q h", q=4)),
                                     reads=[('bank', bi)], acc=['Ost'])
                    for cq in range(0, NS, 32):
                        ce = min(NS, cq + 32)
                        self.dma('pool', yfv[:, cq:ce, half * 128:(half + 1) * 128], Ost[0:MD, cq:ce, :], 'st_yf', reads=['Ost'])
                    B.op('pool', lambda e: e.memset(Ost[:, 0:1, 0:1], 0.0), reads=[], writes=['Ost'])
                    B.op('pool', lambda e: e.memset(Y[:, 0:1, 0:1, 0:1], 0.0), reads=[], writes=['Y'])
            B.barrier()
            B.emit()
    def ffn_pass(self, L, src, dst):
        B, I, S, C, NT, banks = self.B, self.I, self.S, self.C, self.NT, self.banks
        SEG = self.SEG
        NF = 22
        with ExitStack() as st:
            A = lambda n, sh, dt: self.alloc(st, n, sh, dt)
            Wup = self.load_weight(st, 'Wup', I['ffn_w_up'][L], D, 5632)
            Wdn = self.load_weight(st, 'Wdn', I['ffn_w_down'][L], 2816, D)
            gain = self.cload(st, 'gain', [128, D], F32, I['norms'][:, 2 + L, :])
            fconv = self.cload(st, 'fconv', [128, NF, 4], F32, I['ffn_conv'][:, L, :, :])
            self.epsc = A('epsc', [128, 1], F32)
            B.op('pool', lambda e: e.memset(self.epsc[:, :], EPS), writes=['epsc'])
            xt = A('xt', [128, 4, D], F32)
            sq = None
            rs = A('rs', [128, 8], F32)
            xn = A('xn', [128, 4, D], BF16)
            xnT = A('xnT', [128, 8, 512], BF16)
            GH = A('GH', [128, NF, 2 * NT], F32)
            Gb = A('Gb', [128, 2, 514], F32)
            acc = A('acc', [128, 1, 512], F32)
            actT = A('actT', [128, NF, 512], BF16)
            hx = A('hx', [128, 2, NF], F32)
            hall = A('hall', [128, 8, 2 * NF], F32)
            fxl = A('fxl', [128, 2, NF], F32)
            srcv = src.rearrange("(i t) d -> i t d", t=512)
            self.dma('sp', xt[0:NT, 0, :], srcv[:, 0, :], 'ld_xb0', writes=[('xb', 0)])
            self.dma('sp', xt[NT:2 * NT, 0, :], srcv[:, 511, :], 'ld_xb0', acc=[('xb', 0)])
            self.norm_tile(xt, ('xb', 0), 2 * NT, 1, gain, 'gain', xn, 'xn', None, rs)
            self.transpose_tile(xn, [('xn', 0)], 2 * NT, 1, 8, xnT, 'xnT')
            for f in range(NF):
                bi = self.nb()
                self.mm_group(bi, banks[bi][:, 0:2 * NT],
                              [(Wup[:, k, f * 128:(f + 1) * 128], xnT[:, k, 0:2 * NT]) for k in range(8)],
                              reads=self.xk('xnT') + ['Wup'])
                B.op('act', lambda e, bi=bi, f=f: e.activation(out=GH[:, f, :], in_=banks[bi][:, 0:2 * NT], func=AF.Copy),
                     reads=[('bank', bi)], acc=['GH'])
            hin, hout = S['hinf%d' % L], S['houtf%d' % L]
            B.op('pool', lambda e: e.tensor_copy(out=hx[:, 0, :], in_=GH[:, :, SEG]), reads=['GH'], writes=[('hx', 0)])
            B.op('pool', lambda e: e.tensor_copy(out=hx[:, 1, :], in_=GH[:, :, 2 * NT - 1]), reads=['GH'], writes=[('hx', 1)])
            self.dma('pool', hin, hx[:, :, :].rearrange("p a c -> p (a c)"), 'st_hx', reads=[('hx', 0), ('hx', 1)], writes=['hin'])
            B.op('pool', lambda e: e.collective_compute("AllGather", ALU.bypass, replica_groups=[list(range(8))],
                                                        ins=[hin.opt()], outs=[hout.opt()]),
                 reads=['hin'], writes=['hout'], dma='cc', dinc=1)
            self.dma('sp', hall[:, :, :], hout.rearrange("(r p) w -> p r w", p=128), 'ld_hall', reads=['hout'], writes=['hall'])
            for sd in range(2):
                for r in range(8):
                    scol = C['coef'][:, 32 + 8 * sd + r:32 + 8 * sd + r + 1]
                    srcp = hall[:, r, NF:2 * NF] if sd == 0 else hall[:, r, 0:NF]
                    if r == 0:
                        B.op('dve', lambda e, sd=sd, scol=scol, srcp=srcp: e.tensor_scalar(out=fxl[:, sd, :], in0=srcp, scalar1=scol,
                                                                                          scalar2=None, op0=ALU.mult),
                             reads=['hall', 'c_coef'], writes=[('fxl', sd)])
                    else:
                        B.op('dve', lambda e, sd=sd, scol=scol, srcp=srcp: e.scalar_tensor_tensor(out=fxl[:, sd, :], in0=srcp, scalar=scol,
                                                                                                 in1=fxl[:, sd, :], op0=ALU.mult, op1=ALU.add),
                             reads=['hall', 'c_coef', ('fxl', sd)], writes=[('fxl', sd)])
            def pro_a(i):
                for half in range(2):
                    keys = []
                    for b in range(2):
                        j = 2 * half + b
                        r0 = 512 * i + 128 * j
                        self.dma('sp', xt[:, b, :], src[r0:r0 + 128, :], 'ld_xb%d' % b, writes=[('xb', b)])
                        keys.append(('xb', b))
                    self.norm_tile(xt, None, 128, 2, gain, 'gain', xn, 'xn', None, rs, jo=2 * half, xkeys=keys)

            def pro_b(i):
                self.transpose_tile(xn, self.xk('xn', 4), 128, 4, 8, xnT, 'xnT')

            pro_a(0)
            pro_b(0)
            for i in range(NT):
                xr = self.xk('xnT') + ['Wup']
                for f in range(NF):
                    if f == 2 and i + 1 < NT:
                        pro_a(i + 1)
                    s = f % 2
                    bg = self.nb()
                    self.mm_group(bg, banks[bg][:, :], [(Wup[:, k, f * 128:(f + 1) * 128], xnT[:, k, :]) for k in range(8)], reads=xr)
                    bv = self.nb()
                    self.mm_group(bv, banks[bv][:, :],
                                  [(Wup[:, k, 2816 + f * 128:2816 + (f + 1) * 128], xnT[:, k, :]) for k in range(8)], reads=xr)
                    B.op('act', lambda e, bg=bg, s=s: e.activation(out=Gb[:, s, 1:513], in_=banks[bg][:, :], func=AF.Copy),
                         reads=[('bank', bg)], writes=[('Gb', s)])
                    if i == 0:
                        B.op('pool', lambda e, s=s: e.memset(Gb[:, s, 0:1], 0.0), writes=[('Gb', s, 'l')])
                    elif i == SEG:
                        B.op('pool', lambda e, s=s, f=f: e.tensor_copy(out=Gb[:, s, 0:1], in_=fxl[:, 0, f:f + 1]),
                             reads=[('fxl', 0)], writes=[('Gb', s, 'l')])
                    else:
                        B.op('pool', lambda e, s=s, f=f, i=i: e.tensor_copy(out=Gb[:, s, 0:1], in_=GH[:, f, NT + i - 1:NT + i]),
                             reads=['GH'], writes=[('Gb', s, 'l')])
                    if i == SEG - 1:
                        B.op('pool', lambda e, s=s: e.memset(Gb[:, s, 513:514], 0.0), writes=[('Gb', s, 'r')])
                    elif i == NT - 1:
                        B.op('pool', lambda e, s=s, f=f: e.tensor_copy(out=Gb[:, s, 513:514], in_=fxl[:, 1, f:f + 1]),
                             reads=[('fxl', 1)], writes=[('Gb', s, 'r')])
                    else:
                        B.op('pool', lambda e, s=s, f=f, i=i: e.tensor_copy(out=Gb[:, s, 513:514], in_=GH[:, f, i + 1:i + 2]),
                             reads=['GH'], writes=[('Gb', s, 'r')])
                    B.op('dve', lambda e, s=s, f=f: e.tensor_scalar(out=acc[:, 0, :], in0=Gb[:, s, 1:513], scalar1=fconv[:, f, 1:2],
                                                                    scalar2=fconv[:, f, 3:4], op0=ALU.mult, op1=ALU.add),
                         reads=[('Gb', s), 'fconv'], writes=[('acc', 0)])
                    B.op('dve', lambda e, s=s, f=f: e.scalar_tensor_tensor(out=acc[:, 0, :], in0=Gb[:, s, 0:512], scalar=fconv[:, f, 0:1],
                                                                            in1=acc[:, 0, :], op0=ALU.mult, op1=ALU.add),
                         reads=[('Gb', s), ('Gb', s, 'l'), ('acc', 0), 'fconv'], writes=[('acc', 0)])
                    B.op('dve', lambda e, s=s, f=f: e.scalar_tensor_tensor(out=acc[:, 0, :], in0=Gb[:, s, 2:514], scalar=fconv[:, f, 2:3],
                                                                           in1=acc[:, 0, :], op0=ALU.mult, op1=ALU.add),
                         reads=[('Gb', s), ('Gb', s, 'r'), ('acc', 0), 'fconv'], writes=[('acc', 0)])
                    B.op('act', lambda e: e.activation(out=acc[:, 0, :], in_=acc[:, 0, :], func=AF.Silu),
                         reads=[('acc', 0)], writes=[('acc', 0)])
                    B.op('dve', lambda e, f=f, bv=bv: e.tensor_tensor(out=actT[:, f, :], in0=acc[:, 0, :], in1=banks[bv][:, :],
                                                                      op=ALU.mult),
                         reads=[('acc', 0), ('bank', bv)], writes=[('actT', f)])
                if i + 1 < NT:
                    pro_b(i + 1)
                for j in range(4):
                    sl_ = 2 + (j % 2)
                    r0 = 512 * i + 128 * j
                    self.dma('sp', xt[:, sl_, :], src[r0:r0 + 128, :], 'ld_xb%d' % sl_, writes=[('xb', sl_)])
                    for n in range(2):
                        bi = self.nb()
                        self.mm_group(bi, banks[bi][:, :],
                                      [(actT[:, f, j * 128:(j + 1) * 128], Wdn[:, f, n * 512:(n + 1) * 512]) for f in range(NF)],
                                      reads=self.xk('actT', NF) + ['Wdn'])
                        B.op('dve', lambda e, bi=bi, sl_=sl_, n=n: e.tensor_tensor(out=xt[:, sl_, n * 512:(n + 1) * 512],
                                                                                   in0=xt[:, sl_, n * 512:(n + 1) * 512],
                                                                                   in1=banks[bi][:, :], op=ALU.add),
                             reads=[('bank', bi), ('xb', sl_)], acc=[('xb', sl_)])
                    self.dma('pool', dst[r0:r0 + 128, :], xt[:, sl_, :], 'st_h%d' % sl_, reads=[('xb', sl_)])
            B.barrier()
            B.emit()

    def ple_pass(self, li, src, dst, final):
        B, I, NT = self.B, self.I, self.NT
        with ExitStack() as st:
            A = lambda n, sh, dt: self.alloc(st, n, sh, dt)
            self.epsc = A('epsc', [128, 1], F32)
            B.op('pool', lambda e: e.memset(self.epsc[:, :], EPS), writes=['epsc'])
            xts = [A('xt%d' % s, [128, 4, D], F32) for s in range(2)]
            rs = A('rs', [128, 8], F32)
            if final:
                gain = self.cload(st, 'gain', [128, D], F32, I['norms'][:, 6, :])
                yo = A('yo', [128, 4, D], F32)
            P = self.ple_alloc(st, li)

            def pro(i, s, part=None):
                tsl = slice(512 * i, 512 * (i + 1))
                if part in (None, 0):
                    self.dma('sp', xts[s][:, :, :], src[tsl, :].rearrange("(j p) d -> p j d", p=128), 'ld_x%d' % s, writes=['xt%d' % s])
                self.ple_pro(P, i, xts[s], 'xt%d' % s, s, part)

            pro(0, 0)
            for i in range(NT):
                s = i % 2
                xt, xtk = xts[s], 'xt%d' % s
                tsl = slice(512 * i, 512 * (i + 1))
                hook = (lambda i=i, s=s: pro(i + 1, 1 - s, 0)) if i + 1 < NT else None
                hook2 = (lambda i=i, s=s: pro(i + 1, 1 - s, 1)) if i + 1 < NT else None
                self.ple_body(P, xt, xtk, s, hook, hook2)
                if final:
                    self.norm_tile(xt, xtk, 128, 4, gain, 'gain', yo, 'yo', None, rs)
                    self.dma('pool', dst[tsl, :].rearrange("(j p) d -> p j d", p=128), yo[:, :, :], 'st_y', reads=self.xk('yo', 4))
                else:
                    self.dma('pool', dst[tsl, :].rearrange("(j p) d -> p j d", p=128), xt[:, :, :], 'st_y', reads=[xtk])
            B.barrier()
            B.emit()


def _bf(a):
    return np.ascontiguousarray(a).astype(ml_dtypes.bfloat16)


def _dftR(n, nblk=1):
    a = np.arange(n)
    ang = 2 * np.pi * np.outer(a, a) / n
    N = n * nblk
    Cm = np.zeros((N, N))
    Sm = np.zeros((N, N))
    for j in range(nblk):
        Cm[j * n:(j + 1) * n, j * n:(j + 1) * n] = np.cos(ang)
        Sm[j * n:(j + 1) * n, j * n:(j + 1) * n] = np.sin(ang)
    R = np.zeros((N, 2, 2 * N))
    R[:, 0, :N] = Cm
    R[:, 0, N:] = -Sm
    R[:, 1, :N] = -Sm
    R[:, 1, N:] = -Cm
    return _bf(R.astype(np.float32))


def _dftG(Aa, d0, nd):
    Sx = 128 * Aa
    b = np.arange(128, dtype=np.int64)[:, None, None]
    cc = np.arange(Aa, dtype=np.int64)[None, :, None]
    d = (d0 + np.arange(nd, dtype=np.int64))[None, None, :]
    ph = 2 * np.pi * ((b * (cc + Aa * d)) % Sx) / Sx
    nrm = 1.0 / np.sqrt(Sx * 64.0)
    G = np.zeros((128, Aa, 2, nd))
    G[:, :, 0, :] = np.cos(ph) * nrm
    G[:, :, 1, :] = np.sin(ph) * nrm
    return _bf(G.astype(np.float32))


def _consts(NT):
    NAs = 2 * NT
    c = {}
    c['ident'] = _bf(np.eye(128, dtype=np.float32))
    lp = np.arange(128)[:, None]
    l = np.arange(128)[None, :]
    mats = [(lp <= l), (lp > l), (lp >= l), (lp < l)]
    tri = np.zeros((128, 8, 128), np.float32)
    for q, m in enumerate(mats):
        tri[:, q, :] = m.astype(np.float32) * (-1.0 / 16.0)
        tri[:, 4 + q, :] = m.astype(np.float32) * (-1.0)
    c['tri'] = tri
    am = np.zeros((128, 2, 512), np.float32)
    am[:, 0, :] = np.tile((lp <= l).astype(np.float32), (1, 4))
    am[:, 1, :] = np.tile((lp >= l).astype(np.float32), (1, 4))
    c['amask'] = _bf(am)
    nc_ = np.zeros((128, 2), np.float32)
    nc_[:, 0] = -1.0 / 16.0
    nc_[:, 1] = -1.0
    c['negcol'] = nc_
    c['negones'] = -np.ones((128, 128), np.float32)
    ch = np.arange(64)
    ang = 2 * np.pi * np.outer(ch, ch) / 64.0
    cs = np.zeros((128, 2, 128), np.float64)
    for b in range(2):
        cs[b * 64:(b + 1) * 64, 0, b * 64:(b + 1) * 64] = np.cos(ang)
        cs[b * 64:(b + 1) * 64, 1, b * 64:(b + 1) * 64] = np.sin(ang)
    c['cs64'] = _bf(cs.astype(np.float32))
    c['fRs'] = _dftR(NAs)
    c['fRp'] = _dftR(4 * NAs)
    c['fGs'] = _dftG(NAs, 0, 128)
    return c


_NC_CACHE = {}


def _get_nc(NT, debug=False):
    key = (NT, debug)
    if key not in _NC_CACHE:
        g = Gen(NT, debug=debug)
        g.build()
        _NC_CACHE[key] = g
    return _NC_CACHE[key]


def _rep(v):
    v = np.asarray(v, np.float32).reshape(1, -1)
    return np.ascontiguousarray(np.broadcast_to(v, (128, v.shape[1])))


def _shared_inputs(w, NT):
    f32 = lambda a: np.ascontiguousarray(np.asarray(a, np.float32))
    d = dict(_consts(NT))
    d['e_w_in'] = f32(w['e_w_in'][0])
    d['e_w_out'] = f32(w['e_w_out'][0])
    d['o_w_in'] = f32(w['o_w_in'][0])
    d['o_w_out'] = f32(w['o_w_out'][0])
    d['ffn_w_up'] = f32(w['ffn_w_up'])
    d['ffn_w_down'] = f32(w['ffn_w_down'])
    d['ple_w'] = f32(w['ple_w'])
    d['ple_gate_w'] = f32(w['ple_gate_w'])
    norms = np.stack([w['e_norm'][0], w['o_norm'][0], w['ffn_norm'][0], w['ffn_norm'][1],
                      w['ple_gate_norm'][0], w['ple_gate_norm'][1], w['final_norm']], 0).astype(np.float32)
    d['norms'] = np.ascontiguousarray(np.broadcast_to(norms[None], (128, 7, D)))
    d['gla_norm'] = _rep(w['e_gla_norm'][0])
    d['mlstm_norm'] = _rep(w['o_mlstm_norm'][0])
    d['gate_bias'] = _rep(w['o_gate_bias'][0])
    w2 = np.zeros((2, 17, 384), np.float32)
    w2[0, :16] = w['e_gla_w2_f'][0]
    w2[0, 16] = w['e_gla_b_f'][0]
    w2[1, :16] = w['e_gla_w2_b'][0]
    w2[1, 16] = w['e_gla_b_b'][0]
    d['w2aug'] = w2
    oc = np.zeros((128, 4, 4), np.float32)
    cw = np.asarray(w['o_conv_w'][0], np.float32)
    cb = np.asarray(w['o_conv_b'][0], np.float32)
    for t in range(3):
        oc[:, :, t] = cw[t].reshape(4, 128).T
    oc[:, :, 3] = cb.reshape(4, 128).T
    d['o_conv'] = oc
    fc = np.zeros((128, 2, 22, 4), np.float32)
    for L in range(2):
        fw = np.asarray(w['ffn_conv_w'][L], np.float32)
        fb = np.asarray(w['ffn_conv_b'][L], np.float32)
        for t in range(3):
            fc[:, L, :, t] = fw[t].reshape(22, 128).T
        fc[:, L, :, 3] = fb.reshape(22, 128).T
    d['ffn_conv'] = fc
    return d


def run_model(x_prompt, x_sample, p_prompt, p_sample, w, NT, debug=False):
    TB = 256 * NT
    NAs = 2 * NT
    g = _get_nc(NT, debug)
    shared = _shared_inputs(w, NT)
    gp = [_dftG(4 * NAs, 32 * jq, 32) for jq in range(4)]
    in_maps = []
    for c in range(8):
        pb, jq = c // 4, c % 4
        d = dict(shared)
        qs = slice(jq * TB, (jq + 1) * TB)
        d['x'] = np.ascontiguousarray(np.concatenate([x_sample[c], x_prompt[pb, qs]], 0), np.float32)
        d['xfull'] = np.ascontiguousarray(x_prompt[pb], np.float32)
        d['p'] = np.ascontiguousarray(np.concatenate([p_sample[:, c], p_prompt[:, pb, qs]], 1), np.float32)
        coef = np.zeros((48,), np.float32)
        for r in range(8):
            same = (r // 4 == pb)
            uf = 1.0 if (same and (r % 4) < jq) else 0.0
            ub = 1.0 if (same and (r % 4) > jq) else 0.0
            coef[r] = uf
            coef[8 + r] = 1.0 - uf
            coef[16 + r] = ub
            coef[24 + r] = 1.0 - ub
            coef[32 + r] = 1.0 if (same and r == c - 1) else 0.0
            coef[40 + r] = 1.0 if (same and r == c + 1) else 0.0
        d['coef'] = _rep(coef)
        d['fGp'] = gp[jq]
        in_maps.append(d)
    res = run_bass_kernel_spmd(g.nc, in_maps, core_ids=list(range(8)))
    ys = [np.asarray(res.results[c]['y']) for c in range(8)]
    y_sample = np.stack([ys[c][0:TB] for c in range(8)], 0).astype(np.float32)
    y_prompt = np.stack([np.concatenate([ys[4 * b + q][TB:2 * TB] for q in range(4)], 0) for b in range(2)], 0).astype(np.float32)
    return y_prompt, y_sample, res


def kernel(x_prompt, x_sample, p_prompt, p_sample, **w):
    xp = np.asarray(x_prompt, np.float32)
    xs = np.asarray(x_sample, np.float32)
    pp = np.asarray(p_prompt, np.float32)
    ps = np.asarray(p_sample, np.float32)
    y_prompt, y_sample, _ = run_model(xp, xs, pp, ps, w, 16)
    return (y_prompt, y_sample)
```

```python
import numpy as np
import ml_dtypes
from contextlib import ExitStack
import concourse.bass as bass
import concourse.mybir as mybir
from concourse.bass_utils import run_bass_kernel_spmd

F32 = mybir.dt.float32
BF16 = mybir.dt.bfloat16
ALU = mybir.AluOpType
AF = mybir.ActivationFunctionType
AX = mybir.AxisListType

D = 1024
EPS = 1e-6
ENGS = ('pe', 'act', 'dve', 'pool', 'sp')


class Op:
    __slots__ = ('eng', 'fn', 'waits', 'needed', 'sig', 'dkey', 'dval', 'idx', 'dinc')


class Res:
    __slots__ = ('w', 'r')

    def __init__(self):
        self.w = []
        self.r = []


class Builder:
    def __init__(self, nc, stack):
        self.nc = nc
        self.stack = stack
        self.ops = {e: [] for e in ENGS}
        self.nops = {e: 0 for e in ENGS}
        self.sigcnt = {e: 0 for e in ENGS}
        self.known = {e: {} for e in ENGS}
        self.last = {e: None for e in ENGS}
        self.dtot = {}
        self.sems = {}
        self.res = {}
        for e in ENGS:
            self.sems['e_' + e] = stack.enter_context(nc.semaphore('e_' + e))

    def R(self, key):
        r = self.res.get(key)
        if r is None:
            r = self.res[key] = Res()
        return r

    def _dsem(self, key):
        n = 'd_' + key
        if n not in self.sems:
            self.sems[n] = self.stack.enter_context(self.nc.semaphore(n))
        return self.sems[n]

    def op(self, eng, fn, reads=(), writes=(), dma=None, acc=(), dinc=16):
        o = Op()
        o.dinc = dinc
        o.eng = eng
        o.fn = fn
        o.needed = False
        o.sig = None
        o.dkey = dma
        o.dval = None
        self.nops[eng] += 1
        o.idx = self.nops[eng]
        evs = []
        for k in reads:
            evs += [(ev, 0) for ev in self.R(k).w]
            if isinstance(k, tuple) and k[0] == 'bank':
                evs += [(ev, 0) for ev in self.R(k).r if ev[0] == 'e' and ev[1].eng != eng]
        for k in writes:
            r = self.R(k)
            evs += [(ev, 1) for ev in r.w]
            evs += [(ev, 2) for ev in r.r]
        for k in acc:
            r = self.R(k)
            evs += [(ev, 3) for ev in r.w]
            evs += [(ev, 2) for ev in r.r]
        waits = {}
        kn = self.known[eng]
        for ev, kind in evs:
            if ev[0] == 'e':
                p = ev[1]
                if p.eng == eng:
                    if eng == 'pe' or kind >= 2:
                        continue
                k = ('e', p.eng)
                v = p.idx
            else:
                p = None
                k = ('d', ev[1])
                v = ev[2]
            if kn.get(k, 0) >= v:
                continue
            if k not in waits or waits[k][0] < v:
                waits[k] = (v, p)
        for k, (v, p) in waits.items():
            kn[k] = v
            if p is not None:
                p.needed = True
        o.waits = waits
        if dma is not None:
            self._dsem(dma)
            self.dtot[dma] = self.dtot.get(dma, 0) + dinc
            o.dval = self.dtot[dma]
            ev = ('d', dma, o.dval)
        else:
            ev = ('e', o)
        for k in reads:
            self.R(k).r.append(ev)
        for k in writes:
            r = self.R(k)
            r.w = [ev]
            r.r = []
        for k in acc:
            self.R(k).w.append(ev)
        self.ops[eng].append(o)
        if dma is None and fn is not None:
            self.last[eng] = o
        return o

    def fence(self, eng):
        o = Op()
        o.eng = eng
        o.fn = None
        o.needed = False
        o.sig = None
        o.dkey = None
        o.dval = None
        o.dinc = 16
        self.nops[eng] += 1
        o.idx = self.nops[eng]
        waits = {}
        for key, tot in self.dtot.items():
            if self.known[eng].get(('d', key), 0) >= tot:
                continue
            waits[('d', key)] = (tot, None)
            self.known[eng][('d', key)] = tot
        o.waits = waits
        self.ops[eng].append(o)

    def collective(self, fn, reads, writes):
        self.ncc = getattr(self, 'ncc', 0) + 1
        self.fence('pool')
        self.op('pool', fn, reads=reads, writes=writes, dma='cc%d' % self.ncc, dinc=1)
        self.fence('pool')

    def barrier(self):
        lasts = {e: self.last[e] for e in ENGS if self.last[e] is not None}
        for e in ENGS:
            o = Op()
            o.eng = e
            o.fn = None
            o.needed = False
            o.sig = None
            o.dkey = None
            o.dval = None
            o.dinc = 16
            self.nops[e] += 1
            o.idx = self.nops[e]
            waits = {}
            for e2, p in lasts.items():
                if e2 == e:
                    continue
                if self.known[e].get(('e', e2), 0) >= p.idx:
                    continue
                waits[('e', e2)] = (p.idx, p)
                p.needed = True
                self.known[e][('e', e2)] = p.idx
            for key, tot in self.dtot.items():
                if self.known[e].get(('d', key), 0) >= tot:
                    continue
                waits[('d', key)] = (tot, None)
                self.known[e][('d', key)] = tot
            o.waits = waits
            self.ops[e].append(o)
        self.res = {}

    def emit(self):
        for e in ENGS:
            for o in self.ops[e]:
                if o.needed and o.dkey is None and o.fn is not None:
                    self.sigcnt[e] += 1
                    o.sig = self.sigcnt[e]
        sems = self.sems
        with self.nc.Block() as blk:
            decos = {'pe': blk.tensor, 'act': blk.scalar, 'dve': blk.vector, 'pool': blk.gpsimd, 'sp': blk.sync}
            for e in ENGS:
                ops = self.ops[e]

                def body(eng, ops=ops, e=e):
                    for o in ops:
                        for k, (v, p) in o.waits.items():
                            if k[0] == 'e':
                                eng.wait_ge(sems['e_' + k[1]], p.sig)
                            else:
                                eng.wait_ge(sems['d_' + k[1]], v)
                        if o.fn is None:
                            continue
                        ins = o.fn(eng)
                        if o.dkey is not None:
                            ins.then_inc(sems['d_' + o.dkey], o.dinc)
                        elif o.sig is not None:
                            ins.then_inc(sems['e_' + e], 1)

                decos[e](body)
        self.ops = {e: [] for e in ENGS}


class Gen:
    def __init__(self, NT, debug=False):
        self.NT = NT
        self.SEG = NT // 2
        self.T = 512 * NT
        self.TB = 256 * NT
        self.NAs = 2 * NT
        self.debug = debug
        self.nc = bass.Bass("TRN2", target_bir_lowering=False)

    def inp(self, name, shape, dt=F32):
        return self.nc.dram_tensor(name, list(shape), dt, kind="ExternalInput").ap()

    def scratch(self, name, shape, dt):
        kind = "ExternalOutput" if self.debug else "Internal"
        return self.nc.dram_tensor(name, list(shape), dt, kind=kind).ap()

    def build(self):
        nc = self.nc
        NT, T, TB, NAs = self.NT, self.T, self.TB, self.NAs
        I = {}
        I['x'] = self.inp('x', [T, D])
        I['xfull'] = self.inp('xfull', [4 * TB, D])
        I['p'] = self.inp('p', [2, T, 256])
        I['coef'] = self.inp('coef', [128, 48])
        I['e_w_in'] = self.inp('e_w_in', [D, 2592])
        I['e_w_out'] = self.inp('e_w_out', [D, D])
        I['o_w_in'] = self.inp('o_w_in', [D, 3600])
        I['o_w_out'] = self.inp('o_w_out', [D, D])
        I['ffn_w_up'] = self.inp('ffn_w_up', [2, D, 5632])
        I['ffn_w_down'] = self.inp('ffn_w_down', [2, 2816, D])
        I['ple_w'] = self.inp('ple_w', [2, 256, D])
        I['ple_gate_w'] = self.inp('ple_gate_w', [2, D, D])
        I['norms'] = self.inp('norms', [128, 7, D])
        I['gla_norm'] = self.inp('gla_norm', [128, 768])
        I['mlstm_norm'] = self.inp('mlstm_norm', [128, 512])
        I['gate_bias'] = self.inp('gate_bias', [128, 16])
        I['w2aug'] = self.inp('w2aug', [2, 17, 384])
        I['o_conv'] = self.inp('o_conv', [128, 4, 4])
        I['ffn_conv'] = self.inp('ffn_conv', [128, 2, 22, 4])
        I['ident'] = self.inp('ident', [128, 128], BF16)
        I['tri'] = self.inp('tri', [128, 8, 128])
        I['amask'] = self.inp('amask', [128, 2, 512], BF16)
        I['negcol'] = self.inp('negcol', [128, 2])
        I['negones'] = self.inp('negones', [128, 128])
        I['cs64'] = self.inp('cs64', [128, 2, 128], BF16)
        I['fRs'] = self.inp('fRs', [NAs, 2, 2 * NAs], BF16)
        I['fRp'] = self.inp('fRp', [4 * NAs, 2, 8 * NAs], BF16)
        I['fGs'] = self.inp('fGs', [128, NAs, 2, 128], BF16)
        I['fGp'] = self.inp('fGp', [128, 4 * NAs, 2, 32], BF16)
        self.I = I
        yout = nc.dram_tensor('y', [T, D], F32, kind="ExternalOutput").ap()
        S = {}
        S['Zs'] = self.scratch('Zs', [512, TB], BF16)
        S['Zp'] = self.scratch('Zp', [512, 4 * TB], BF16)
        S['yf'] = self.scratch('yf', [T, 256], BF16)
        S['ofw0'] = self.scratch('ofw0', [T, 768], F32)
        S['ofw1'] = self.scratch('ofw1', [T, 512], F32)
        S['hA'] = self.scratch('hA', [T, D], F32)
        S['hB'] = self.scratch('hB', [T, D], F32)
        S['hC'] = self.scratch('hC', [T, D], F32)
        self.SUMW = [2 * (4 * 192 + 4), 2 * (4 * 129 + 4)]
        for L in range(2):
            S['sumin%d' % L] = nc.dram_tensor('sumin%d' % L, [128, self.SUMW[L]], F32).ap()
            S['sumout%d' % L] = nc.dram_tensor('sumout%d' % L, [8 * 128, self.SUMW[L]], F32).ap()
            S['hinf%d' % L] = nc.dram_tensor('hinf%d' % L, [128, 44], F32).ap()
            S['houtf%d' % L] = nc.dram_tensor('houtf%d' % L, [8 * 128, 44], F32).ap()
        S['hins'] = nc.dram_tensor('hins', [128, 8], F32).ap()
        S['houts'] = nc.dram_tensor('houts', [8 * 128, 8], F32).ap()
        self.S = S

        with ExitStack() as stack:
            B = Builder(nc, stack)
            self.B = B
            self.banks = [stack.enter_context(nc.psum_tensor('bank%d' % i, [128, 512], F32)) for i in range(8)]
            self.bank_i = 0
            C = {}
            C['ident'] = stack.enter_context(nc.sbuf_tensor('c_ident', [128, 128], BF16))
            C['coef'] = stack.enter_context(nc.sbuf_tensor('c_coef', [128, 48], F32))
            C['Sin'] = stack.enter_context(nc.sbuf_tensor('c_Sin', [128, 2, 4, 192], F32))
            self.C = C
            for nm in ('ident', 'coef'):
                self.dma('sp', C[nm][:], I[nm], 'c_' + nm, writes=['c_' + nm])

            self.mixer_pass(0, 'f', I['x'], None, pre=True)
            self.mixer_pass(0, 'b', I['x'], None, pre=True)
            self.zpre_pass()
            self.mixer_pass(0, 'f', I['x'], None)
            self.fnet_pass()
            self.mixer_pass(0, 'b', I['x'], S['hA'])
            self.ffn_pass(0, S['hA'], S['hB'])
            self.ple_pass(0, S['hB'], S['hC'], final=False)
            self.mixer_pass(1, 'f', S['hC'], None, pre=True)
            self.mixer_pass(1, 'b', S['hC'], None, pre=True)
            self.mixer_pass(1, 'f', S['hC'], None)
            self.mixer_pass(1, 'b', S['hC'], S['hA'])
            self.ffn_pass(1, S['hA'], S['hB'])
            self.ple_pass(1, S['hB'], yout, final=True)
        return nc

    def nb(self):
        i = self.bank_i
        self.bank_i = (i + 1) % 8
        return i

    def dma(self, eng, dst, src, key, reads=(), writes=(), acc=()):
        return self.B.op(eng, lambda e: e.dma_start(out=dst, in_=src), reads=list(reads), writes=list(writes),
                         acc=list(acc), dma=key)

    def alloc(self, st, name, shape, dt):
        self.uid = getattr(self, 'uid', 0) + 1
        return st.enter_context(self.nc.sbuf_tensor('sb%d_%s' % (self.uid, name), list(shape), dt))

    def load_weight(self, st, name, src, K, N):
        kc = K // 128
        w = self.alloc(st, name, [128, kc, N], BF16)
        srcv = src.rearrange("(k p) n -> p k n", p=128)
        step = max(1, 4096 // N)
        for k0 in range(0, kc, step):
            k1 = min(kc, k0 + step)
            self.dma('pool', w[:, k0:k1, :], srcv[:, k0:k1, :], 'w_' + name, acc=[name])
        return w

    def cload(self, st, name, shape, dt, src, eng='sp'):
        t = self.alloc(st, name, shape, dt)
        self.dma(eng, t[:], src, 'c_' + name, writes=[name])
        return t

    def norm_tile(self, xt, xtk, nrows, nblk, gain, gaink, xn, xnk, sq, rs, jo=0, xkeys=None):
        B = self.B
        B.op('dve', lambda e: e.memset(rs[:, 0:4], 0.0), writes=[(xnk, 'ssq')])
        for j in range(nblk):
            B.op('act', lambda e, j=j: e.activation(out=xn[0:nrows, jo + j, :], in_=xt[0:nrows, j, :], func=AF.Square,
                                                    accum_out=rs[0:nrows, j:j + 1]),
                 reads=[xkeys[j] if xkeys else xtk], acc=[(xnk, 'ssq'), (xnk, jo + j)])
        B.op('act', lambda e: e.activation(out=rs[0:nrows, 4:4 + nblk], in_=rs[0:nrows, 0:nblk], func=AF.Ln,
                                           scale=1.0 / D, bias=self.epsc[0:nrows, 0:1]),
             reads=[(xnk, 'ssq'), 'epsc'], writes=[(xnk, 'rs0')])
        B.op('act', lambda e: e.activation(out=rs[0:nrows, 4:4 + nblk], in_=rs[0:nrows, 4:4 + nblk], func=AF.Exp,
                                           scale=-0.5),
             reads=[(xnk, 'rs0')], writes=[(xnk, 'rs')])
        for j in range(nblk):
            eng = 'dve'
            B.op(eng, lambda e, j=j: e.scalar_tensor_tensor(out=xn[0:nrows, jo + j, :], in0=xt[0:nrows, j, :],
                                                            scalar=rs[0:nrows, 4 + j:5 + j], in1=gain[0:nrows, :],
                                                            op0=ALU.mult, op1=ALU.mult),
                 reads=[xkeys[j] if xkeys else xtk, (xnk, 'rs'), gaink], writes=[(xnk, jo + j)])

    def transpose_tile(self, src, src_keys, nrows, nblk, nk, dstT, dstk, kofs=0, cw=128, col0=0):
        B = self.B
        ident = self.C['ident']
        for k in range(nk):
            bi = self.nb()
            pb = self.banks[bi].bitcast(BF16)

            def f(e, k=k, pb=pb):
                ins = None
                for j in range(nblk):
                    ins = e.transpose(out=pb[0:cw, j * nrows:(j + 1) * nrows],
                                      in_=src[0:nrows, j, col0 + k * cw:col0 + (k + 1) * cw],
                                      identity=ident[0:nrows, 0:nrows])
                return ins
            B.op('pe', f, reads=list(src_keys) + ['c_ident'], writes=[('bank', bi)])
            if k % 2 == 0:
                B.op('act', lambda e, k=k, pb=pb: e.activation(out=dstT[0:cw, kofs + k, 0:nblk * nrows],
                                                               in_=pb[0:cw, 0:nblk * nrows], func=AF.Copy),
                     reads=[('bank', bi)], writes=[(dstk, kofs + k)])
            else:
                B.op('dve', lambda e, k=k, pb=pb: e.tensor_copy(out=dstT[0:cw, kofs + k, 0:nblk * nrows],
                                                                in_=pb[0:cw, 0:nblk * nrows]),
                     reads=[('bank', bi)], writes=[(dstk, kofs + k)])

    def mm_group(self, bi, out_ap, pairs, reads, more_banks=()):
        n = len(pairs)

        def f(e):
            ins = None
            for i, (l, r) in enumerate(pairs):
                ins = e.matmul(out_ap, lhsT=l, rhs=r, start=(i == 0), stop=(i == n - 1))
            return ins
        return self.B.op('pe', f, reads=list(reads), writes=[('bank', bi)] + [('bank', b) for b in more_banks])

    def mm_multi(self, banks_used, groups, reads):
        def f(e):
            ins = None
            for out_ap, pairs in groups:
                n = len(pairs)
                for i, (l, r) in enumerate(pairs):
                    ins = e.matmul(out_ap, lhsT=l, rhs=r, start=(i == 0), stop=(i == n - 1))
            return ins
        return self.B.op('pe', f, reads=list(reads), writes=[('bank', b) for b in banks_used])

    def xk(self, name, n=8):
        return [(name, k) for k in range(n)]

    def ple_alloc(self, st, li):
        I = self.I
        P = {'li': li}
        P['gw'] = self.load_weight(st, 'ple_gw', I['ple_gate_w'][li], D, D)
        P['pw'] = self.load_weight(st, 'ple_pw', I['ple_w'][li], 256, D)
        P['gain'] = self.cload(st, 'ple_gain', [128, D], F32, I['norms'][:, 4 + li, :])
        P['hn'] = self.alloc(st, 'ple_hn', [128, 4, D], BF16)
        P['hnT'] = [self.alloc(st, 'ple_hnT%d' % s, [128, 8, 512], BF16) for s in range(2)]
        P['pb'] = self.alloc(st, 'ple_pb', [128, 4, 256], BF16)
        P['pT'] = [self.alloc(st, 'ple_pT%d' % s, [128, 2, 512], BF16) for s in range(2)]
        P['sg'] = self.alloc(st, 'ple_sg', [128, 2, 512], F32)
        P['rs'] = self.alloc(st, 'ple_rs', [128, 8], F32)
        return P

    def ple_pro(self, P, i, ht, htk, s, part=None):
        I = self.I
        li = P['li']
        if part in (None, 0):
            psrc = I['p'][li, 512 * i:512 * (i + 1), :].rearrange("(j p) c -> p j c", p=128)
            self.dma('pool', P['pb'][:], psrc, 'ple_p', writes=['ple_pb'])
            self.norm_tile(ht, htk, 128, 4, P['gain'], 'ple_gain', P['hn'], 'ple_hn', None, P['rs'])
        if part == 0:
            return
        self.transpose_tile(P['hn'], self.xk('ple_hn', 4), 128, 4, 8, P['hnT'][s], 'ple_hnT%d' % s)
        self.transpose_tile(P['pb'], ['ple_pb'], 128, 4, 2, P['pT'][s], 'ple_pT%d' % s)

    def ple_body(self, P, ht, htk, s, hook=None, hook2=None):
        B = self.B
        banks = self.banks
        hnT, pT = P['hnT'][s], P['pT'][s]
        for j in range(4):
            if hook is not None and j == 0:
                hook()
            if hook2 is not None and j == 3:
                hook2()
            for n in range(2):
                bg = self.nb()
                self.mm_group(bg, banks[bg][:, :],
                              [(hnT[:, k, j * 128:(j + 1) * 128], P['gw'][:, k, n * 512:(n + 1) * 512]) for k in range(8)],
                              reads=self.xk('ple_hnT%d' % s) + ['ple_gw'])
                B.op('act', lambda e, bg=bg, n=n: e.activation(out=P['sg'][:, n, :], in_=banks[bg][:, :], func=AF.Sigmoid),
                     reads=[('bank', bg)], writes=[('ple_sg', n)])
                bp = self.nb()
                self.mm_group(bp, banks[bp][:, :],
                              [(pT[:, k, j * 128:(j + 1) * 128], P['pw'][:, k, n * 512:(n + 1) * 512]) for k in range(2)],
                              reads=self.xk('ple_pT%d' % s, 2) + ['ple_pw'])
                B.op('dve', lambda e, bp=bp, n=n: e.tensor_tensor(out=P['sg'][:, n, :], in0=P['sg'][:, n, :],
                                                                  in1=banks[bp][:, :], op=ALU.mult),
                     reads=[('bank', bp), ('ple_sg', n)], writes=[('ple_sg', n)])
                B.op('dve', lambda e, j=j, n=n, ht=ht: e.tensor_tensor(out=ht[:, j, n * 512:(n + 1) * 512],
                                                                       in0=ht[:, j, n * 512:(n + 1) * 512],
                                                                       in1=P['sg'][:, n, :], op=ALU.add),
                     reads=[('ple_sg', n), htk], acc=[htk])

    def mixer_pass(self, L, dirn, src, dst, pre=False):
        B, I, S, C, NT, banks = self.B, self.I, self.S, self.C, self.NT, self.banks
        SEG = self.SEG
        ple_out = None
        fwd = dirn == 'f'
        even = (L == 0)
        H = 4
        if even:
            DK, DVP, DV, QW, VW = 96, 192, 192, 384, 768
            qo, ko, vo, ro, YO, WN = 256, 640, 1024, 1792, 256, 2592
        else:
            DK, DVP, DV, QW, VW = 128, 129, 128, 512, 512
            qo, ko, vo, ro, YO, WN = 1536, 2048, 2560, 3072, 512, 3600
        LW = QW if even else 4
        qscale = float(DK) ** -0.5
        ofw = S['ofw0'] if even else S['ofw1']
        tI, tS = ((0, 1) if fwd else (2, 3))
        if not even:
            tI, tS = tI + 4, tS + 4
        with ExitStack() as st:
            A = lambda n, sh, dt: self.alloc(st, n, sh, dt)
            Win = self.load_weight(st, 'Win', I['e_w_in'] if even else I['o_w_in'], D, WN)
            gain = self.cload(st, 'gain', [128, D], F32, I['norms'][:, L, :])
            self.epsc = A('epsc', [128, 1], F32)
            B.op('pool', lambda e: e.memset(self.epsc[:, :], EPS), writes=['epsc'])
            xts = [A('xt%d' % s_, [128, 4, D], F32) for s_ in range(2)]
            xt = xts[0]
            sq = None
            rs = A('rs', [128, 8], F32)
            xn = A('xn', [128, 4, D], BF16)
            xnTs = [A('xnT%d' % s_, [128, 8, 512], BF16) for s_ in range(2)]
            xnT = xnTs[0]
            tri = self.cload(st, 'tri', [128, 8, 128], F32, I['tri'])
            amask = self.cload(st, 'amask', [128, 512], BF16, I['amask'][:, 0 if fwd else 1, :])
            negcol = self.cload(st, 'negcol', [128, 2], F32, I['negcol'])
            negones = self.cload(st, 'negones', [128, 128], F32, I['negones'])
            qtms = [A('qtm%d' % s_, [128, QW], BF16) for s_ in range(2)]
            ktms = [A('ktm%d' % s_, [128, QW], BF16) for s_ in range(2)]
            vbs = [A('vb%d' % s_, [128, H, DVP], BF16) for s_ in range(2)]
            laps = [A('lap%d' % s_, [128, LW], F32) for s_ in range(2)]
            E = A('E', [128, 3, LW], F32)
            qin = A('qin', [128, QW], BF16)
            kin = A('kin', [128, QW], BF16)
            kout = A('kout', [128, QW], BF16)
            qinT = A('qinT', [128, 1, 512], BF16)
            kinT = A('kinT', [128, 1, 512], BF16)
            attm = A('attm', [128, 512], BF16)
            dec = A('dec', [128, H], F32)
            S32 = A('S32', [128, H, DVP], F32)
            Sbf = A('Sbf', [128, H, DVP], BF16)
            ost = A('ost', [128, 4, VW], F32)
            if even:
                gTa = A('gTa', [17, 512], BF16)
                w2 = A('w2', [17, 384], BF16)
                self.dma('pool', w2[:, :], I['w2aug'][0 if fwd else 1], 'c_w2', writes=['w2'])
                B.op('pool', lambda e: e.memset(gTa[:, :], 1.0), writes=['gTa'])
                if fwd and not pre:
                    uT = A('uT', [128, 2, 512], BF16)
                    zst = A('zst', [128, 4, 512], BF16)
                    cs64 = self.cload(st, 'cs64', [128, 2, 128], BF16, I['cs64'])
            else:
                gtss = [A('gts%d' % s_, [128, 16], F32) for s_ in range(2)]
                gbias = self.cload(st, 'gbias', [128, 16], F32, I['gate_bias'])
                smal = A('smal', [128, 16], F32)
                for s_ in range(2):
                    B.op('pool', lambda e, s_=s_: e.memset(vbs[s_][:, :, :], 1.0), writes=[('vb', s_)])
            if not even:
                rec = A('rec', [128, 8], F32)
                otmp = A('otmp', [128, VW], F32)
            if (not fwd) and (not pre):
                Wout = self.load_weight(st, 'Wout', I['e_w_out'] if even else I['o_w_out'], D, D)
                rgs = [A('rg%d' % s_, [128, VW], BF16) for s_ in range(2)]
                ybf = A('ybf', [128, 4, D], BF16)
                yT = A('yT', [128, 8, 512], BF16)
                gmix = self.cload(st, 'gmix', [128, VW], F32, I['gla_norm'] if even else I['mlstm_norm'])
                osum = A('osum', [128, VW], F32)
                ytmp = A('ytmp', [128, VW], F32)
                hrs = A('hrs', [128, 8], F32)
                if even:
                    yfa = A('yfa', [128, 4, 256], BF16)
                    yfb = A('yfb', [128, 4, 256], BF16)
                    yft = A('yft', [128, 4, 256], F32)
                else:
                    PB = A('PB', [128, 4, 514], F32)
                    scs = A('scs', [128, 512], F32)
                    sxl = A('sxl', [128, 2, 4], F32)
                    hx = A('hx', [128, 2, 4], F32)
                    hall = A('hall', [128, 8, 8], F32)
                    cacc = A('cacc', [128, 512], F32)
                    oconv = self.cload(st, 'oconv', [128, 4, 4], F32, I['o_conv'])
                    HALO = A('HALO', [128, 4, 2 * NT], F32)
            if pre:
                Dlog = A('Dlog', [128, H], F32)
                B.op('pool', lambda e: e.memset(Dlog[:, :], 0.0), writes=['Dlog'])

            B.op('dve', lambda e: e.memset(S32[:, :, :], 0.0), writes=['S32'])
            B.op('pool', lambda e: e.memset(Sbf[:, :, :], 0.0), writes=['Sbf'])
            Sin = C['Sin']
            if (not pre) and fwd:
                SW = self.SUMW[L]
                HWc = 4 * DVP + 4
                sumin, sumout = S['sumin%d' % L], S['sumout%d' % L]
                SUM = A('SUM', [128, 2, SW], F32)
                ctmp = A('ctmp', [128, 4, DVP], F32)

            def do_combine():
                B.op('dve', lambda e: e.memset(Sin[:, :, :, :], 0.0), writes=['Sin'])
                nld = [0]
                for di in range(2):
                    Sv = Sin[:, di, :, 0:DVP]
                    ranks = range(8) if di == 0 else range(7, -1, -1)
                    for r in ranks:
                        off = di * HWc
                        ss = nld[0] % 2
                        nld[0] += 1
                        self.dma('sp', SUM[:, ss, :], sumout[r * 128:(r + 1) * 128, :], 'ld_sum%d' % ss, reads=[], writes=[('SUM', ss)])
                        for h in range(H):
                            B.op('dve', lambda e, ss=ss, h=h, off=off, Sv=Sv: e.scalar_tensor_tensor(
                                out=ctmp[:, h, :], in0=Sv[:, h, :], scalar=SUM[:, ss, off + 4 * DVP + h:off + 4 * DVP + h + 1],
                                in1=SUM[:, ss, off + h * DVP:off + (h + 1) * DVP], op0=ALU.mult, op1=ALU.add),
                                 reads=['Sin', ('SUM', ss)], acc=['ctmp'] if h else [], writes=[] if h else ['ctmp'])
                        ucol = C['coef'][:, 16 * di + r:16 * di + r + 1]
                        ncol = C['coef'][:, 16 * di + 8 + r:16 * di + 8 + r + 1]
                        B.op('dve', lambda e, Sv=Sv, ncol=ncol: e.tensor_scalar(out=Sv, in0=Sv, scalar1=ncol, scalar2=None, op0=ALU.mult),
                             reads=['Sin', 'ctmp', 'c_coef'], writes=['Sin'])
                        B.op('dve', lambda e, Sv=Sv, ucol=ucol: e.scalar_tensor_tensor(out=Sv, in0=ctmp[:, :, :], scalar=ucol, in1=Sv,
                                                                                      op0=ALU.mult, op1=ALU.add),
                             reads=['Sin', 'ctmp', 'c_coef'], writes=['Sin'])

            if (not even) and (not fwd) and (not pre):
                srcv = src.rearrange("(i t) d -> i t d", t=512)
                self.dma('sp', xt[0:NT, 0, :], srcv[:, 0, :], 'ld_x', writes=['xt0'])
                self.dma('sp', xt[NT:2 * NT, 0, :], srcv[:, 511, :], 'ld_x', acc=['xt0'])
                self.norm_tile(xt, 'xt0', 2 * NT, 1, gain, 'gain', xn, 'xn', sq, rs)
                self.transpose_tile(xn, [('xn', 0)], 2 * NT, 1, 8, xnT, 'xnT0')
                for c in range(4):
                    b1 = self.nb()
                    self.mm_group(b1, banks[b1][:, 0:2 * NT],
                                  [(Win[:, k, 512 + c * 128:512 + (c + 1) * 128], xnT[:, k, 0:2 * NT]) for k in range(8)],
                                  reads=self.xk('xnT0') + ['Win'])
                    B.op('act', lambda e, b1=b1: e.activation(out=scs[:, 0:2 * NT], in_=banks[b1][:, 0:2 * NT], func=AF.Copy),
                         reads=[('bank', b1)], writes=['scs'])
                    b2 = self.nb()
                    self.mm_group(b2, banks[b2][:, 0:2 * NT],
                                  [(Win[:, k, 1024 + c * 128:1024 + (c + 1) * 128], xnT[:, k, 0:2 * NT]) for k in range(8)],
                                  reads=self.xk('xnT0') + ['Win'])
                    B.op('dve', lambda e, b2=b2, c=c: e.tensor_tensor(out=HALO[:, c, :], in0=scs[:, 0:2 * NT],
                                                                      in1=banks[b2][:, 0:2 * NT], op=ALU.mult),
                         reads=[('bank', b2), 'scs'], acc=['HALO'])
                B.op('pool', lambda e: e.tensor_copy(out=hx[:, 0, :], in_=HALO[:, :, SEG]), reads=['HALO'], writes=[('hx', 0)])
                B.op('pool', lambda e: e.tensor_copy(out=hx[:, 1, :], in_=HALO[:, :, 2 * NT - 1]), reads=['HALO'], writes=[('hx', 1)])
                self.dma('pool', S['hins'], hx[:, :, :].rearrange("p a c -> p (a c)"), 'st_hx', reads=[('hx', 0), ('hx', 1)], writes=['hins'])
                B.collective(lambda e: e.collective_compute("AllGather", ALU.bypass, replica_groups=[list(range(8))],
                                                            ins=[S['hins'].opt()], outs=[S['houts'].opt()]),
                     reads=['hins'], writes=['houts'])
                self.dma('sp', hall[:, :, :], S['houts'].rearrange("(r p) w -> p r w", p=128), 'ld_hall', reads=['houts'], writes=['hall'])
                for sd in range(2):
                    for r in range(8):
                        scol = C['coef'][:, 32 + 8 * sd + r:32 + 8 * sd + r + 1]
                        srcp = hall[:, r, 4:8] if sd == 0 else hall[:, r, 0:4]
                        if r == 0:
                            B.op('dve', lambda e, sd=sd, scol=scol, srcp=srcp: e.tensor_scalar(out=sxl[:, sd, :], in0=srcp, scalar1=scol,
                                                                                              scalar2=None, op0=ALU.mult),
                                 reads=['hall', 'c_coef'], writes=[('sxl', sd)])
                        else:
                            B.op('dve', lambda e, sd=sd, scol=scol, srcp=srcp: e.scalar_tensor_tensor(out=sxl[:, sd, :], in0=srcp, scalar=scol,
                                                                                                     in1=sxl[:, sd, :], op0=ALU.mult, op1=ALU.add),
                                 reads=['hall', 'c_coef', ('sxl', sd)], writes=[('sxl', sd)])

            tiles = list(range(SEG, NT)) if pre else list(range(NT))
            order = tiles if fwd else tiles[::-1]
            def pro_a(i, xt, xtk, s_):
                tsl = slice(512 * i, 512 * (i + 1))
                self.dma('sp', xt[:, :, :], src[tsl, :].rearrange("(j p) d -> p j d", p=128), 'ld_x%d' % s_, writes=[xtk])
                self.norm_tile(xt, xtk, 128, 4, gain, 'gain', xn, 'xn', sq, rs)

            def pro_b(i, xnT, xnTk):
                self.transpose_tile(xn, self.xk('xn', 4), 128, 4, 8, xnT, xnTk)

            def tile_body(i, xt, xtk, xnT, xnTk, hooks):
                tsl = slice(512 * i, 512 * (i + 1))
                xr = self.xk(xnTk) + ['Win']
                if (not fwd) and (not pre):
                    self.dma('sp', ost[:, :, :], ofw[tsl, :].rearrange("(j p) d -> p j d", p=128), 'ld_o', writes=['ost'])
                    if even:
                        self.dma('sp', yfa[:, :, :], S['yf'][tsl, :].rearrange("(j p) d -> p j d", p=128), 'ld_yfa',
                                 writes=['yfa'])
                if even:
                    go = 2560 if fwd else 2576
                    bi = self.nb()
                    self.mm_group(bi, banks[bi][0:16, :], [(Win[:, k, go:go + 16], xnT[:, k, :]) for k in range(8)], reads=xr)
                    B.op('act', lambda e, bi=bi: e.activation(out=gTa[0:16, :], in_=banks[bi][0:16, :], func=AF.Copy),
                         reads=[('bank', bi)], acc=['gTa'])
                    if fwd and (not pre) and i < SEG:
                        for m in range(2):
                            bi = self.nb()
                            self.mm_group(bi, banks[bi][:, :],
                                          [(Win[:, k, m * 128:(m + 1) * 128], xnT[:, k, :]) for k in range(8)], reads=xr)
                            B.op('dve', lambda e, bi=bi, m=m: e.tensor_copy(out=uT[:, m, :], in_=banks[bi][:, :]),
                                 reads=[('bank', bi)], writes=[('uT', m)])
                        for m in range(2):
                            for t in range(2):
                                bi = self.nb()
                                self.mm_group(bi, banks[bi][:, :], [(cs64[:, t, :], uT[:, m, :])], reads=[('uT', m), 'cs64'])
                                B.op('act', lambda e, bi=bi, m=m, t=t: e.activation(out=zst[:, t * 2 + m, :],
                                                                                    in_=banks[bi][:, :], func=AF.Copy),
                                     reads=[('bank', bi)], acc=['zst'])
                        self.dma('pool', S['Zs'].rearrange("(r p) t -> p r t", p=128)[:, :, tsl], zst[:, :, :], 'st_z',
                                 reads=['zst'])
                inj = None
                if not pre:
                    if fwd and i == SEG:
                        inj = 0
                    elif (not fwd) and i == NT - 1:
                        inj = 1
                    elif (not fwd) and i == SEG - 1:
                        inj = 'zero'
                if inj == 'zero':
                    B.op('dve', lambda e: e.memset(S32[:, :, :], 0.0), reads=[], writes=['S32'])
                    B.op('pool', lambda e: e.memset(Sbf[:, :, :], 0.0), reads=[], writes=['Sbf'])
                elif inj is not None:
                    B.op('dve', lambda e, inj=inj: e.tensor_copy(out=S32[:, :, :], in_=Sin[:, inj, :, 0:DVP]), reads=['Sin'], writes=['S32'])
                    B.op('act', lambda e: e.activation(out=Sbf[:, :, :], in_=S32[:, :, :], func=AF.Copy), reads=['S32'], writes=['Sbf'])
                if (not even) and (not fwd) and (not pre):
                    if i == 0:
                        B.op('pool', lambda e: e.memset(PB[:, :, 0:1], 0.0), writes=[('PB', 'l')])
                    elif i == SEG:
                        B.op('pool', lambda e: e.tensor_copy(out=PB[:, :, 0], in_=sxl[:, 0, :]), reads=[('sxl', 0)], writes=[('PB', 'l')])
                    else:
                        B.op('pool', lambda e, i=i: e.tensor_copy(out=PB[:, :, 0], in_=HALO[:, :, NT + i - 1]), reads=['HALO'], writes=[('PB', 'l')])
                    if i == SEG - 1:
                        B.op('pool', lambda e: e.memset(PB[:, :, 513:514], 0.0), writes=[('PB', 'r')])
                    elif i == NT - 1:
                        B.op('pool', lambda e: e.tensor_copy(out=PB[:, :, 513], in_=sxl[:, 1, :]), reads=[('sxl', 1)], writes=[('PB', 'r')])
                    else:
                        B.op('pool', lambda e, i=i: e.tensor_copy(out=PB[:, :, 513], in_=HALO[:, :, i + 1]), reads=['HALO'], writes=[('PB', 'r')])
                    for c in range(4):
                        b1 = self.nb()
                        self.mm_group(b1, banks[b1][:, :],
                                      [(Win[:, k, 512 + c * 128:512 + (c + 1) * 128], xnT[:, k, :]) for k in range(8)], reads=xr)
                        B.op('act', lambda e, b1=b1: e.activation(out=scs[:, :], in_=banks[b1][:, :], func=AF.Copy),
                             reads=[('bank', b1)], writes=['scs'])
                        b2 = self.nb()
                        self.mm_group(b2, banks[b2][:, :],
                                      [(Win[:, k, 1024 + c * 128:1024 + (c + 1) * 128], xnT[:, k, :]) for k in range(8)], reads=xr)
                        B.op('dve', lambda e, b2=b2, c=c: e.tensor_tensor(out=PB[:, c, 1:513], in0=scs[:, :],
                                                                          in1=banks[b2][:, :], op=ALU.mult),
                             reads=[('bank', b2), 'scs'], writes=[('PB', c)])
                        B.op('dve', lambda e, c=c: e.tensor_scalar(out=cacc[:, :], in0=PB[:, c, 1:513],
                                                                   scalar1=oconv[:, c, 1:2], scalar2=oconv[:, c, 3:4],
                                                                   op0=ALU.mult, op1=ALU.add),
                             reads=[('PB', c), 'oconv'], writes=['cacc'])
                        B.op('dve', lambda e, c=c: e.scalar_tensor_tensor(out=cacc[:, :], in0=PB[:, c, 0:512],
                                                                           scalar=oconv[:, c, 0:1], in1=cacc[:, :],
                                                                           op0=ALU.mult, op1=ALU.add),
                             reads=[('PB', c), ('PB', 'l'), 'cacc', 'oconv'], writes=['cacc'])
                        B.op('dve', lambda e, c=c: e.scalar_tensor_tensor(out=cacc[:, :], in0=PB[:, c, 2:514],
                                                                          scalar=oconv[:, c, 2:3], in1=cacc[:, :],
                                                                          op0=ALU.mult, op1=ALU.add),
                             reads=[('PB', c), ('PB', 'r'), 'cacc', 'oconv'], writes=['cacc'])
                        b3 = self.nb()
                        self.mm_group(b3, banks[b3][:, :],
                                      [(Win[:, k, c * 128:(c + 1) * 128], xnT[:, k, :]) for k in range(8)], reads=xr)
                        B.op('dve', lambda e, b3=b3, c=c: e.tensor_tensor(out=yT[:, c, :], in0=cacc[:, :],
                                                                          in1=banks[b3][:, :], op=ALU.mult),
                             reads=[('bank', b3), 'cacc'], writes=[('yT', c)])

                def blk_proj(j, sl):
                    cs = slice(j * 128, (j + 1) * 128)
                    io, fo = (0, 4) if fwd else (8, 12)
                    if not pre:
                        bq = self.nb()
                        self.mm_group(bq, banks[bq][:, 0:QW], [(xnT[:, k, cs], Win[:, k, qo:qo + QW]) for k in range(8)], reads=xr)
                        B.op('act', lambda e, bq=bq: e.activation(out=qtms[sl][:, :], in_=banks[bq][:, 0:QW], func=AF.Copy),
                             reads=[('bank', bq)], writes=[('qtm', sl)])
                    bk = self.nb()
                    self.mm_group(bk, banks[bk][:, 0:QW], [(xnT[:, k, cs], Win[:, k, ko:ko + QW]) for k in range(8)], reads=xr)
                    B.op('dve', lambda e, bk=bk: e.tensor_copy(out=ktms[sl][:, :], in_=banks[bk][:, 0:QW]),
                         reads=[('bank', bk)], writes=[('ktm', sl)])
                    if even:
                        for g2 in range(2):
                            bv = self.nb()
                            self.mm_group(bv, banks[bv][:, 0:384],
                                          [(xnT[:, k, cs], Win[:, k, vo + g2 * 384:vo + (g2 + 1) * 384]) for k in range(8)], reads=xr)
                            B.op('act' if g2 == 0 else 'dve',
                                 (lambda e, bv=bv, g2=g2: e.activation(out=vbs[sl][:, 2 * g2:2 * g2 + 2, :],
                                                                       in_=banks[bv][:, 0:384].rearrange("p (h d) -> p h d", h=2),
                                                                       func=AF.Copy)) if g2 == 0 else
                                 (lambda e, bv=bv, g2=g2: e.tensor_copy(out=vbs[sl][:, 2 * g2:2 * g2 + 2, :],
                                                                        in_=banks[bv][:, 0:384].rearrange("p (h d) -> p h d", h=2))),
                                 reads=[('bank', bv)], acc=[('vb', sl)])
                        if (not fwd) and (not pre):
                            for g2 in range(2):
                                br = self.nb()
                                self.mm_group(br, banks[br][:, 0:384],
                                              [(xnT[:, k, cs], Win[:, k, ro + g2 * 384:ro + (g2 + 1) * 384]) for k in range(8)], reads=xr)
                                B.op('act', lambda e, br=br, g2=g2: e.activation(out=rgs[sl][:, g2 * 384:(g2 + 1) * 384],
                                                                                 in_=banks[br][:, 0:384], func=AF.Silu),
                                     reads=[('bank', br)], acc=[('rg', sl)])
                        bz = self.nb()
                        self.mm_group(bz, banks[bz][:, 0:384], [(gTa[0:17, cs], w2[0:17, :])], reads=['gTa', 'w2'])
                        B.op('act', lambda e, bz=bz: e.activation(out=E[:, 0, :], in_=banks[bz][:, 0:384], func=AF.Exp, scale=-1.0),
                             reads=[('bank', bz)], writes=[('E', 0)])
                        B.op('act', lambda e: e.activation(out=laps[sl][:, :], in_=E[:, 0, :], func=AF.Ln, bias=1.0),
                             reads=[('E', 0)], writes=[('lap', sl)])
                    else:
                        bv = self.nb()
                        self.mm_group(bv, banks[bv][:, :], [(xnT[:, k, cs], Win[:, k, vo:vo + 512]) for k in range(8)], reads=xr)
                        B.op('act', lambda e, bv=bv: e.activation(out=vbs[sl][:, :, 0:128],
                                                                  in_=banks[bv][:, :].rearrange("p (h d) -> p h d", h=4),
                                                                  func=AF.Copy),
                             reads=[('bank', bv)], acc=[('vb', sl)])
                        if (not fwd) and (not pre):
                            br = self.nb()
                            self.mm_group(br, banks[br][:, :], [(xnT[:, k, cs], Win[:, k, ro:ro + 512]) for k in range(8)], reads=xr)
                            B.op('act', lambda e, br=br: e.activation(out=rgs[sl][:, :], in_=banks[br][:, :], func=AF.Sigmoid),
                                 reads=[('bank', br)], writes=[('rg', sl)])
                        bz = self.nb()
                        self.mm_group(bz, banks[bz][:, 0:16], [(xnT[:, k, cs], Win[:, k, 3584:3600]) for k in range(8)], reads=xr)
                        B.op('dve', lambda e, bz=bz: e.tensor_tensor(out=gtss[sl][:, :], in0=banks[bz][:, 0:16], in1=gbias[:, :], op=ALU.add),
                             reads=[('bank', bz), 'gbias'], writes=[('gts', sl)])
                        io, fo = (0, 4) if fwd else (8, 12)
                        B.op('act', lambda e, fo=fo: e.activation(out=smal[:, 0:4], in_=gtss[sl][:, fo:fo + 4], func=AF.Exp, scale=-1.0),
                             reads=[('gts', sl)], writes=['smal'])
                        B.op('act', lambda e: e.activation(out=laps[sl][:, :], in_=smal[:, 0:4], func=AF.Ln, bias=1.0),
                             reads=['smal'], writes=[('lap', sl)])
                def blk_chain(j, sl):
                    io, fo = (0, 4) if fwd else (8, 12)
                    bc = self.nb()
                    if not pre:
                        self.mm_group(bc, banks[bc][:, 0:LW], [(tri[:, tI, :], laps[sl][:, :])], reads=['tri', ('lap', sl)])
                    bt = self.nb()
                    self.mm_group(bt, banks[bt][:, 0:LW], [(tri[:, tS, :], laps[sl][:, :])], reads=['tri', ('lap', sl)])
                    bd = self.nb()
                    if even:
                        self.mm_multi([bd], [(banks[bd][0:DK, h:h + 1], [(laps[sl][:, h * DK:(h + 1) * DK], negcol[:, 0:1])]) for h in range(H)],
                                      reads=[('lap', sl), 'negcol'])
                        if not pre:
                            B.op('act', lambda e, bc=bc: e.activation(out=E[:, 0, :], in_=banks[bc][:, 0:LW], func=AF.Exp),
                                 reads=[('bank', bc)], writes=[('E', 0)])
                            B.op('act', lambda e, bc=bc: e.activation(out=E[:, 1, :], in_=banks[bc][:, 0:LW], func=AF.Exp, scale=-1.0),
                                 reads=[('bank', bc)], writes=[('E', 1)])
                        B.op('act', lambda e, bt=bt: e.activation(out=E[:, 2, :], in_=banks[bt][:, 0:LW], func=AF.Exp),
                             reads=[('bank', bt)], writes=[('E', 2)])
                        B.op('act', lambda e, bd=bd: e.activation(out=dec[0:DK, :], in_=banks[bd][0:DK, 0:H], func=AF.Exp),
                             reads=[('bank', bd)], writes=['dec'])
                        if not pre:
                            B.op('dve', lambda e: e.scalar_tensor_tensor(out=qin[:, :], in0=qtms[sl][:, :], scalar=qscale, in1=E[:, 0, :],
                                                                         op0=ALU.mult, op1=ALU.mult),
                                 reads=[('qtm', sl), ('E', 0)], writes=['qin'])
                            B.op('dve', lambda e: e.tensor_tensor(out=kin[:, :], in0=ktms[sl][:, :], in1=E[:, 1, :], op=ALU.mult),
                                 reads=[('ktm', sl), ('E', 1)], writes=['kin'])
                        B.op('dve', lambda e: e.tensor_tensor(out=kout[:, :], in0=ktms[sl][:, :], in1=E[:, 2, :], op=ALU.mult),
                             reads=[('ktm', sl), ('E', 2)], writes=['kout'])
                    else:
                        self.mm_group(bd, banks[bd][:, 0:4], [(negones[:, :], laps[sl][:, :])], reads=['negones', ('lap', sl)])
                        if not pre:
                            B.op('dve', lambda e, bc=bc, io=io: e.tensor_tensor(out=E[:, 1, :], in0=gtss[sl][:, io:io + 4], in1=banks[bc][:, 0:4],
                                                                                op=ALU.subtract),
                                 reads=[('bank', bc), ('gts', sl)], writes=[('E', 1)])
                        B.op('dve', lambda e, bt=bt, io=io: e.tensor_tensor(out=E[:, 2, :], in0=gtss[sl][:, io:io + 4], in1=banks[bt][:, 0:4],
                                                                            op=ALU.add),
                             reads=[('bank', bt), ('gts', sl)], writes=[('E', 2)])
                        if not pre:
                            B.op('act', lambda e, bc=bc: e.activation(out=E[:, 0, :], in_=banks[bc][:, 0:4], func=AF.Exp),
                                 reads=[('bank', bc)], writes=[('E', 0)])
                            B.op('act', lambda e: e.activation(out=E[:, 1, :], in_=E[:, 1, :], func=AF.Exp),
                                 reads=[('E', 1)], writes=[('E', 1)])
                        B.op('act', lambda e: e.activation(out=E[:, 2, :], in_=E[:, 2, :], func=AF.Exp),
                             reads=[('E', 2)], writes=[('E', 2)])
                        B.op('act', lambda e, bd=bd: e.activation(out=dec[:, :], in_=banks[bd][:, 0:4], func=AF.Exp),
                             reads=[('bank', bd)], writes=['dec'])
                        v3 = lambda t: t[:, :].rearrange("p (h d) -> p h d", h=4)
                        bc3 = lambda s_: E[:, s_, :].unsqueeze(2).to_broadcast([128, 4, 128])
                        if not pre:
                            B.op('dve', lambda e: e.scalar_tensor_tensor(out=v3(qin), in0=v3(qtms[sl]), scalar=qscale, in1=bc3(0),
                                                                         op0=ALU.mult, op1=ALU.mult),
                                 reads=[('qtm', sl), ('E', 0)], writes=['qin'])
                            B.op('dve', lambda e: e.tensor_tensor(out=v3(kin), in0=v3(ktms[sl]), in1=bc3(1), op=ALU.mult),
                                 reads=[('ktm', sl), ('E', 1)], writes=['kin'])
                        B.op('dve', lambda e: e.tensor_tensor(out=v3(kout), in0=v3(ktms[sl]), in1=bc3(2), op=ALU.mult),
                             reads=[('ktm', sl), ('E', 2)], writes=['kout'])
                    if pre:
                        B.op('dve', lambda e, bd=bd: e.tensor_tensor(out=Dlog[0:DK, :], in0=Dlog[0:DK, :], in1=banks[bd][0:DK, 0:H], op=ALU.add),
                             reads=[('bank', bd), 'Dlog'], writes=['Dlog'])
                    if not pre:
                        self.transpose_tile(qin[:, :].rearrange("p (h d) -> p h d", h=H), ['qin'], 128, H, 1, qinT, 'qinT', cw=DK)
                        self.transpose_tile(kin[:, :].rearrange("p (h d) -> p h d", h=H), ['kin'], 128, H, 1, kinT, 'kinT', cw=DK)
                        ba = self.nb()
                        self.mm_multi([ba], [(banks[ba][:, h * 128:(h + 1) * 128],
                                              [(kinT[0:DK, 0, h * 128:(h + 1) * 128], qinT[0:DK, 0, h * 128:(h + 1) * 128])]) for h in range(H)],
                                      reads=[('qinT', 0), ('kinT', 0)])
                        B.op('dve', lambda e, ba=ba: e.tensor_tensor(out=attm[:, :], in0=banks[ba][:, :], in1=amask[:, :], op=ALU.mult),
                             reads=[('bank', ba), 'amask'], writes=['attm'])
                        b0 = self.nb()
                        b1 = self.nb()
                        ob = [b0, b1]
                        self.mm_multi(ob, [(banks[ob[h // 2]][:, (h % 2) * DVP:(h % 2 + 1) * DVP],
                                            [(attm[:, h * 128:(h + 1) * 128], vbs[sl][:, h, :]),
                                             (qinT[0:DK, 0, h * 128:(h + 1) * 128], Sbf[0:DK, h, :])]) for h in range(H)],
                                      reads=['attm', ('vb', sl), ('qinT', 0), 'Sbf'])
                    c0 = self.nb()
                    c1 = self.nb()
                    cb = [c0, c1]
                    self.mm_multi(cb, [(banks[cb[h // 2]][0:DK, (h % 2) * DVP:(h % 2 + 1) * DVP],
                                        [(kout[:, h * DK:(h + 1) * DK], vbs[sl][:, h, :])]) for h in range(H)],
                                  reads=['kout', ('vb', sl)])
                    for h in range(H):
                        B.op('dve', lambda e, h=h, cb=cb: e.scalar_tensor_tensor(
                            out=S32[0:DK, h, :], in0=S32[0:DK, h, :], scalar=dec[0:DK, h:h + 1],
                            in1=banks[cb[h // 2]][0:DK, (h % 2) * DVP:(h % 2 + 1) * DVP], op0=ALU.mult, op1=ALU.add),
                             reads=['S32', 'dec', ('bank', cb[h // 2])], acc=['S32'])
                    B.op('act', lambda e: e.activation(out=Sbf[0:DK, :, :], in_=S32[0:DK, :, :], func=AF.Copy),
                         reads=['S32'], writes=['Sbf'])
                    if not pre:
                        ov = lambda b_: banks[b_][:, 0:2 * DVP].rearrange("p (h d) -> p h d", h=2)
                        if even:
                            if fwd:
                                B.op('act', lambda e, b0=b0, j=j: e.activation(out=ost[:, j, 0:384], in_=banks[b0][:, 0:384], func=AF.Copy),
                                     reads=[('bank', b0)], acc=['ost'])
                                B.op('dve', lambda e, b1=b1, j=j: e.tensor_copy(out=ost[:, j, 384:768], in_=banks[b1][:, 0:384]),
                                     reads=[('bank', b1)], acc=['ost'])
                            else:
                                for g2, bb in enumerate(ob):
                                    B.op('dve', lambda e, bb=bb, g2=g2, j=j: e.tensor_tensor(out=osum[:, g2 * 384:(g2 + 1) * 384],
                                                                                             in0=banks[bb][:, 0:384],
                                                                                             in1=ost[:, j, g2 * 384:(g2 + 1) * 384], op=ALU.add),
                                         reads=[('bank', bb), 'ost'], acc=['osum'])
                        else:
                            for g2, bb in enumerate(ob):
                                B.op('act', lambda e, bb=bb, g2=g2: e.activation(out=rec[:, 2 * g2:2 * g2 + 2], in_=ov(bb)[:, :, 128],
                                                                                 func=AF.Square),
                                     reads=[('bank', bb)], acc=['rec0'])
                            B.op('dve', lambda e: e.tensor_scalar(out=rec[:, 0:4], in0=rec[:, 0:4], scalar1=1.0, scalar2=None, op0=ALU.max),
                                 reads=['rec0'], writes=['rec0'])
                            B.op('act', lambda e: e.activation(out=rec[:, 4:8], in_=rec[:, 0:4], func=AF.Ln), reads=['rec0'], writes=['rec1'])
                            B.op('act', lambda e: e.activation(out=rec[:, 4:8], in_=rec[:, 4:8], func=AF.Exp, scale=-0.5),
                                 reads=['rec1'], writes=['rec'])
                            for g2, bb in enumerate(ob):
                                dst_ap = (ost[:, j, g2 * 256:(g2 + 1) * 256] if fwd else otmp[:, g2 * 256:(g2 + 1) * 256])
                                B.op('dve', lambda e, bb=bb, g2=g2, dst_ap=dst_ap: e.tensor_tensor(
                                    out=dst_ap.rearrange("p (h d) -> p h d", h=2), in0=ov(bb)[:, :, 0:128],
                                    in1=rec[:, 4 + 2 * g2:6 + 2 * g2].unsqueeze(2).to_broadcast([128, 2, 128]), op=ALU.mult),
                                     reads=[('bank', bb), 'rec'], acc=['ost' if fwd else 'otmp'])
                            if not fwd:
                                B.op('dve', lambda e, j=j: e.tensor_tensor(out=osum[:, :], in0=otmp[:, :], in1=ost[:, j, :], op=ALU.add),
                                     reads=['otmp', 'ost'], writes=['osum'])
                        if not fwd:
                            B.op('pool', lambda e: e.memset(hrs[:, 0:4], 0.0), writes=['hrs0'])
                            for h in range(H):
                                B.op('act', lambda e, h=h: e.activation(out=ytmp[:, h * DV:(h + 1) * DV], in_=osum[:, h * DV:(h + 1) * DV],
                                                                        func=AF.Square, accum_out=hrs[:, h:h + 1]),
                                     reads=['osum'], acc=['hrs0', 'ytmp'])
                            B.op('act', lambda e: e.activation(out=hrs[:, 4:8], in_=hrs[:, 0:4], func=AF.Ln, scale=1.0 / DV,
                                                               bias=self.epsc[:, 0:1]),
                                 reads=['hrs0', 'epsc'], writes=['hrs1'])
                            B.op('act', lambda e: e.activation(out=hrs[:, 4:8], in_=hrs[:, 4:8], func=AF.Exp, scale=-0.5),
                                 reads=['hrs1'], writes=['hrs'])
                            B.op('dve', lambda e: e.tensor_tensor(out=ytmp[:, :].rearrange("p (h d) -> p h d", h=H),
                                                                  in0=osum[:, :].rearrange("p (h d) -> p h d", h=H),
                                                                  in1=hrs[:, 4:8].unsqueeze(2).to_broadcast([128, H, DV]), op=ALU.mult),
                                 reads=['osum', 'hrs'], writes=['ytmp'])
                            B.op('dve', lambda e: e.tensor_tensor(out=ytmp[:, :], in0=ytmp[:, :], in1=gmix[:, :], op=ALU.mult),
                                 reads=['ytmp', 'gmix'], writes=['ytmp'])
                            B.op('dve', lambda e, j=j: e.tensor_tensor(out=ybf[:, j, YO:YO + VW], in0=ytmp[:, :], in1=rgs[sl][:, :], op=ALU.mult),
                                 reads=['ytmp', ('rg', sl)], writes=[('ybf', j)])
                blks = list(range(4)) if fwd else [3, 2, 1, 0]
                PIPE = True
                if PIPE:
                    blk_proj(blks[0], 0)
                for jn, j in enumerate(blks):
                    if hooks and jn == 1:
                        hooks[0]()
                    if hooks and jn == 3:
                        hooks[1]()
                    if PIPE:
                        if jn + 1 < 4:
                            blk_proj(blks[jn + 1], (jn + 1) % 2)
                    else:
                        blk_proj(j, jn % 2)
                    blk_chain(j, jn % 2)
                if pre:
                    pass
                elif fwd:
                    self.dma('pool', ofw[tsl, :].rearrange("(j p) d -> p j d", p=128), ost[:, :, :], 'st_o', reads=['ost'])
                else:
                    if even:
                        B.op('pool', lambda e: e.tensor_copy(out=ybf[:, :, 0:256], in_=yfa[:, :, :]), reads=['yfa'], writes=[('ybf', 'f')])
                        self.transpose_tile(ybf, self.xk('ybf', 4) + [('ybf', 'f')], 128, 4, 8, yT, 'yT')
                    else:
                        self.transpose_tile(ybf, self.xk('ybf', 4), 128, 4, 4, yT, 'yT', kofs=4, col0=512)
                    for j in range(4):
                        for n in range(2):
                            bi = self.nb()
                            self.mm_group(bi, banks[bi][:, :],
                                          [(yT[:, k, j * 128:(j + 1) * 128], Wout[:, k, n * 512:(n + 1) * 512]) for k in range(8)],
                                          reads=self.xk('yT') + ['Wout'])
                            B.op('dve', lambda e, bi=bi, j=j, n=n: e.tensor_tensor(out=xt[:, j, n * 512:(n + 1) * 512],
                                                                                   in0=xt[:, j, n * 512:(n + 1) * 512],
                                                                                   in1=banks[bi][:, :], op=ALU.add),
                                 reads=[('bank', bi), xtk], acc=[xtk])
                    self.dma('pool', dst[tsl, :].rearrange("(j p) d -> p j d", p=128), xt[:, :, :], 'st_h', reads=[xtk])
            pro_a(order[0], xts[0], 'xt0', 0)
            pro_b(order[0], xnTs[0], 'xnT0')
            for n_, i_ in enumerate(order):
                s_ = n_ % 2
                hooks = []
                if (not pre) and fwd and n_ == 2:
                    do_combine()
                if n_ + 1 < len(order):
                    i2, s2 = order[n_ + 1], (n_ + 1) % 2
                    hooks = [(lambda i2=i2, s2=s2: pro_a(i2, xts[s2], 'xt%d' % s2, s2)),
                             (lambda i2=i2, s2=s2: pro_b(i2, xnTs[s2], 'xnT%d' % s2))]
                tile_body(i_, xts[s_], 'xt%d' % s_, xnTs[s_], 'xnT%d' % s_, hooks)
            if pre:
                di = 0 if fwd else 1
                off = di * (4 * DVP + 4)
                Dex = A('Dex', [128, H], F32)
                B.op('act', lambda e: e.activation(out=Dex[:, :], in_=Dlog[:, :], func=AF.Exp), reads=['Dlog'], writes=['Dex'])
                sumin = S['sumin%d' % L]
                self.dma('pool', sumin[:, off:off + 4 * DVP], S32[:, :, :].rearrange("p h d -> p (h d)"), 'st_sum', reads=['S32'])
                self.dma('pool', sumin[:, off + 4 * DVP:off + 4 * DVP + 4], Dex[:, :], 'st_sum', reads=['Dex'])
                if not fwd:
                    sumout = S['sumout%d' % L]
                    B.collective(lambda e: e.collective_compute("AllGather", ALU.bypass, replica_groups=[list(range(8))],
                                                                ins=[sumin.opt()], outs=[sumout.opt()]),
                                 reads=[], writes=[])
            B.barrier()
            B.emit()


    def zpre_pass(self):
        B, I, S, C, NT, banks = self.B, self.I, self.S, self.C, self.NT, self.banks
        NTF = 2 * NT
        with ExitStack() as st:
            A = lambda n, sh, dt: self.alloc(st, n, sh, dt)
            Wu = self.alloc(st, 'Wu', [128, 8, 256], BF16)
            self.dma('pool', Wu[:, :, :], I['e_w_in'].rearrange("(k p) n -> p k n", p=128)[:, :, 0:256], 'w_Wu', writes=['Wu'])
            gain = self.cload(st, 'gain', [128, D], F32, I['norms'][:, 0, :])
            cs64 = self.cload(st, 'cs64', [128, 2, 128], BF16, I['cs64'])
            self.epsc = A('epsc', [128, 1], F32)
            B.op('pool', lambda e: e.memset(self.epsc[:, :], EPS), writes=['epsc'])
            rss = [A('rs%d' % s, [128, 8], F32) for s in range(2)]
            xts = [A('xt%d' % s, [128, 4, D], F32) for s in range(2)]
            xns = [A('xn%d' % s, [128, 4, D], BF16) for s in range(2)]
            xnT = A('xnT', [128, 8, 512], BF16)
            uT = A('uT', [128, 2, 512], BF16)
            zsts = [A('zst%d' % s, [128, 4, 512], BF16) for s in range(2)]
            Zv = S['Zp'].rearrange("(r p) t -> p r t", p=128)
            for i in range(NTF):
                s = i % 2
                xt, xn, zst = xts[s], xns[s], zsts[s]
                tsl = slice(512 * i, 512 * (i + 1))
                self.dma('sp', xt[:, :, :], I['xfull'][tsl, :].rearrange("(j p) d -> p j d", p=128), 'ld_x%d' % s, writes=['xt%d' % s])
                self.norm_tile(xt, 'xt%d' % s, 128, 4, gain, 'gain', xn, 'xn%d' % s, None, rss[s])
                self.transpose_tile(xn, self.xk('xn%d' % s, 4), 128, 4, 8, xnT, 'xnT')
                for m in range(2):
                    bi = self.nb()
                    self.mm_group(bi, banks[bi][:, :], [(Wu[:, k, m * 128:(m + 1) * 128], xnT[:, k, :]) for k in range(8)],
                                  reads=self.xk('xnT') + ['Wu'])
                    B.op('dve', lambda e, bi=bi, m=m: e.tensor_copy(out=uT[:, m, :], in_=banks[bi][:, :]),
                         reads=[('bank', bi)], writes=[('uT', m)])
                for m in range(2):
                    for t in range(2):
                        bi = self.nb()
                        self.mm_group(bi, banks[bi][:, :], [(cs64[:, t, :], uT[:, m, :])], reads=[('uT', m), 'cs64'])
                        B.op('act', lambda e, bi=bi, m=m, t=t, zst=zst: e.activation(out=zst[:, t * 2 + m, :], in_=banks[bi][:, :],
                                                                                     func=AF.Copy),
                             reads=[('bank', bi)], acc=['zst%d' % s])
                self.dma('pool', Zv[:, :, tsl], zst[:, :, :], 'st_z%d' % s, reads=['zst%d' % s])
            B.barrier()
            B.emit()

    def fnet_pass(self):
        B, I, S, NT, NAs, banks, TB = self.B, self.I, self.S, self.NT, self.NAs, self.banks, self.TB
        with ExitStack() as st:
            A = lambda n, sh, dt: self.alloc(st, n, sh, dt)
            zt = A('zt', [128, 2, 128, 128], BF16)
            Y = A('Y', [128, 2, 4 * NAs, 128], BF16)
            Ost = A('Ost', [128, 4 * NAs, 128], BF16)
            Rm = A('Rm', [128, 2, 8 * NAs], BF16)
            GC = 8
            Gt = A('Gt', [128, GC, 2, 128], BF16)
            for path in (0, 1):
                if path == 0:
                    K, NS, MD = NAs, NAs, 128
                    Zv = S['Zs'].rearrange("(r c) (a b) -> a r c b", r=2, b=128)
                    self.dma('sp', Rm[0:K, :, 0:2 * NS], I['fRs'], 'c_Rm', writes=['Rm'])
                    G = I['fGs']
                    yfv = S['yf'][0:TB, :].rearrange("(d c) h -> d c h", c=NS)
                else:
                    K, NS, MD = 4 * NAs, 4 * NAs, 32
                    Zv = S['Zp'].rearrange("(r c) (a b) -> a r c b", r=2, b=128)
                    self.dma('sp', Rm[0:K, :, 0:2 * NS], I['fRp'], 'c_Rm', writes=['Rm'])
                    G = I['fGp']
                    yfv = S['yf'][TB:2 * TB, :].rearrange("(d c) h -> d c h", c=NS)
                for half in range(2):
                    for ri in range(2):
                        for cq in range(4):
                            self.dma('sp', zt[0:K, ri, cq * 32:(cq + 1) * 32, :],
                                     Zv[:, ri, half * 128 + cq * 32:half * 128 + (cq + 1) * 32, :], 'ld_z%d' % ri,
                                     writes=[('zt', ri)] if cq == 0 else [], acc=[] if cq == 0 else [('zt', ri)])
                    for c in range(128):
                        bi = self.nb()
                        self.mm_group(bi, banks[bi][:, 0:2 * NS],
                                      [(zt[0:K, 0, c, :], Rm[0:K, 0, 0:2 * NS]), (zt[0:K, 1, c, :], Rm[0:K, 1, 0:2 * NS])],
                                      reads=[('zt', 0), ('zt', 1), 'Rm'])
                        srcv = banks[bi][:, 0:2 * NS].rearrange("p (r s) -> p r s", r=2)
                        if c % 2 == 0:
                            B.op('act', lambda e, c=c, srcv=srcv, NS=NS: e.activation(out=Y[:, :, 0:NS, c], in_=srcv, func=AF.Copy),
                                 reads=[('bank', bi)], acc=['Y'])
                        else:
                            B.op('dve', lambda e, c=c, srcv=srcv, NS=NS: e.tensor_copy(out=Y[:, :, 0:NS, c], in_=srcv),
                                 reads=[('bank', bi)], acc=['Y'])
                    for c0 in range(0, NS, GC):
                        gw = 128 if path == 0 else 32
                        self.dma('sp', Gt[:, 0:GC, :, 0:gw], G[:, c0:c0 + GC, :, :], 'ld_g', writes=['Gt'])
                        for c4 in range(c0, c0 + GC, 4):
                            bi = self.nb()
                            self.mm_multi([bi], [(banks[bi][0:MD, q * 128:(q + 1) * 128],
                                                  [(Gt[:, c4 + q - c0, 0, 0:gw], Y[:, 0, c4 + q, :]),
                                                   (Gt[:, c4 + q - c0, 1, 0:gw], Y[:, 1, c4 + q, :])]) for q in range(4)],
                                          reads=['Gt', 'Y'])
                            if (c4 // 4) % 2 == 0:
                                B.op('act', lambda e, bi=bi, c4=c4, MD=MD: e.activation(
                                    out=Ost[0:MD, c4:c4 + 4, :], in_=banks[bi][0:MD, :].rearrange("p (q h) -> p q h", q=4), func=AF.Copy),
                                     reads=[('bank', bi)], acc=['Ost'])
                            else:
                                B.op('dve', lambda e, bi=bi, c4=c4, MD=MD: e.tensor_copy(
                                    out=Ost[0:MD, c4:c4 + 4, :], in_=banks[bi][0:MD, :].rearrange("p (q h) -> p q h", q=4)),
                                     reads=[('bank', bi)], acc=['Ost'])
                    for cq in range(0, NS, 32):
                        ce = min(NS, cq + 32)
                        self.dma('pool', yfv[:, cq:ce, half * 128:(half + 1) * 128], Ost[0:MD, cq:ce, :], 'st_yf', reads=['Ost'])
                    B.op('pool', lambda e: e.memset(Ost[:, 0:1, 0:1], 0.0), reads=[], writes=['Ost'])
                    B.op('pool', lambda e: e.memset(Y[:, 0:1, 0:1, 0:1], 0.0), reads=[], writes=['Y'])
            B.barrier()
            B.emit()
    def ffn_pass(self, L, src, dst):
        B, I, S, C, NT, banks = self.B, self.I, self.S, self.C, self.NT, self.banks
        SEG = self.SEG
        NF = 22
        with ExitStack() as st:
            A = lambda n, sh, dt: self.alloc(st, n, sh, dt)
            Wup = self.load_weight(st, 'Wup', I['ffn_w_up'][L], D, 5632)
            Wdn = self.load_weight(st, 'Wdn', I['ffn_w_down'][L], 2816, D)
            gain = self.cload(st, 'gain', [128, D], F32, I['norms'][:, 2 + L, :])
            fconv = self.cload(st, 'fconv', [128, NF, 4], F32, I['ffn_conv'][:, L, :, :])
            self.epsc = A('epsc', [128, 1], F32)
            B.op('pool', lambda e: e.memset(self.epsc[:, :], EPS), writes=['epsc'])
            xt = A('xt', [128, 4, D], F32)
            sq = None
            rs = A('rs', [128, 8], F32)
            xn = A('xn', [128, 4, D], BF16)
            xnT = A('xnT', [128, 8, 512], BF16)
            GH = A('GH', [128, NF, 2 * NT], F32)
            Gb = A('Gb', [128, 2, 514], F32)
            acc = A('acc', [128, 1, 512], F32)
            actT = A('actT', [128, NF, 512], BF16)
            hx = A('hx', [128, 2, NF], F32)
            hall = A('hall', [128, 8, 2 * NF], F32)
            fxl = A('fxl', [128, 2, NF], F32)
            srcv = src.rearrange("(i t) d -> i t d", t=512)
            self.dma('sp', xt[0:NT, 0, :], srcv[:, 0, :], 'ld_xb0', writes=[('xb', 0)])
            self.dma('sp', xt[NT:2 * NT, 0, :], srcv[:, 511, :], 'ld_xb0', acc=[('xb', 0)])
            self.norm_tile(xt, ('xb', 0), 2 * NT, 1, gain, 'gain', xn, 'xn', None, rs)
            self.transpose_tile(xn, [('xn', 0)], 2 * NT, 1, 8, xnT, 'xnT')
            for f in range(NF):
                bi = self.nb()
                self.mm_group(bi, banks[bi][:, 0:2 * NT],
                              [(Wup[:, k, f * 128:(f + 1) * 128], xnT[:, k, 0:2 * NT]) for k in range(8)],
                              reads=self.xk('xnT') + ['Wup'])
                B.op('act', lambda e, bi=bi, f=f: e.activation(out=GH[:, f, :], in_=banks[bi][:, 0:2 * NT], func=AF.Copy),
                     reads=[('bank', bi)], acc=['GH'])
            hin, hout = S['hinf%d' % L], S['houtf%d' % L]
            B.op('pool', lambda e: e.tensor_copy(out=hx[:, 0, :], in_=GH[:, :, SEG]), reads=['GH'], writes=[('hx', 0)])
            B.op('pool', lambda e: e.tensor_copy(out=hx[:, 1, :], in_=GH[:, :, 2 * NT - 1]), reads=['GH'], writes=[('hx', 1)])
            self.dma('pool', hin, hx[:, :, :].rearrange("p a c -> p (a c)"), 'st_hx', reads=[('hx', 0), ('hx', 1)], writes=['hin'])
            B.collective(lambda e: e.collective_compute("AllGather", ALU.bypass, replica_groups=[list(range(8))],
                                                        ins=[hin.opt()], outs=[hout.opt()]),
                     reads=['hin'], writes=['hout'])
            self.dma('sp', hall[:, :, :], hout.rearrange("(r p) w -> p r w", p=128), 'ld_hall', reads=['hout'], writes=['hall'])
            for sd in range(2):
                for r in range(8):
                    scol = C['coef'][:, 32 + 8 * sd + r:32 + 8 * sd + r + 1]
                    srcp = hall[:, r, NF:2 * NF] if sd == 0 else hall[:, r, 0:NF]
                    if r == 0:
                        B.op('dve', lambda e, sd=sd, scol=scol, srcp=srcp: e.tensor_scalar(out=fxl[:, sd, :], in0=srcp, scalar1=scol,
                                                                                          scalar2=None, op0=ALU.mult),
                             reads=['hall', 'c_coef'], writes=[('fxl', sd)])
                    else:
                        B.op('dve', lambda e, sd=sd, scol=scol, srcp=srcp: e.scalar_tensor_tensor(out=fxl[:, sd, :], in0=srcp, scalar=scol,
                                                                                                 in1=fxl[:, sd, :], op0=ALU.mult, op1=ALU.add),
                             reads=['hall', 'c_coef', ('fxl', sd)], writes=[('fxl', sd)])
            def pro_a(i):
                for half in range(2):
                    keys = []
                    for b in range(2):
                        j = 2 * half + b
                        r0 = 512 * i + 128 * j
                        self.dma('sp', xt[:, b, :], src[r0:r0 + 128, :], 'ld_xb%d' % b, writes=[('xb', b)])
                        keys.append(('xb', b))
                    self.norm_tile(xt, None, 128, 2, gain, 'gain', xn, 'xn', None, rs, jo=2 * half, xkeys=keys)

            def pro_b(i):
                self.transpose_tile(xn, self.xk('xn', 4), 128, 4, 8, xnT, 'xnT')

            pro_a(0)
            pro_b(0)
            for i in range(NT):
                xr = self.xk('xnT') + ['Wup']
                for f in range(NF):
                    if f == 2 and i + 1 < NT:
                        pro_a(i + 1)
                    s = f % 2
                    bg = self.nb()
                    self.mm_group(bg, banks[bg][:, :], [(Wup[:, k, f * 128:(f + 1) * 128], xnT[:, k, :]) for k in range(8)], reads=xr)
                    bv = self.nb()
                    self.mm_group(bv, banks[bv][:, :],
                                  [(Wup[:, k, 2816 + f * 128:2816 + (f + 1) * 128], xnT[:, k, :]) for k in range(8)], reads=xr)
                    B.op('act', lambda e, bg=bg, s=s: e.activation(out=Gb[:, s, 1:513], in_=banks[bg][:, :], func=AF.Copy),
                         reads=[('bank', bg)], writes=[('Gb', s)])
                    if i == 0:
                        B.op('pool', lambda e, s=s: e.memset(Gb[:, s, 0:1], 0.0), writes=[('Gb', s, 'l')])
                    elif i == SEG:
                        B.op('pool', lambda e, s=s, f=f: e.tensor_copy(out=Gb[:, s, 0:1], in_=fxl[:, 0, f:f + 1]),
                             reads=[('fxl', 0)], writes=[('Gb', s, 'l')])
                    else:
                        B.op('pool', lambda e, s=s, f=f, i=i: e.tensor_copy(out=Gb[:, s, 0:1], in_=GH[:, f, NT + i - 1:NT + i]),
                             reads=['GH'], writes=[('Gb', s, 'l')])
                    if i == SEG - 1:
                        B.op('pool', lambda e, s=s: e.memset(Gb[:, s, 513:514], 0.0), writes=[('Gb', s, 'r')])
                    elif i == NT - 1:
                        B.op('pool', lambda e, s=s, f=f: e.tensor_copy(out=Gb[:, s, 513:514], in_=fxl[:, 1, f:f + 1]),
                             reads=[('fxl', 1)], writes=[('Gb', s, 'r')])
                    else:
                        B.op('pool', lambda e, s=s, f=f, i=i: e.tensor_copy(out=Gb[:, s, 513:514], in_=GH[:, f, i + 1:i + 2]),
                             reads=['GH'], writes=[('Gb', s, 'r')])
                    B.op('dve', lambda e, s=s, f=f: e.tensor_scalar(out=acc[:, 0, :], in0=Gb[:, s, 1:513], scalar1=fconv[:, f, 1:2],
                                                                    scalar2=fconv[:, f, 3:4], op0=ALU.mult, op1=ALU.add),
                         reads=[('Gb', s), 'fconv'], writes=[('acc', 0)])
                    B.op('dve', lambda e, s=s, f=f: e.scalar_tensor_tensor(out=acc[:, 0, :], in0=Gb[:, s, 0:512], scalar=fconv[:, f, 0:1],
                                                                            in1=acc[:, 0, :], op0=ALU.mult, op1=ALU.add),
                         reads=[('Gb', s), ('Gb', s, 'l'), ('acc', 0), 'fconv'], writes=[('acc', 0)])
                    B.op('dve', lambda e, s=s, f=f: e.scalar_tensor_tensor(out=acc[:, 0, :], in0=Gb[:, s, 2:514], scalar=fconv[:, f, 2:3],
                                                                           in1=acc[:, 0, :], op0=ALU.mult, op1=ALU.add),
                         reads=[('Gb', s), ('Gb', s, 'r'), ('acc', 0), 'fconv'], writes=[('acc', 0)])
                    B.op('act', lambda e: e.activation(out=acc[:, 0, :], in_=acc[:, 0, :], func=AF.Silu),
                         reads=[('acc', 0)], writes=[('acc', 0)])
                    B.op('dve', lambda e, f=f, bv=bv: e.tensor_tensor(out=actT[:, f, :], in0=acc[:, 0, :], in1=banks[bv][:, :],
                                                                      op=ALU.mult),
                         reads=[('acc', 0), ('bank', bv)], writes=[('actT', f)])
                if i + 1 < NT:
                    pro_b(i + 1)
                for j in range(4):
                    sl_ = 2 + (j % 2)
                    r0 = 512 * i + 128 * j
                    self.dma('sp', xt[:, sl_, :], src[r0:r0 + 128, :], 'ld_xb%d' % sl_, writes=[('xb', sl_)])
                    for n in range(2):
                        bi = self.nb()
                        self.mm_group(bi, banks[bi][:, :],
                                      [(actT[:, f, j * 128:(j + 1) * 128], Wdn[:, f, n * 512:(n + 1) * 512]) for f in range(NF)],
                                      reads=self.xk('actT', NF) + ['Wdn'])
                        B.op('dve', lambda e, bi=bi, sl_=sl_, n=n: e.tensor_tensor(out=xt[:, sl_, n * 512:(n + 1) * 512],
                                                                                   in0=xt[:, sl_, n * 512:(n + 1) * 512],
                                                                                   in1=banks[bi][:, :], op=ALU.add),
                             reads=[('bank', bi), ('xb', sl_)], acc=[('xb', sl_)])
                    self.dma('pool', dst[r0:r0 + 128, :], xt[:, sl_, :], 'st_h%d' % sl_, reads=[('xb', sl_)])
            B.barrier()
            B.emit()

    def ple_pass(self, li, src, dst, final):
        B, I, NT = self.B, self.I, self.NT
        with ExitStack() as st:
            A = lambda n, sh, dt: self.alloc(st, n, sh, dt)
            self.epsc = A('epsc', [128, 1], F32)
            B.op('pool', lambda e: e.memset(self.epsc[:, :], EPS), writes=['epsc'])
            xts = [A('xt%d' % s, [128, 4, D], F32) for s in range(2)]
            rs = A('rs', [128, 8], F32)
            if final:
                gain = self.cload(st, 'gain', [128, D], F32, I['norms'][:, 6, :])
                yo = A('yo', [128, 4, D], F32)
            P = self.ple_alloc(st, li)

            def pro(i, s, part=None):
                tsl = slice(512 * i, 512 * (i + 1))
                if part in (None, 0):
                    self.dma('sp', xts[s][:, :, :], src[tsl, :].rearrange("(j p) d -> p j d", p=128), 'ld_x%d' % s, writes=['xt%d' % s])
                self.ple_pro(P, i, xts[s], 'xt%d' % s, s, part)

            pro(0, 0)
            for i in range(NT):
                s = i % 2
                xt, xtk = xts[s], 'xt%d' % s
                tsl = slice(512 * i, 512 * (i + 1))
                hook = (lambda i=i, s=s: pro(i + 1, 1 - s, 0)) if i + 1 < NT else None
                hook2 = (lambda i=i, s=s: pro(i + 1, 1 - s, 1)) if i + 1 < NT else None
                self.ple_body(P, xt, xtk, s, hook, hook2)
                if final:
                    self.norm_tile(xt, xtk, 128, 4, gain, 'gain', yo, 'yo', None, rs)
                    self.dma('pool', dst[tsl, :].rearrange("(j p) d -> p j d", p=128), yo[:, :, :], 'st_y', reads=self.xk('yo', 4))
                else:
                    self.dma('pool', dst[tsl, :].rearrange("(j p) d -> p j d", p=128), xt[:, :, :], 'st_y', reads=[xtk])
            B.barrier()
            B.emit()


def _bf(a):
    return np.ascontiguousarray(a).astype(ml_dtypes.bfloat16)


def _dftR(n, nblk=1):
    a = np.arange(n)
    ang = 2 * np.pi * np.outer(a, a) / n
    N = n * nblk
    Cm = np.zeros((N, N))
    Sm = np.zeros((N, N))
    for j in range(nblk):
        Cm[j * n:(j + 1) * n, j * n:(j + 1) * n] = np.cos(ang)
        Sm[j * n:(j + 1) * n, j * n:(j + 1) * n] = np.sin(ang)
    R = np.zeros((N, 2, 2 * N))
    R[:, 0, :N] = Cm
    R[:, 0, N:] = -Sm
    R[:, 1, :N] = -Sm
    R[:, 1, N:] = -Cm
    return _bf(R.astype(np.float32))


def _dftG(Aa, d0, nd):
    Sx = 128 * Aa
    b = np.arange(128, dtype=np.int64)[:, None, None]
    cc = np.arange(Aa, dtype=np.int64)[None, :, None]
    d = (d0 + np.arange(nd, dtype=np.int64))[None, None, :]
    ph = 2 * np.pi * ((b * (cc + Aa * d)) % Sx) / Sx
    nrm = 1.0 / np.sqrt(Sx * 64.0)
    G = np.zeros((128, Aa, 2, nd))
    G[:, :, 0, :] = np.cos(ph) * nrm
    G[:, :, 1, :] = np.sin(ph) * nrm
    return _bf(G.astype(np.float32))


def _consts(NT):
    NAs = 2 * NT
    c = {}
    c['ident'] = _bf(np.eye(128, dtype=np.float32))
    lp = np.arange(128)[:, None]
    l = np.arange(128)[None, :]
    mats = [(lp <= l), (lp > l), (lp >= l), (lp < l)]
    tri = np.zeros((128, 8, 128), np.float32)
    for q, m in enumerate(mats):
        tri[:, q, :] = m.astype(np.float32) * (-1.0 / 16.0)
        tri[:, 4 + q, :] = m.astype(np.float32) * (-1.0)
    c['tri'] = tri
    am = np.zeros((128, 2, 512), np.float32)
    am[:, 0, :] = np.tile((lp <= l).astype(np.float32), (1, 4))
    am[:, 1, :] = np.tile((lp >= l).astype(np.float32), (1, 4))
    c['amask'] = _bf(am)
    nc_ = np.zeros((128, 2), np.float32)
    nc_[:, 0] = -1.0 / 16.0
    nc_[:, 1] = -1.0
    c['negcol'] = nc_
    c['negones'] = -np.ones((128, 128), np.float32)
    ch = np.arange(64)
    ang = 2 * np.pi * np.outer(ch, ch) / 64.0
    cs = np.zeros((128, 2, 128), np.float64)
    for b in range(2):
        cs[b * 64:(b + 1) * 64, 0, b * 64:(b + 1) * 64] = np.cos(ang)
        cs[b * 64:(b + 1) * 64, 1, b * 64:(b + 1) * 64] = np.sin(ang)
    c['cs64'] = _bf(cs.astype(np.float32))
    c['fRs'] = _dftR(NAs)
    c['fRp'] = _dftR(4 * NAs)
    c['fGs'] = _dftG(NAs, 0, 128)
    return c


_NC_CACHE = {}


def _get_nc(NT, debug=False):
    key = (NT, debug)
    if key not in _NC_CACHE:
        g = Gen(NT, debug=debug)
        g.build()
        _NC_CACHE[key] = g
    return _NC_CACHE[key]


def _rep(v):
    v = np.asarray(v, np.float32).reshape(1, -1)
    return np.ascontiguousarray(np.broadcast_to(v, (128, v.shape[1])))


def _shared_inputs(w, NT):
    f32 = lambda a: np.ascontiguousarray(np.asarray(a, np.float32))
    d = dict(_consts(NT))
    d['e_w_in'] = f32(w['e_w_in'][0])
    d['e_w_out'] = f32(w['e_w_out'][0])
    d['o_w_in'] = f32(w['o_w_in'][0])
    d['o_w_out'] = f32(w['o_w_out'][0])
    d['ffn_w_up'] = f32(w['ffn_w_up'])
    d['ffn_w_down'] = f32(w['ffn_w_down'])
    d['ple_w'] = f32(w['ple_w'])
    d['ple_gate_w'] = f32(w['ple_gate_w'])
    norms = np.stack([w['e_norm'][0], w['o_norm'][0], w['ffn_norm'][0], w['ffn_norm'][1],
                      w['ple_gate_norm'][0], w['ple_gate_norm'][1], w['final_norm']], 0).astype(np.float32)
    d['norms'] = np.ascontiguousarray(np.broadcast_to(norms[None], (128, 7, D)))
    d['gla_norm'] = _rep(w['e_gla_norm'][0])
    d['mlstm_norm'] = _rep(w['o_mlstm_norm'][0])
    d['gate_bias'] = _rep(w['o_gate_bias'][0])
    w2 = np.zeros((2, 17, 384), np.float32)
    w2[0, :16] = w['e_gla_w2_f'][0]
    w2[0, 16] = w['e_gla_b_f'][0]
    w2[1, :16] = w['e_gla_w2_b'][0]
    w2[1, 16] = w['e_gla_b_b'][0]
    d['w2aug'] = w2
    oc = np.zeros((128, 4, 4), np.float32)
    cw = np.asarray(w['o_conv_w'][0], np.float32)
    cb = np.asarray(w['o_conv_b'][0], np.float32)
    for t in range(3):
        oc[:, :, t] = cw[t].reshape(4, 128).T
    oc[:, :, 3] = cb.reshape(4, 128).T
    d['o_conv'] = oc
    fc = np.zeros((128, 2, 22, 4), np.float32)
    for L in range(2):
        fw = np.asarray(w['ffn_conv_w'][L], np.float32)
        fb = np.asarray(w['ffn_conv_b'][L], np.float32)
        for t in range(3):
            fc[:, L, :, t] = fw[t].reshape(22, 128).T
        fc[:, L, :, 3] = fb.reshape(22, 128).T
    d['ffn_conv'] = fc
    return d


def run_model(x_prompt, x_sample, p_prompt, p_sample, w, NT, debug=False):
    TB = 256 * NT
    NAs = 2 * NT
    g = _get_nc(NT, debug)
    shared = _shared_inputs(w, NT)
    gp = [_dftG(4 * NAs, 32 * jq, 32) for jq in range(4)]
    in_maps = []
    for c in range(8):
        pb, jq = c // 4, c % 4
        d = dict(shared)
        qs = slice(jq * TB, (jq + 1) * TB)
        d['x'] = np.ascontiguousarray(np.concatenate([x_sample[c], x_prompt[pb, qs]], 0), np.float32)
        d['xfull'] = np.ascontiguousarray(x_prompt[pb], np.float32)
        d['p'] = np.ascontiguousarray(np.concatenate([p_sample[:, c], p_prompt[:, pb, qs]], 1), np.float32)
        coef = np.zeros((48,), np.float32)
        for r in range(8):
            same = (r // 4 == pb)
            uf = 1.0 if (same and (r % 4) < jq) else 0.0
            ub = 1.0 if (same and (r % 4) > jq) else 0.0
            coef[r] = uf
            coef[8 + r] = 1.0 - uf
            coef[16 + r] = ub
            coef[24 + r] = 1.0 - ub
            coef[32 + r] = 1.0 if (same and r == c - 1) else 0.0
            coef[40 + r] = 1.0 if (same and r == c + 1) else 0.0
        d['coef'] = _rep(coef)
        d['fGp'] = gp[jq]
        in_maps.append(d)
    res = run_bass_kernel_spmd(g.nc, in_maps, core_ids=list(range(8)))
    ys = [np.asarray(res.results[c]['y']) for c in range(8)]
    y_sample = np.stack([ys[c][0:TB] for c in range(8)], 0).astype(np.float32)
    y_prompt = np.stack([np.concatenate([ys[4 * b + q][TB:2 * TB] for q in range(4)], 0) for b in range(2)], 0).astype(np.float32)
    return y_prompt, y_sample, res


def kernel(x_prompt, x_sample, p_prompt, p_sample, **w):
    xp = np.asarray(x_prompt, np.float32)
    xs = np.asarray(x_sample, np.float32)
    pp = np.asarray(p_prompt, np.float32)
    ps = np.asarray(p_sample, np.float32)
    y_prompt, y_sample, _ = run_model(xp, xs, pp, ps, w, 16)
    return (y_prompt, y_sample)
```
